# Optimizing a Trainium2 kernel written in Bass

```python
import jax, jax.numpy as jnp
from jax import lax
import numpy as np

D_MODEL = 1024
BATCH = 16
SEQ = 2048
DEPTH = 2

CHUNK = 64
MEM_LEN = 256
SGU_BLOCK = 128
A_GROUPS = 4
A_WIDTH = D_MODEL // 2
A_GROUP_DIM = A_WIDTH // A_GROUPS
POOL_WINDOWS = (2, 4, 8, 16)
B_WIDTH = D_MODEL // 2
B_GROUP_DIM = B_WIDTH // len(POOL_WINDOWS)
C_WIDTH = D_MODEL
CONV_K = 3
XATTN_HEADS = 4
XATTN_HEAD_DIM = D_MODEL // XATTN_HEADS
D_FF = ((8 * D_MODEL // 3 + 127) // 128) * 128
FFN_CONV_K = 3
LN_EPS = 1e-5
DEEPNORM_ALPHA = (2 * DEPTH) ** 0.25
DEEPNORM_BETA = (8 * DEPTH) ** -0.25
N_EVEN = (DEPTH + 1) // 2
N_ODD = DEPTH // 2

kernel_name = "hybrid_sgu_pool_shortconv_deepnorm_trunk"


def layer_norm(x, g, b):
    xf = x.astype(jnp.float32)
    mu = jnp.mean(xf, axis=-1, keepdims=True)
    var = jnp.mean(jnp.square(xf - mu), axis=-1, keepdims=True)
    y = (xf - mu) * lax.rsqrt(var + LN_EPS)
    return (y * g.astype(jnp.float32) + b.astype(jnp.float32)).astype(x.dtype)


def causal_dwconv(x, w):
    k, c = w.shape
    return lax.conv_general_dilated(
        x, w[:, None, :].astype(x.dtype), window_strides=(1,), padding=[(k - 1, 0)],
        dimension_numbers=('NWC', 'WIO', 'NWC'), feature_group_count=c)


def trailing_mean_minus_self(v, window):
    seq = v.shape[1]
    vf = v.astype(jnp.float32)
    cs = jnp.cumsum(vf, axis=1)
    lagged = jnp.pad(cs, ((0, 0), (window, 0), (0, 0)))[:, :seq]
    count = jnp.minimum(jnp.arange(1, seq + 1), window).astype(jnp.float32)
    mean = (cs - lagged) / count[None, :, None]
    return (mean - vf).astype(v.dtype)


def mixer_ab(x, w_in, ln_g, ln_b, w_s, b_s, pool_w, pool_scale, w_out):
    bsz, seq, _ = x.shape
    proj = x @ w_in
    uv = jax.nn.gelu(proj[..., :2 * A_WIDTH])
    u, v = uv[..., :A_WIDTH], uv[..., A_WIDTH:]
    v = layer_norm(v, ln_g, ln_b)
    chunk_id = jnp.arange(SGU_BLOCK) // CHUNK
    mask = chunk_id[None, :] <= chunk_id[:, None]
    w_m = jnp.where(mask[None], w_s, 0.0).astype(v.dtype)
    vb = v.reshape(bsz, seq // SGU_BLOCK, SGU_BLOCK, A_GROUPS, A_GROUP_DIM)
    sp = jnp.einsum('gij,bnjgc->bnigc', w_m, vb) + b_s.T[None, None, :, :, None]
    y_a = u * sp.reshape(bsz, seq, A_WIDTH)
    xb = proj[..., 2 * A_WIDTH:].reshape(bsz, seq, len(POOL_WINDOWS), B_GROUP_DIM)
    pooled = jnp.stack([trailing_mean_minus_self(xb[:, :, g], w)
                        for g, w in enumerate(POOL_WINDOWS)], axis=2)
    y_b = jnp.einsum('bsgc,gcd->bsgd', pooled, pool_w).reshape(bsz, seq, B_WIDTH) * pool_scale
    return jnp.concatenate([y_a, y_b], axis=-1) @ w_out


def mixer_c(x, w_in, conv_w, w_out):
    proj = x @ w_in
    gate_b, gate_c, h = jnp.split(proj, 3, axis=-1)
    y = gate_b * causal_dwconv(gate_c * h, conv_w)
    return y @ w_out


def cross_attend(x, mem, wq, wkv, wo):
    bsz, seq, d = x.shape
    q = (x @ wq).reshape(bsz, seq, XATTN_HEADS, XATTN_HEAD_DIM)
    kv = mem @ wkv
    k = kv[..., :d].reshape(bsz, -1, XATTN_HEADS, XATTN_HEAD_DIM)
    v = kv[..., d:].reshape(bsz, -1, XATTN_HEADS, XATTN_HEAD_DIM)
    scores = jnp.einsum('bshd,bmhd->bhsm', q.astype(jnp.float32), k.astype(jnp.float32))
    probs = jax.nn.softmax(scores * (XATTN_HEAD_DIM ** -0.5), axis=-1).astype(x.dtype)
    out = jnp.einsum('bhsm,bmhd->bshd', probs, v).reshape(bsz, seq, d)
    return out @ wo


def conv_ffn(x, w_up, conv_w, conv_b, w_down):
    up = x @ w_up
    a, g = up[..., :D_FF], up[..., D_FF:]
    g = causal_dwconv(g, conv_w) + conv_b
    return (jax.nn.gelu(g) * a) @ w_down


def _normal(k, shape, scale):
    return jax.random.normal(k, shape, jnp.float32) * scale


def setup_inputs(seed: int = 0) -> dict:
    key = jax.random.key(seed)
    ks = iter(jax.random.split(key, 32))
    d = D_MODEL
    beta = DEEPNORM_BETA
    x = _normal(next(ks), (BATCH, SEQ, d), 1.0)
    mem = _normal(next(ks), (BATCH, MEM_LEN, d), 1.0)
    ab_w_in = _normal(next(ks), (N_EVEN, d, 2 * A_WIDTH + B_WIDTH), d ** -0.5)
    sgu_ln_g = 1.0 + _normal(next(ks), (N_EVEN, A_WIDTH), 0.05)
    sgu_ln_b = _normal(next(ks), (N_EVEN, A_WIDTH), 0.02)
    sgu_w = _normal(next(ks), (N_EVEN, A_GROUPS, SGU_BLOCK, SGU_BLOCK), SGU_BLOCK ** -0.5)
    sgu_b = 1.0 + _normal(next(ks), (N_EVEN, A_GROUPS, SGU_BLOCK), 0.1)
    pool_w = _normal(next(ks), (N_EVEN, len(POOL_WINDOWS), B_GROUP_DIM, B_GROUP_DIM), B_GROUP_DIM ** -0.5)
    pool_scale = 1.0 + _normal(next(ks), (N_EVEN, B_WIDTH), 0.1)
    ab_w_out = _normal(next(ks), (N_EVEN, A_WIDTH + B_WIDTH, d), (A_WIDTH + B_WIDTH) ** -0.5 * beta)
    c_w_in = _normal(next(ks), (N_ODD, d, 3 * C_WIDTH), d ** -0.5)
    c_conv_w = _normal(next(ks), (N_ODD, CONV_K, C_WIDTH), CONV_K ** -0.5)
    c_w_out = _normal(next(ks), (N_ODD, C_WIDTH, d), C_WIDTH ** -0.5 * beta)
    ln_mix_g = 1.0 + _normal(next(ks), (DEPTH, d), 0.05)
    ln_mix_b = _normal(next(ks), (DEPTH, d), 0.02)
    xa_wq = _normal(next(ks), (DEPTH, d, d), d ** -0.5)
    xa_wk = _normal(next(ks), (DEPTH, d, d), d ** -0.5)
    xa_wv = _normal(next(ks), (DEPTH, d, d), d ** -0.5 * beta)
    xa_wkv = jnp.concatenate([xa_wk, xa_wv], axis=-1)
    xa_wo = _normal(next(ks), (DEPTH, d, d), d ** -0.5 * beta)
    ln_xa_g = 1.0 + _normal(next(ks), (DEPTH, d), 0.05)
    ln_xa_b = _normal(next(ks), (DEPTH, d), 0.02)
    ffn_w_up = _normal(next(ks), (DEPTH, d, 2 * D_FF), d ** -0.5)
    ffn_conv_w = _normal(next(ks), (DEPTH, FFN_CONV_K, D_FF), FFN_CONV_K ** -0.5)
    ffn_conv_b = _normal(next(ks), (DEPTH, D_FF), 0.02)
    ffn_w_down = _normal(next(ks), (DEPTH, D_FF, d), D_FF ** -0.5 * beta)
    ln_ffn_g = 1.0 + _normal(next(ks), (DEPTH, d), 0.05)
    ln_ffn_b = _normal(next(ks), (DEPTH, d), 0.02)
    return {"x": x, "mem": mem, "ab_w_in": ab_w_in, "sgu_ln_g": sgu_ln_g, "sgu_ln_b": sgu_ln_b,
            "sgu_w": sgu_w, "sgu_b": sgu_b, "pool_w": pool_w, "pool_scale": pool_scale,
            "ab_w_out": ab_w_out, "c_w_in": c_w_in, "c_conv_w": c_conv_w, "c_w_out": c_w_out,
            "ln_mix_g": ln_mix_g, "ln_mix_b": ln_mix_b, "xa_wq": xa_wq, "xa_wkv": xa_wkv,
            "xa_wo": xa_wo, "ln_xa_g": ln_xa_g, "ln_xa_b": ln_xa_b, "ffn_w_up": ffn_w_up,
            "ffn_conv_w": ffn_conv_w, "ffn_conv_b": ffn_conv_b, "ffn_w_down": ffn_w_down,
            "ln_ffn_g": ln_ffn_g, "ln_ffn_b": ln_ffn_b}


def reference(x, mem, ab_w_in, sgu_ln_g, sgu_ln_b, sgu_w, sgu_b, pool_w, pool_scale,
              ab_w_out, c_w_in, c_conv_w, c_w_out, ln_mix_g, ln_mix_b, xa_wq, xa_wkv,
              xa_wo, ln_xa_g, ln_xa_b, ffn_w_up, ffn_conv_w, ffn_conv_b, ffn_w_down,
              ln_ffn_g, ln_ffn_b):
    alpha = DEEPNORM_ALPHA
    for layer in range(DEPTH):
        j = layer // 2
        if layer % 2 == 0:
            mix = mixer_ab(x, ab_w_in[j], sgu_ln_g[j], sgu_ln_b[j], sgu_w[j], sgu_b[j],
                           pool_w[j], pool_scale[j], ab_w_out[j])
        else:
            mix = mixer_c(x, c_w_in[j], c_conv_w[j], c_w_out[j])
        x = layer_norm(alpha * x + mix, ln_mix_g[layer], ln_mix_b[layer])
        x = layer_norm(alpha * x + cross_attend(x, mem, xa_wq[layer], xa_wkv[layer], xa_wo[layer]),
                       ln_xa_g[layer], ln_xa_b[layer])
        x = layer_norm(alpha * x + conv_ffn(x, ffn_w_up[layer], ffn_conv_w[layer],
                                            ffn_conv_b[layer], ffn_w_down[layer]),
                       ln_ffn_g[layer], ln_ffn_b[layer])
    return x
```

```python
import numpy as np
from contextlib import ExitStack
import concourse.bass as bass
import concourse.mybir as mybir
from concourse.bass_utils import run_bass_kernel_spmd

F32 = mybir.dt.float32
BF16 = mybir.dt.bfloat16
AF = mybir.ActivationFunctionType
ALU = mybir.AluOpType

ENGS = ("pe", "act", "dve", "pool", "sp")
NDMASEM = 16


class Tile:
    __slots__ = ("name", "lastw", "readers", "dreaders")

    def __init__(self, name=""):
        self.name = name
        self.lastw = None
        self.readers = {}
        self.dreaders = []


class Rec:
    __slots__ = ("eng", "fn", "deps", "dma", "signal", "sigval", "dmaidx")

    def __init__(self, eng, fn, deps, dma):
        self.eng = eng
        self.fn = fn
        self.deps = deps
        self.dma = dma
        self.signal = False
        self.sigval = 0
        self.dmaidx = -1


class Sched:
    def __init__(self):
        self.recs = []
        self.ndma = {e: 0 for e in ENGS}

    def op(self, eng, fn, reads=(), writes=(), dma=False):
        idx = len(self.recs)
        recs = self.recs
        deps = set()
        rawset = set()
        for t in reads:
            if t.lastw is not None:
                deps.add(t.lastw)
                rawset.add(t.lastw)
        for t in writes:
            if t.lastw is not None:
                deps.add(t.lastw)
            deps.update(t.readers.values())
            deps.update(t.dreaders)
        real = []
        for d in deps:
            r = recs[d]
            if r.eng == eng and not r.dma and not dma:
                if eng == "pe":
                    continue
                if d not in rawset:
                    continue
            real.append(d)
        rec = Rec(eng, fn, real, dma)
        if dma:
            rec.dmaidx = self.ndma[eng]
            self.ndma[eng] += 1
        recs.append(rec)
        for d in real:
            recs[d].signal = True
        for t in reads:
            if dma:
                t.dreaders.append(idx)
            else:
                t.readers[eng] = idx
        for t in writes:
            t.lastw = idx
            t.readers = {}
            t.dreaders = []
        return idx

    def emit(self, nc, final_dma_wait_eng="sp"):
        recs = self.recs
        cnt = {e: 0 for e in ENGS}
        for r in recs:
            if r.dma:
                r.signal = True
                continue
            if r.signal:
                cnt[r.eng] += 1
                r.sigval = cnt[r.eng]
        with ExitStack() as es:
            prog = {e: es.enter_context(nc.semaphore("prog_" + e)) for e in ENGS}
            dsem = {}
            for e in ENGS:
                if self.ndma[e] > 0:
                    dsem[e] = [es.enter_context(nc.semaphore("dma_%s_%d" % (e, i)))
                               for i in range(min(NDMASEM, self.ndma[e]))]
            block = es.enter_context(nc.Block())

            nsem = {e: len(dsem[e]) for e in dsem}
            know = {e: {x: 0 for x in ENGS} for e in ENGS}
            dwaited = {e: {} for e in ENGS}
            sigcount = {e: 0 for e in ENGS}
            vc = [None] * len(recs)
            plan = [None] * len(recs)
            nw_before = 0
            nw_after = 0
            for i, r in enumerate(recs):
                E = r.eng
                K = know[E]
                waits = {}
                merged = []
                seen_old = set()
                for d in r.deps:
                    rd = recs[d]
                    if rd.dma:
                        k = nsem[rd.eng]
                        key = ("d", rd.eng, rd.dmaidx % k)
                        val = 16 * (rd.dmaidx // k + 1)
                        if dwaited[E].get(key, 0) < val:
                            waits[key] = max(waits.get(key, 0), val)
                            merged.append(d)
                    else:
                        X = rd.eng
                        seen_old.add(X)
                        if K[X] < rd.sigval:
                            waits[("p", X)] = max(waits.get(("p", X), 0), rd.sigval)
                            merged.append(d)
                nw_before += len(seen_old)
                if r.dma:
                    k = nsem[E]
                    if r.dmaidx >= k:
                        key = ("d", E, r.dmaidx % k)
                        val = 16 * (r.dmaidx // k)
                        if dwaited[E].get(key, 0) < val:
                            waits[key] = max(waits.get(key, 0), val)
                for key, val in waits.items():
                    if key[0] == "d":
                        dwaited[E][key] = val
                    else:
                        nw_after += 1
                        if K[key[1]] < val:
                            K[key[1]] = val
                for d in merged:
                    vd = vc[d]
                    for x in ENGS:
                        if vd[x] > K[x]:
                            K[x] = vd[x]
                v = dict(K)
                if not r.dma:
                    if r.signal:
                        sigcount[E] = r.sigval
                    if sigcount[E] > v[E]:
                        v[E] = sigcount[E]
                vc[i] = v
                plan[i] = list(waits.items())

            def sem_of(key):
                if key[0] == "d":
                    return dsem[key[1]][key[2]]
                return prog[key[1]]

            def run(eng_name, e):
                for i, r in enumerate(recs):
                    if r.eng != eng_name:
                        continue
                    todo = [(sem_of(key), val) for key, val in plan[i]]
                    for sem, val in todo[:-1]:
                        e.wait_ge(sem, val)
                    ins = r.fn(e)
                    if todo:
                        ins._wait_ge(todo[-1][0], todo[-1][1])
                    if r.dma:
                        k = len(dsem[r.eng])
                        ins.then_inc(dsem[r.eng][r.dmaidx % k], 16)
                    elif r.signal:
                        ins.then_inc(prog[r.eng], 1)
                if eng_name == final_dma_wait_eng:
                    for en in ENGS:
                        n = self.ndma[en]
                        if n == 0:
                            continue
                        k = len(dsem[en])
                        for j in range(k):
                            uses = (n - j + k - 1) // k
                            if uses > 0:
                                e.wait_ge(dsem[en][j], 16 * uses)

            @block.tensor
            def _(e):
                run("pe", e)

            @block.scalar
            def _(e):
                run("act", e)

            @block.vector
            def _(e):
                run("dve", e)

            @block.gpsimd
            def _(e):
                run("pool", e)

            @block.sync
            def _(e):
                run("sp", e)


P = 128
D = 1024
DC = 8
SEQ = 2048
NB_LOCAL = 2
STW = 1024
TT = 512
NTT = 2
NBLK = 8
DFF = 2816
FC = 22
MEM = 256
ALPHA = float(4 ** 0.25)
EPS = 1e-5
NSLOT = 6
POOL_WINDOWS = (2, 4, 8, 16)

PV = {}
_c = 0
for _l in range(2):
    for _n in ("mix_g", "mix_b", "xa_g", "xa_b", "ffn_g", "ffn_b"):
        PV[(_n, _l)] = _c
        _c += 8
PV["pool_scale"] = _c
_c += 4
PV["c_conv_w"] = _c
_c += 24
for _l in range(2):
    PV[("ffn_conv_w", _l)] = _c
    _c += 66
for _l in range(2):
    PV[("ffn_conv_b", _l)] = _c
    _c += 22
NPV = _c


def build_program(n_st=4, stop_after=None):
    nc = bass.Bass("TRN2", target_bir_lowering=False)
    S = Sched()

    def dram_in(name, shape):
        return nc.dram_tensor(name, list(shape), F32, kind="ExternalInput").ap()

    xT_d = dram_in("xT", [NB_LOCAL, DC, P, SEQ])
    memT_d = dram_in("memT", [NB_LOCAL, DC, P, MEM])
    ab_w_in = dram_in("ab_w_in", [D, 1536])
    ab_w_out = dram_in("ab_w_out", [D, D])
    c_w_in = dram_in("c_w_in", [D, 3072])
    c_w_out = dram_in("c_w_out", [D, D])
    xa_wq = dram_in("xa_wq", [2, D, D])
    xa_wkv = dram_in("xa_wkv", [2, D, 2 * D])
    xa_wo = dram_in("xa_wo", [2, D, D])
    ffn_w_up = dram_in("ffn_w_up", [2, D, 2 * DFF])
    ffn_w_down = dram_in("ffn_w_down", [2, DFF, D])
    pv_d = dram_in("pv", [P, NPV])
    sgub_d = dram_in("sgub", [P, 1024])
    sguwT_d = dram_in("sguwT", [P, 4, P])
    sgubias_d = dram_in("sgubias", [1, 512])
    poolmats_d = dram_in("poolmats", [P, 20, P])
    poolw_d = dram_in("poolw", [P, 4, P])
    outT_d = nc.dram_tensor("outT", [NB_LOCAL, DC, P, SEQ], F32, kind="ExternalOutput").ap()

    with ExitStack() as es:
        def sb(name, shape, dt):
            return es.enter_context(nc.sbuf_tensor("sb_" + name, list(shape), dt))

        xT32 = sb("xT32", [P, DC, STW], F32)
        xTb = sb("xTb", [P, DC, STW], BF16)
        NAR = 44
        arena = sb("arena", [P, NAR, TT], BF16)
        wring = [sb("wslot%d" % i, [P, 4096], BF16) for i in range(NSLOT)]
        kT = [sb("kT%d" % l, [P, DC, MEM], BF16) for l in range(2)]
        vtok = [sb("vtok%d" % l, [P, 2, D], BF16) for l in range(2)]
        pv = sb("pv", [P, NPV], F32)
        sgub = sb("sgub", [P, 1024], F32)
        WmT = sb("WmT", [P, 4, P], BF16)
        bs_hi = sb("bs_hi", [1, 512], BF16)
        bs_lo = sb("bs_lo", [1, 512], BF16)
        poolmats = sb("poolmats", [P, 20, P], BF16)
        poolw = sb("poolw", [P, 4, P], BF16)
        onesm = sb("onesm", [P, P], BF16)
        ones1 = sb("ones1", [P, P], BF16)
        neghalf = sb("neghalf", [P, 8], F32)
        NF = 6
        ftmp = [sb("ftmp%d" % i, [P, TT], F32) for i in range(NF)]
        bs32 = ftmp[0][0:1, :]
        bsh32 = ftmp[1][0:1, :]
        rstd_b = [sb("rstd%d" % i, [P, TT], F32) for i in range(2)]
        nmr_b = [sb("nmr%d" % i, [P, TT], F32) for i in range(2)]
        rsq_b = [sb("rsq%d" % i, [P, TT], BF16) for i in range(3)]
        gbuf = [sb("gbuf%d" % i, [P, TT + 2], F32) for i in range(4)]
        halo_c = sb("halo_c", [P, DC, 2], F32)
        halo_f = [sb("halo_f%d" % l, [P, FC, 2], F32) for l in range(2)]
        small = sb("small", [P, 64], F32)
        xbt = sb("xbt", [P, 4, TT], BF16)
        psum = [es.enter_context(nc.psum_tensor("ps%d" % i, [P, TT], F32)) for i in range(8)]

        t_xT32 = [[Tile("x32_%d_%d" % (c, t)) for t in range(NTT)] for c in range(DC)]
        t_xTb = [[Tile("xb_%d_%d" % (c, t)) for t in range(NTT)] for c in range(DC)]
        t_ar = [Tile("ar%d" % i) for i in range(NAR)]
        t_ws = [Tile("ws%d" % i) for i in range(NSLOT)]
        t_kT = [Tile("kT%d" % l) for l in range(2)]
        t_vt = [Tile("vt%d" % l) for l in range(2)]
        t_const = Tile("const")
        t_ft = [Tile("ft%d" % i) for i in range(NF)]
        t_rstd = [Tile("rstd%d" % i) for i in range(2)]
        t_nmr = [Tile("nmr%d" % i) for i in range(2)]
        t_rsq = [Tile("rsq%d" % i) for i in range(3)]
        t_gbuf = [Tile("gbuf%d" % i) for i in range(4)]
        t_halo_c = Tile("halo_c")
        t_halo_f = [Tile("halo_f%d" % l) for l in range(2)]
        t_small = Tile("small")
        t_xbt = [Tile("xbt%d" % i) for i in range(4)]
        t_ps = [Tile("ps%d" % i) for i in range(8)]
        t_out = Tile("out")

        def x32(c, tt):
            return xT32[:, c, tt * TT:(tt + 1) * TT]

        def xb(c, tt):
            return xTb[:, c, tt * TT:(tt + 1) * TT]

        def ar(i):
            return arena[:, i, :]

        ps_state = {"free": list(range(8)), "i": 0}

        def ps_next():
            fl = ps_state["free"]
            b = fl[ps_state["i"] % len(fl)]
            ps_state["i"] += 1
            return b

        ft_state = {"i": 0}

        def ft_next():
            i = ft_state["i"] % NF
            ft_state["i"] += 1
            return i

        wplan = []

        def wview(W, c0, n):
            return W.rearrange("(k p) n -> p k n", p=P)[:, :, c0:c0 + n]

        def plan_kv(l):
            for i in range(4):
                wplan.append((wview(xa_wkv[l], i * 512, 512), 8, 512))

        def plan_layer(l, kv_after_mix=False):
            if l == 0:
                for i in range(3):
                    wplan.append((wview(ab_w_in, i * 512, 512), 8, 512))
                for i in range(2):
                    wplan.append((wview(ab_w_out, i * 512, 512), 8, 512))
                if kv_after_mix:
                    plan_kv(0)
                    plan_kv(1)
            else:
                for q in range(2):
                    for part in (1, 2, 0):
                        wplan.append((wview(c_w_in, part * 1024 + q * 512, 512), 8, 512))
                for i in range(2):
                    wplan.append((wview(c_w_out, i * 512, 512), 8, 512))
            for i in range(2):
                wplan.append((wview(xa_wq[l], i * 512, 512), 8, 512))
            for i in range(2):
                wplan.append((wview(xa_wo[l], i * 512, 512), 8, 512))
            for q in range(6):
                n = 512 if q < 5 else 256
                wplan.append((wview(ffn_w_up[l], DFF + q * 512, n), 8, n))
                wplan.append((wview(ffn_w_up[l], q * 512, n), 8, n))
            for m in range(8):
                wplan.append((wview(ffn_w_down[l], m * 128, 128), FC, 128))

        for st in range(n_st):
            if st % 2 == 0 and st > 0:
                plan_kv(0)
                plan_kv(1)
            plan_layer(0, st == 0)
            plan_layer(1)

        wstate = {"issued": 0, "next": 0, "done": 0}

        def w_pump():
            lim = min(len(wplan), wstate["done"] + NSLOT)
            while wstate["issued"] < lim:
                j = wstate["issued"]
                src, kc, n = wplan[j]
                s = j % NSLOT
                dst = wring[s][:, 0:kc * n].rearrange("p (k n) -> p k n", k=kc)
                S.op("pool", lambda e, dst=dst, src=src: e.dma_start(out=dst, in_=src),
                     writes=[t_ws[s]], dma=True)
                wstate["issued"] += 1

        def w_release(upto):
            if upto > wstate["done"]:
                wstate["done"] = upto
            w_pump()

        def w_next(kc, n, release_prior=True):
            i = wstate["next"]
            wstate["next"] += 1
            assert wplan[i][1] == kc and wplan[i][2] == n, (i, wplan[i][1:], kc, n)
            if release_prior:
                w_release(i)
            else:
                w_pump()
            assert wstate["issued"] > i, "too many live weight slots"
            s = i % NSLOT
            view = wring[s][:, 0:kc * n].rearrange("p (k n) -> p k n", k=kc)
            return view, t_ws[s]

        S.op("sp", lambda e: e.dma_start(out=pv[:], in_=pv_d), writes=[t_const], dma=True)
        S.op("sp", lambda e: e.dma_start(out=sgub[:], in_=sgub_d), writes=[t_const], dma=True)
        S.op("sp", lambda e: e.dma_start(out=bs32, in_=sgubias_d), writes=[t_ft[0]], dma=True)
        t_c2 = Tile("const2")
        S.op("pool", lambda e: e.dma_start(out=WmT[:], in_=sguwT_d), writes=[t_c2], dma=True)
        t_c3 = Tile("const3")
        S.op("pool", lambda e: e.dma_start(out=poolmats[:], in_=poolmats_d), writes=[t_c3], dma=True)
        t_c4 = Tile("const4")
        S.op("pool", lambda e: e.dma_start(out=poolw[:], in_=poolw_d), writes=[t_c4], dma=True)
        S.op("dve", lambda e: e.memset(WmT[64:128, :, 0:64], 0.0), writes=[t_c2])
        t_c5 = Tile("const5")
        S.op("dve", lambda e: e.memset(onesm[:], 1.0 / 1024.0), writes=[t_c5])
        S.op("dve", lambda e: e.memset(ones1[:], 1.0), writes=[t_c5])
        S.op("dve", lambda e: e.memset(neghalf[:], -0.5), writes=[t_c5])
        t_bs = Tile("bs")
        S.op("dve", lambda e: e.tensor_copy(bs_hi[:], bs32), reads=[t_ft[0]], writes=[t_bs])
        S.op("dve", lambda e: e.tensor_copy(bsh32, bs_hi[:]), reads=[t_bs], writes=[t_ft[1]])
        S.op("dve", lambda e: e.tensor_tensor(bsh32, bs32, bsh32, ALU.subtract),
             reads=[t_ft[0], t_ft[1]], writes=[t_ft[1]])
        S.op("dve", lambda e: e.tensor_copy(bs_lo[:], bsh32), reads=[t_ft[1]], writes=[t_bs])
        t_consts_all = [t_const, t_c2, t_c3, t_c4, t_c5, t_bs]

        def pvc(key, i=0, n=1):
            c = PV[key] + i
            return pv[:, c:c + n]

        from collections import deque
        deferred = deque()

        def pump(k=1):
            for _ in range(k):
                if not deferred:
                    return
                deferred.popleft()[1]()

        def pump_crit():
            while deferred and deferred[0][0]:
                deferred.popleft()[1]()

        def flush():
            while deferred:
                deferred.popleft()[1]()

        def mm_unit(wv, wt, coff, KC, in_ap, in_tile, tt):
            b = ps_next()
            for k in range(KC):
                S.op("pe", lambda e, b=b, k=k: e.matmul(
                    psum[b][:], wv[:, k, coff:coff + P], in_ap(k, tt),
                    start=(k == 0), stop=(k == KC - 1)),
                    reads=[wt, in_tile(k, tt)], writes=[t_ps[b]])
            return b

        def proj_ln(groups, fetch, KC, in_ap, in_tile, gkey, bkey, l, boundary=None):
            saved_free = ps_state["free"]
            nfl = len(saved_free)
            order = [saved_free[(ps_state["i"] + j) % nfl] for j in range(nfl)]
            assert nfl == 8
            stat = [order[4], order[6], order[5], order[7]]
            ps_state["free"] = order[0:4]
            ps_state["i"] = 0
            nseen = [0, 0]

            def stats(m, tt, r, first, last):
                S.op("pe", lambda e: e.matmul(
                    psum[stat[tt]][:], onesm[:], xb(m, tt), start=first, stop=last),
                    reads=[t_c5, t_xTb[m][tt]], writes=[t_ps[stat[tt]]])
                S.op("pe", lambda e: e.matmul(
                    psum[stat[2 + tt]][:], onesm[:], rsq_b[r][:], start=first, stop=last),
                    reads=[t_c5, t_rsq[r]], writes=[t_ps[stat[2 + tt]]])

            def finalize_items(tt, after_head=None, per_chunk=None, after_all=None):
                items = []

                def head():
                    f0 = ft_next()
                    S.op("act", lambda e: e.activation(ftmp[f0][:], psum[stat[tt]][:], AF.Square),
                         reads=[t_ps[stat[tt]]], writes=[t_ft[f0]])
                    S.op("dve", lambda e: e.scalar_tensor_tensor(
                        ftmp[f0][:], psum[stat[2 + tt]][:], EPS, ftmp[f0][:], ALU.add, ALU.subtract),
                        reads=[t_ps[stat[2 + tt]], t_ft[f0]], writes=[t_ft[f0]])
                    S.op("act", lambda e: e.activation(ftmp[f0][:], ftmp[f0][:], AF.Ln),
                         reads=[t_ft[f0]], writes=[t_ft[f0]])
                    S.op("act", lambda e: e.activation(rstd_b[tt][:], ftmp[f0][:], AF.Exp, scale=-0.5),
                         reads=[t_ft[f0]], writes=[t_rstd[tt]])
                    S.op("dve", lambda e: e.scalar_tensor_tensor(
                        nmr_b[tt][:], psum[stat[tt]][:], -1.0, rstd_b[tt][:], ALU.mult, ALU.mult),
                        reads=[t_ps[stat[tt]], t_rstd[tt]], writes=[t_nmr[tt]])
                items.append((True, head))
                if after_head is not None:
                    items.append((True, after_head))
                late = []
                for m in range(DC):
                    def app(m=m):
                        eng = "dve"
                        S.op(eng, lambda e: e.tensor_tensor(
                            x32(m, tt), x32(m, tt), rstd_b[tt][:], ALU.mult),
                            reads=[t_xT32[m][tt], t_rstd[tt]], writes=[t_xT32[m][tt]])
                        S.op(eng, lambda e: e.tensor_tensor(
                            x32(m, tt), x32(m, tt), nmr_b[tt][:], ALU.add),
                            reads=[t_xT32[m][tt], t_nmr[tt]], writes=[t_xT32[m][tt]])
                        if per_chunk is None:
                            S.op("act", lambda e: e.activation(
                                xb(m, tt), x32(m, tt), AF.Identity,
                                bias=pvc((bkey, l), m), scale=pvc((gkey, l), m)),
                                reads=[t_xT32[m][tt], t_const], writes=[t_xTb[m][tt]])
                    items.append((True, app))

                    def aff(m=m):
                        if m in (3, 6, 7):
                            S.op("act", lambda e: e.activation(
                                x32(m, tt), x32(m, tt), AF.Identity,
                                bias=pvc((bkey, l), m), scale=pvc((gkey, l), m)),
                                reads=[t_xT32[m][tt], t_const], writes=[t_xT32[m][tt]])
                        else:
                            S.op("dve", lambda e: e.tensor_scalar(
                                x32(m, tt), x32(m, tt), pvc((gkey, l), m), pvc((bkey, l), m),
                                ALU.mult, ALU.add),
                                reads=[t_xT32[m][tt], t_const], writes=[t_xT32[m][tt]])
                    if per_chunk is not None:
                        items.append((True, aff))
                        items.append((True, per_chunk[m]))
                    else:
                        late.append((False, aff))
                items.extend(late)
                if after_all is not None:
                    items.extend((True, f) for f in after_all)
                return items

            STAT_LAG = 2
            pendq = deque()

            def emit_stats(p):
                stats(*p)
                if p[4] and p[1] == 0:
                    if boundary is not None:
                        deferred.extend(finalize_items(0, None, boundary[0], boundary[1]))
                    else:
                        deferred.extend(finalize_items(0))

            ng = len(groups)
            for gi, group in enumerate(groups):
                ws = fetch(group)
                for tt in range(NTT):
                    for m in group:
                        wv, wt, coff = ws[m]
                        b = mm_unit(wv, wt, coff, KC, in_ap, in_tile, tt)
                        S.op("dve", lambda e, m=m, tt=tt, b=b: e.scalar_tensor_tensor(
                            x32(m, tt), x32(m, tt), ALPHA, psum[b][:], ALU.mult, ALU.add),
                            reads=[t_xT32[m][tt], t_ps[b]], writes=[t_xT32[m][tt]])
                        S.op("act", lambda e, m=m, tt=tt: e.activation(xb(m, tt), x32(m, tt), AF.Copy),
                             reads=[t_xT32[m][tt]], writes=[t_xTb[m][tt]])
                        r = rsq_state["i"] % 3
                        rsq_state["i"] += 1
                        S.op("act", lambda e, m=m, tt=tt, r=r: e.activation(rsq_b[r][:], x32(m, tt), AF.Square),
                             reads=[t_xT32[m][tt]], writes=[t_rsq[r]])
                        first = nseen[tt] == 0
                        nseen[tt] += 1
                        last = nseen[tt] == DC
                        pendq.append((m, tt, r, first, last))
                        while len(pendq) > STAT_LAG:
                            emit_stats(pendq.popleft())
                        if boundary is not None:
                            pump_crit()
                        if len(group) >= 8:
                            pump(2 if m == group[0] else 1)
                        else:
                            pump(3)
                    if gi == ng - 1 and tt == 0:
                        flush()
            flush()
            while pendq:
                deferred.append((True, lambda p=pendq.popleft(): stats(*p)))

            def restore():
                ps_state["free"] = saved_free
            if boundary is not None:
                deferred.extend(finalize_items(1, restore, boundary[2], boundary[3]))
            else:
                deferred.extend(finalize_items(1, restore))

        def fetch_2x512(group_sizes=(8,)):
            def fetch(group):
                w0 = w_next(8, 512)
                w1 = w_next(8, 512, False)
                out = {}
                for m in group:
                    wv, wt = (w0, w1)[m // 4]
                    out[m] = (wv, wt, (m % 4) * P)
                return out
            return fetch

        def x_in_ap(k, tt):
            return xb(k, tt)

        def x_in_tile(k, tt):
            return t_xTb[k][tt]

        ALLM = [list(range(DC))]

        def mixer_ab(st):
            WA, tA = w_next(8, 512)
            iWA = wstate["next"] - 1
            WB, tB = w_next(8, 512, False)
            WC, tC = w_next(8, 512, False)
            seq_first_st = (st % 2 == 0)

            def uT(g, tt):
                return g * 2 + tt

            def yT(c, tt):
                return 8 + c * 2 + tt

            def pooledT(g, tt):
                return 24 + g * 2 + tt

            def u_units(tt):
                for m in range(4):
                    b = mm_unit(WA, tA, m * P, 8, x_in_ap, x_in_tile, tt)
                    i = uT(m, tt)
                    S.op("act", lambda e, b=b, i=i: e.activation(ar(i), psum[b][:], AF.Gelu_apprx_tanh),
                         reads=[t_ps[b]], writes=[t_ar[i]])
                    pump(2)

            binfo = {}

            def vxb(j):
                tt = j // 4
                gj = st * NBLK + j
                bv = ps_next()
                bx = ps_next()
                for k in range(8):
                    S.op("pe", lambda e, k=k: e.matmul(
                        psum[bv][:], xTb[:, k, j * P:(j + 1) * P], WB[:, k, :], start=(k == 0), stop=(k == 7)),
                        reads=[t_xTb[k][tt], tB], writes=[t_ps[bv]])
                for k in range(8):
                    S.op("pe", lambda e, k=k: e.matmul(
                        psum[bx][:], xTb[:, k, j * P:(j + 1) * P], WC[:, k, :], start=(k == 0), stop=(k == 7)),
                        reads=[t_xTb[k][tt], tC], writes=[t_ps[bx]])
                pump_crit()
                fv = ft_next()
                S.op("act", lambda e: e.activation(ftmp[fv][:], psum[bv][:], AF.Gelu_apprx_tanh),
                     reads=[t_ps[bv]], writes=[t_ft[fv]])
                ixb = gj % 4
                S.op("act", lambda e: e.activation(xbt[:, ixb, :], psum[bx][:], AF.Copy),
                     reads=[t_ps[bx]], writes=[t_xbt[ixb]])
                so = (gj % 4) * 16
                S.op("dve", lambda e: e.bn_stats(small[:, so:so + 6], ftmp[fv][:]),
                     reads=[t_ft[fv]], writes=[t_small])
                S.op("dve", lambda e: e.bn_aggr(small[:, so + 6:so + 8], small[:, so:so + 6]),
                     reads=[t_small], writes=[t_small])
                S.op("dve", lambda e: e.tensor_scalar(
                    small[:, so + 8:so + 9], small[:, so + 7:so + 8], EPS, None, ALU.add),
                    reads=[t_small], writes=[t_small])
                S.op("pool", lambda e: e.tensor_tensor(
                    small[:, so + 9:so + 10], small[:, so + 8:so + 9], neghalf[:, 0:1], ALU.pow),
                    reads=[t_small, t_c5], writes=[t_small])
                S.op("dve", lambda e: e.tensor_scalar(
                    ftmp[fv][:], ftmp[fv][:], small[:, so + 6:so + 7], small[:, so + 9:so + 10],
                    ALU.subtract, ALU.mult),
                    reads=[t_ft[fv], t_small], writes=[t_ft[fv]])
                S.op("dve", lambda e: e.tensor_tensor(ftmp[fv][:], ftmp[fv][:], sgub[:, 0:512], ALU.mult),
                     reads=[t_ft[fv], t_const], writes=[t_ft[fv]])
                ivl = 32 + gj % 4
                S.op("dve", lambda e: e.tensor_tensor(
                    ar(ivl), ftmp[fv][:], sgub[:, 512:1024], ALU.add),
                    reads=[t_ft[fv], t_const], writes=[t_ar[ivl]])
                binfo[j] = (ivl, ixb, (gj - 1) % 4)
                pump(2)

            def sgu_pool(j):
                tt = j // 4
                col = (j % 4) * P
                ivl, ixb, ixp = binfo[j]
                bs_ = ps_next()
                for g in range(4):
                    S.op("pe", lambda e, g=g: e.matmul(
                        psum[bs_][:, g * P:(g + 1) * P], arena[:, ivl, g * P:(g + 1) * P], WmT[:, g, :],
                        start=True, stop=False),
                        reads=[t_ar[ivl], t_c2], writes=[t_ps[bs_]])
                    S.op("pe", lambda e, g=g: e.matmul(
                        psum[bs_][:, g * P:(g + 1) * P], ones1[0:1, :], bs_hi[0:1, g * P:(g + 1) * P],
                        start=False, stop=False),
                        reads=[t_c5, t_bs], writes=[t_ps[bs_]])
                    S.op("pe", lambda e, g=g: e.matmul(
                        psum[bs_][:, g * P:(g + 1) * P], ones1[0:1, :], bs_lo[0:1, g * P:(g + 1) * P],
                        start=False, stop=True),
                        reads=[t_c5, t_bs], writes=[t_ps[bs_]])
                bp = ps_next()
                first = seq_first_st and j == 0
                for g in range(4):
                    if first:
                        S.op("pe", lambda e, g=g: e.matmul(
                            psum[bp][:, g * P:(g + 1) * P], xbt[:, ixb, g * P:(g + 1) * P], poolmats[:, 8 + g, :],
                            start=True, stop=False),
                            reads=[t_xbt[ixb], t_c3], writes=[t_ps[bp]])
                        S.op("pe", lambda e, g=g: e.matmul(
                            psum[bp][:, g * P:(g + 1) * P], xbt[:, ixb, g * P:(g + 1) * P], poolmats[:, 12 + g, :],
                            start=False, stop=False),
                            reads=[t_xbt[ixb], t_c3], writes=[t_ps[bp]])
                        S.op("pe", lambda e, g=g: e.matmul(
                            psum[bp][:, g * P:(g + 1) * P], xbt[:, ixb, g * P:(g + 1) * P], poolmats[:, 16 + g, :],
                            start=False, stop=True),
                            reads=[t_xbt[ixb], t_c3], writes=[t_ps[bp]])
                    else:
                        S.op("pe", lambda e, g=g: e.matmul(
                            psum[bp][:, g * P:(g + 1) * P], xbt[:, ixb, g * P:(g + 1) * P], poolmats[:, g, :],
                            start=True, stop=False),
                            reads=[t_xbt[ixb], t_c3], writes=[t_ps[bp]])
                        S.op("pe", lambda e, g=g: e.matmul(
                            psum[bp][:, g * P:(g + 1) * P], xbt[:, ixp, g * P:(g + 1) * P], poolmats[:, 4 + g, :],
                            start=False, stop=True),
                            reads=[t_xbt[ixp], t_c3], writes=[t_ps[bp]])
                for g in range(4):
                    iy = yT(g, tt)
                    iu = uT(g, tt)
                    S.op("dve", lambda e, g=g, iy=iy, iu=iu: e.tensor_tensor(
                        arena[:, iy, col:col + P], psum[bs_][:, g * P:(g + 1) * P], arena[:, iu, col:col + P],
                        ALU.mult),
                        reads=[t_ps[bs_], t_ar[iu]], writes=[t_ar[iy]])
                for g in range(4):
                    ip = pooledT(g, tt)
                    S.op("act", lambda e, g=g, ip=ip: e.activation(
                        arena[:, ip, col:col + P], psum[bp][:, g * P:(g + 1) * P], AF.Copy),
                        reads=[t_ps[bp]], writes=[t_ar[ip]])
                pump(1)

            def yb(tt):
                for g in range(4):
                    b = ps_next()
                    ip = pooledT(g, tt)
                    S.op("pe", lambda e, g=g, b=b, ip=ip: e.matmul(
                        psum[b][:], poolw[:, g, :], ar(ip), start=True, stop=True),
                        reads=[t_c4, t_ar[ip]], writes=[t_ps[b]])
                    iy = yT(4 + g, tt)
                    S.op("act", lambda e, g=g, b=b, iy=iy: e.activation(
                        ar(iy), psum[b][:], AF.Identity, scale=pvc("pool_scale", g)),
                        reads=[t_ps[b], t_const], writes=[t_ar[iy]])

            vxb(0)
            vxb(1)
            u_units(0)
            vxb(2)
            sgu_pool(0)
            vxb(3)
            sgu_pool(1)
            flush()
            vxb(4)
            sgu_pool(2)
            vxb(5)
            sgu_pool(3)
            u_units(1)
            yb(0)
            vxb(6)
            sgu_pool(4)
            vxb(7)
            sgu_pool(5)
            sgu_pool(6)
            sgu_pool(7)
            yb(1)
            w_release(iWA + 3)
            flush()
            proj_ln(ALLM, fetch_2x512(), 8, lambda k, tt: ar(yT(k, tt)), lambda k, tt: t_ar[yT(k, tt)],
                    "mix_g", "mix_b", 0)

        def conv3(gb_i, out_f, wcol):
            S.op("dve", lambda e: e.tensor_scalar(
                ftmp[out_f][:], gbuf[gb_i][:, 0:TT], pv[:, wcol:wcol + 1], None, ALU.mult),
                reads=[t_gbuf[gb_i], t_const], writes=[t_ft[out_f]])
            S.op("dve", lambda e: e.scalar_tensor_tensor(
                ftmp[out_f][:], gbuf[gb_i][:, 1:TT + 1], pv[:, wcol + 1:wcol + 2], ftmp[out_f][:],
                ALU.mult, ALU.add),
                reads=[t_gbuf[gb_i], t_const, t_ft[out_f]], writes=[t_ft[out_f]])
            S.op("dve", lambda e: e.scalar_tensor_tensor(
                ftmp[out_f][:], gbuf[gb_i][:, 2:TT + 2], pv[:, wcol + 2:wcol + 3], ftmp[out_f][:],
                ALU.mult, ALU.add),
                reads=[t_gbuf[gb_i], t_const, t_ft[out_f]], writes=[t_ft[out_f]])

        def halo_io(gi, halo_ap, halo_tile):
            S.op("pool", lambda e: e.tensor_copy(gbuf[gi][:, 0:2], halo_ap),
                 reads=[halo_tile], writes=[t_gbuf[gi]])
            S.op("pool", lambda e: e.tensor_copy(halo_ap, gbuf[gi][:, TT:TT + 2]),
                 reads=[t_gbuf[gi]], writes=[halo_tile])

        gb_state = {"i": 0}
        rsq_state = {"i": 0}

        def gb_next():
            i = gb_state["i"] % 4
            gb_state["i"] += 1
            return i

        def mixer_c(st):
            def yT(c, tt):
                return c * 2 + tt
            if st % 2 == 0:
                S.op("pool", lambda e: e.memset(halo_c[:], 0.0), writes=[t_halo_c])
            for q in range(2):
                wc_ = w_next(8, 512)
                wh_ = w_next(8, 512, False)
                wb_ = w_next(8, 512, False)
                for tt in range(NTT):
                    if q == 0 and tt == 1:
                        flush()
                    for m in range(q * 4, q * 4 + 4):
                        coff = (m % 4) * P
                        bc = mm_unit(wc_[0], wc_[1], coff, 8, x_in_ap, x_in_tile, tt)
                        bh = mm_unit(wh_[0], wh_[1], coff, 8, x_in_ap, x_in_tile, tt)
                        pump_crit()
                        fc_ = ft_next()
                        S.op("act", lambda e, fc_=fc_, bc=bc: e.activation(ftmp[fc_][:], psum[bc][:], AF.Copy),
                             reads=[t_ps[bc]], writes=[t_ft[fc_]])
                        gi = gb_next()
                        S.op("dve", lambda e, fc_=fc_, bh=bh, gi=gi: e.tensor_tensor(
                            gbuf[gi][:, 2:TT + 2], ftmp[fc_][:], psum[bh][:], ALU.mult),
                            reads=[t_ft[fc_], t_ps[bh]], writes=[t_gbuf[gi]])
                        halo_io(gi, halo_c[:, m, :], t_halo_c)
                        fo = ft_next()
                        conv3(gi, fo, PV["c_conv_w"] + m * 3)
                        bb = mm_unit(wb_[0], wb_[1], coff, 8, x_in_ap, x_in_tile, tt)
                        iy = yT(m, tt)
                        S.op("dve", lambda e, bb=bb, fo=fo, iy=iy: e.tensor_tensor(
                            ar(iy), ftmp[fo][:], psum[bb][:], ALU.mult),
                            reads=[t_ft[fo], t_ps[bb]], writes=[t_ar[iy]])
                        pump(3)
            flush()
            proj_ln(ALLM, fetch_2x512(), 8, lambda k, tt: ar(yT(k, tt)), lambda k, tt: t_ar[yT(k, tt)],
                    "mix_g", "mix_b", 1)

        def kv_phase(b, l):
            mT = arena[:, 40:44, :].rearrange("p a (c m) -> p (a c) m", m=MEM)
            t_m = t_ar[40:44]
            if l == 0:
                S.op("pool", lambda e: e.dma_start(out=mT, in_=memT_d[b].rearrange("c p m -> p c m")),
                     writes=list(t_m), dma=True)
            ws = [w_next(8, 512, i == 0) for i in range(4)]
            for c in range(DC):
                wv, wt = ws[c // 4]
                bk = ps_next()
                for k in range(8):
                    S.op("pe", lambda e, k=k, bk=bk, wv=wv, c=c: e.matmul(
                        psum[bk][:, 0:MEM], wv[:, k, (c % 4) * P:(c % 4 + 1) * P], mT[:, k, :],
                        start=(k == 0), stop=(k == 7)),
                        reads=[wt] + list(t_m), writes=[t_ps[bk]])
                S.op("act", lambda e, bk=bk, c=c: e.activation(kT[l][:, c, :], psum[bk][:, 0:MEM], AF.Copy),
                     reads=[t_ps[bk]], writes=[t_kT[l]])
            for mc in range(2):
                for n in range(2):
                    wv, wt = ws[2 + n]
                    bk = ps_next()
                    for k in range(8):
                        S.op("pe", lambda e, k=k, bk=bk, wv=wv, mc=mc: e.matmul(
                            psum[bk][:], mT[:, k, mc * P:(mc + 1) * P], wv[:, k, :],
                            start=(k == 0), stop=(k == 7)),
                            reads=[wt] + list(t_m), writes=[t_ps[bk]])
                    S.op("dve", lambda e, bk=bk, mc=mc, n=n: e.tensor_copy(
                        vtok[l][:, mc, n * TT:(n + 1) * TT], psum[bk][:]),
                        reads=[t_ps[bk]], writes=[t_vt[l]])

        def xattn(st, l):
            def qT(c, tt):
                return c * 2 + tt

            def pT(h, mc):
                return 16 + h * 2 + mc

            def aoT(c, tt):
                return 24 + c * 2 + tt
            wq0 = w_next(8, 512)
            wq1 = w_next(8, 512, False)

            def q_unit(m, tt):
                wv, wt = (wq0, wq1)[m // 4]
                b = mm_unit(wv, wt, (m % 4) * P, 8, x_in_ap, x_in_tile, tt)
                pump_crit()
                i = qT(m, tt)
                if m % 2 == 0:
                    S.op("act", lambda e: e.activation(ar(i), psum[b][:], AF.Identity, scale=1.0 / 16.0),
                         reads=[t_ps[b]], writes=[t_ar[i]])
                else:
                    S.op("dve", lambda e: e.tensor_scalar(
                        ar(i), psum[b][:], 1.0 / 16.0, None, ALU.mult),
                        reads=[t_ps[b]], writes=[t_ar[i]])

            def scores(h, tt):
                for mc in range(2):
                    b = ps_next()
                    for kc in range(2):
                        iq = qT(2 * h + kc, tt)
                        S.op("pe", lambda e, b=b, kc=kc, mc=mc, iq=iq: e.matmul(
                            psum[b][:], kT[l][:, 2 * h + kc, mc * P:(mc + 1) * P], ar(iq),
                            start=(kc == 0), stop=(kc == 1)),
                            reads=[t_kT[l], t_ar[iq]], writes=[t_ps[b]])
                    ip = pT(h, mc)
                    S.op("act", lambda e, b=b, ip=ip: e.activation(ar(ip), psum[b][:], AF.Exp),
                         reads=[t_ps[b]], writes=[t_ar[ip]])

            def den_pv(h, tt):
                bd = ps_next()
                for mc in range(2):
                    ip = pT(h, mc)
                    S.op("pe", lambda e, mc=mc, ip=ip: e.matmul(
                        psum[bd][:], ones1[:], ar(ip), start=(mc == 0), stop=(mc == 1)),
                        reads=[t_c5, t_ar[ip]], writes=[t_ps[bd]])
                fr = ft_next()
                S.op("act", lambda e: e.activation(ftmp[fr][:], psum[bd][:], AF.Ln),
                     reads=[t_ps[bd]], writes=[t_ft[fr]])
                S.op("act", lambda e: e.activation(ftmp[fr][:], ftmp[fr][:], AF.Exp, scale=-1.0),
                     reads=[t_ft[fr]], writes=[t_ft[fr]])
                for dc in range(2):
                    bo = ps_next()
                    for mc in range(2):
                        ip = pT(h, mc)
                        S.op("pe", lambda e, bo=bo, mc=mc, ip=ip, dc=dc: e.matmul(
                            psum[bo][:], vtok[l][:, mc, (2 * h + dc) * P:(2 * h + dc + 1) * P], ar(ip),
                            start=(mc == 0), stop=(mc == 1)),
                            reads=[t_vt[l], t_ar[ip]], writes=[t_ps[bo]])
                    io = aoT(2 * h + dc, tt)
                    S.op("dve", lambda e, bo=bo, io=io: e.tensor_tensor(
                        ar(io), psum[bo][:], ftmp[fr][:], ALU.mult),
                        reads=[t_ps[bo], t_ft[fr]], writes=[t_ar[io]])

            for m in range(DC):
                q_unit(m, 0)
                pump(2)
            scores(0, 0)
            scores(1, 0)
            flush()
            q_unit(0, 1)
            q_unit(1, 1)
            for h in range(4):
                if h + 2 < 4:
                    scores(h + 2, 0)
                q_unit(2 + h, 1)
                den_pv(h, 0)
            q_unit(6, 1)
            q_unit(7, 1)
            scores(0, 1)
            scores(1, 1)
            for h in range(4):
                if h + 2 < 4:
                    scores(h + 2, 1)
                den_pv(h, 1)
            flush()
            proj_ln(ALLM, fetch_2x512(), 8, lambda k, tt: ar(aoT(k, tt)), lambda k, tt: t_ar[aoT(k, tt)],
                    "xa_g", "xa_b", l)

        def ffn(st, l, boundary=None):
            def hT(f, tt):
                return f * 2 + tt
            if st % 2 == 0:
                S.op("pool", lambda e: e.memset(halo_f[l][:], 0.0), writes=[t_halo_f[l]])
            pend = [None]

            def gelu_of(p):
                fo, f, ba, ih = p
                S.op("act", lambda e: e.activation(
                    ftmp[fo][:], ftmp[fo][:], AF.Gelu_apprx_tanh, bias=pvc(("ffn_conv_b", l), f)),
                    reads=[t_ft[fo], t_const], writes=[t_ft[fo]])

            def mult_of(p):
                fo, f, ba, ih = p
                S.op("dve", lambda e: e.tensor_tensor(
                    ar(ih), ftmp[fo][:], psum[ba][:], ALU.mult),
                    reads=[t_ft[fo], t_ps[ba]], writes=[t_ar[ih]])

            for q in range(6):
                n = 512 if q < 5 else 256
                wg, tg = w_next(8, n)
                wa, ta = w_next(8, n, False)
                fs = list(range(q * 4, min(FC, q * 4 + 4)))
                for tt in range(NTT):
                    if q == 0 and tt == 1:
                        flush()
                    for f in fs:
                        coff = (f % 4) * P
                        bg = mm_unit(wg, tg, coff, 8, x_in_ap, x_in_tile, tt)
                        ba = mm_unit(wa, ta, coff, 8, x_in_ap, x_in_tile, tt)
                        pump_crit()
                        gi = gb_next()
                        S.op("act", lambda e, gi=gi, bg=bg: e.activation(gbuf[gi][:, 2:TT + 2], psum[bg][:], AF.Copy),
                             reads=[t_ps[bg]], writes=[t_gbuf[gi]])
                        halo_io(gi, halo_f[l][:, f, :], t_halo_f[l])
                        if pend[0] is not None:
                            gelu_of(pend[0])
                        fo = ft_next()
                        conv3(gi, fo, PV[("ffn_conv_w", l)] + f * 3)
                        if pend[0] is not None:
                            mult_of(pend[0])
                        pend[0] = (fo, f, ba, hT(f, tt))
                        pump(3)
            gelu_of(pend[0])
            mult_of(pend[0])
            flush()

            def fetch_down(group):
                out = {}
                for i, m in enumerate(group):
                    wv, wt = w_next(FC, 128, i == 0)
                    out[m] = (wv, wt, 0)
                return out
            proj_ln([[0, 1, 2, 3], [4, 5, 6, 7]], fetch_down, FC,
                    lambda k, tt: ar(hT(k, tt)), lambda k, tt: t_ar[hT(k, tt)], "ffn_g", "ffn_b", l,
                    boundary=boundary)

        def load_tile(st, c, tt):
            b = st // 2
            s0 = (st % 2) * STW
            S.op("sp", lambda e: e.dma_start(
                out=x32(c, tt), in_=xT_d[b, c, :, s0 + tt * TT:s0 + (tt + 1) * TT]),
                writes=[t_xT32[c][tt]], dma=True)
            S.op("pool", lambda e: e.dma_start(
                out=xb(c, tt), in_=xT_d[b, c, :, s0 + tt * TT:s0 + (tt + 1) * TT]),
                writes=[t_xTb[c][tt]], dma=True)

        def store_tile(st, c, tt):
            b = st // 2
            s0 = (st % 2) * STW
            S.op("sp", lambda e: e.dma_start(
                out=outT_d[b, c, :, s0 + tt * TT:s0 + (tt + 1) * TT], in_=x32(c, tt)),
                reads=[t_xT32[c][tt]], writes=[t_out], dma=True)

        done = False
        for st in range(n_st):
            b = st // 2
            if st == 0:
                for tt in range(NTT):
                    for c in range(DC):
                        load_tile(0, c, tt)
            elif st % 2 == 0:
                kv_phase(b, 0)
                kv_phase(b, 1)
            for l in range(2):
                if l == 0:
                    mixer_ab(st)
                else:
                    mixer_c(st)
                if stop_after == (l, "mix"):
                    done = True
                    break
                if st == 0 and l == 0:
                    kv_phase(b, 0)
                    kv_phase(b, 1)
                xattn(st, l)
                if stop_after == (l, "xa"):
                    done = True
                    break
                bnd = None
                if l == 1 and stop_after is None:
                    nxt = st < n_st - 1
                    bnd = ([(lambda st=st, c=c: store_tile(st, c, 0)) for c in range(DC)],
                           [(lambda st=st, c=c: load_tile(st + 1, c, 0)) for c in range(DC)] if nxt else [],
                           [(lambda st=st, c=c: store_tile(st, c, 1)) for c in range(DC)],
                           [(lambda st=st, c=c: load_tile(st + 1, c, 1)) for c in range(DC)] if nxt else [])
                ffn(st, l, bnd)
                if stop_after == (l, "ffn"):
                    done = True
                    break
            last = done or st == n_st - 1
            if last:
                flush()
                if stop_after is not None:
                    for tt in range(NTT):
                        for c in range(DC):
                            store_tile(st, c, tt)
                break
        S.emit(nc)
    return nc, len(S.recs)


def _bf16_round(a):
    u = np.ascontiguousarray(a, dtype=np.float32).view(np.uint32).astype(np.uint64)
    r = ((u + 0x7FFF + ((u >> 16) & 1)) >> 16) << 16
    return r.astype(np.uint32).view(np.float32)


def _pool_mats():
    out = np.zeros((P, 20, P), np.float64)
    t = np.arange(P)
    for g, w in enumerate(POOL_WINDOWS):
        cur = np.zeros((P, P))
        prev = np.zeros((P, P))
        first = np.zeros((P, P))
        for tc in range(P):
            for tp in range(max(0, tc - w + 1), tc + 1):
                cur[tp, tc] += 1.0 / w
            for d in range(tc - w + 1, 0):
                prev[P + d, tc] += 1.0 / w
            cnt = min(tc + 1, w)
            for tp in range(max(0, tc - w + 1), tc + 1):
                first[tp, tc] += 1.0 / cnt
        cur -= np.eye(P)
        first -= np.eye(P)
        out[:, g, :] = cur
        out[:, 4 + g, :] = prev
        hi = _bf16_round(first.astype(np.float32)).astype(np.float64)
        lo = _bf16_round((first - hi).astype(np.float32)).astype(np.float64)
        lo2 = _bf16_round((first - hi - lo).astype(np.float32)).astype(np.float64)
        out[:, 8 + g, :] = hi
        out[:, 12 + g, :] = lo
        out[:, 16 + g, :] = lo2
    return out.astype(np.float32)


def _cols(v, nchunk):
    return np.ascontiguousarray(np.asarray(v, np.float32).reshape(nchunk, P).T)


def _prep_shared(inp):
    pvh = np.zeros((P, NPV), np.float32)
    for l in range(2):
        for n, key in (("mix_g", "ln_mix_g"), ("mix_b", "ln_mix_b"), ("xa_g", "ln_xa_g"),
                       ("xa_b", "ln_xa_b"), ("ffn_g", "ln_ffn_g"), ("ffn_b", "ln_ffn_b")):
            c = PV[(n, l)]
            pvh[:, c:c + 8] = _cols(inp[key][l], 8)
        c = PV[("ffn_conv_w", l)]
        w = np.asarray(inp["ffn_conv_w"][l], np.float32)
        pvh[:, c:c + 66] = w.reshape(3, FC, P).transpose(2, 1, 0).reshape(P, 66)
        c = PV[("ffn_conv_b", l)]
        pvh[:, c:c + 22] = _cols(inp["ffn_conv_b"][l], FC)
    c = PV["pool_scale"]
    pvh[:, c:c + 4] = _cols(inp["pool_scale"][0], 4)
    c = PV["c_conv_w"]
    w = np.asarray(inp["c_conv_w"][0], np.float32)
    pvh[:, c:c + 24] = w.reshape(3, DC, P).transpose(2, 1, 0).reshape(P, 24)
    sgub = np.empty((P, 1024), np.float32)
    sgub[:, 0:512] = np.asarray(inp["sgu_ln_g"][0], np.float32)[None, :]
    sgub[:, 512:1024] = np.asarray(inp["sgu_ln_b"][0], np.float32)[None, :]
    shared = {
        "ab_w_in": np.ascontiguousarray(inp["ab_w_in"][0], dtype=np.float32),
        "ab_w_out": np.ascontiguousarray(inp["ab_w_out"][0], dtype=np.float32),
        "c_w_in": np.ascontiguousarray(inp["c_w_in"][0], dtype=np.float32),
        "c_w_out": np.ascontiguousarray(inp["c_w_out"][0], dtype=np.float32),
        "xa_wq": np.ascontiguousarray(inp["xa_wq"], dtype=np.float32),
        "xa_wkv": np.ascontiguousarray(inp["xa_wkv"], dtype=np.float32),
        "xa_wo": np.ascontiguousarray(inp["xa_wo"], dtype=np.float32),
        "ffn_w_up": np.ascontiguousarray(inp["ffn_w_up"], dtype=np.float32),
        "ffn_w_down": np.ascontiguousarray(inp["ffn_w_down"], dtype=np.float32),
        "pv": pvh,
        "sgub": sgub,
        "sguwT": np.ascontiguousarray(np.asarray(inp["sgu_w"][0], np.float32).transpose(2, 0, 1)),
        "sgubias": np.ascontiguousarray(np.asarray(inp["sgu_b"][0], np.float32).reshape(1, 512)),
        "poolmats": _pool_mats(),
        "poolw": np.ascontiguousarray(np.asarray(inp["pool_w"][0], np.float32).transpose(1, 0, 2)),
    }
    return shared


_CACHE = {}


def _get_program(n_st=4, stop_after=None):
    key = (n_st, stop_after)
    if key not in _CACHE:
        _CACHE[key] = build_program(n_st, stop_after)[0]
    return _CACHE[key]


def kernel(**inputs):
    inp = {k: np.asarray(v) for k, v in inputs.items()}
    n = 8
    shared = _prep_shared(inp)
    x = np.asarray(inp["x"], np.float32)
    mem = np.asarray(inp["mem"], np.float32)
    in_maps = []
    for i in range(n):
        xs = x[2 * i:2 * i + 2]
        xT = np.ascontiguousarray(xs.transpose(0, 2, 1)).reshape(NB_LOCAL, DC, P, SEQ)
        ms = mem[2 * i:2 * i + 2]
        mT = np.ascontiguousarray(ms.transpose(0, 2, 1)).reshape(NB_LOCAL, DC, P, MEM)
        d = dict(shared)
        d["xT"] = xT
        d["memT"] = mT
        in_maps.append(d)
    nc = _get_program()
    res = run_bass_kernel_spmd(nc, in_maps, core_ids=list(range(n)))
    outs = []
    for i in range(n):
        oT = np.asarray(res.results[i]["outT"]).reshape(NB_LOCAL, D, SEQ)
        outs.append(oT.transpose(0, 2, 1))
    return np.ascontiguousarray(np.concatenate(outs, axis=0), dtype=np.float32)
```

```python
import numpy as np
from contextlib import ExitStack
import concourse.bass as bass
import concourse.mybir as mybir
from concourse.bass_utils import run_bass_kernel_spmd

F32 = mybir.dt.float32
BF16 = mybir.dt.bfloat16
AF = mybir.ActivationFunctionType
ALU = mybir.AluOpType

ENGS = ("pe", "act", "dve", "pool", "sp")
NDMASEM = 16


class Tile:
    __slots__ = ("name", "lastw", "readers", "dreaders")

    def __init__(self, name=""):
        self.name = name
        self.lastw = None
        self.readers = {}
        self.dreaders = []


class Rec:
    __slots__ = ("eng", "fn", "deps", "dma", "signal", "sigval", "dmaidx")

    def __init__(self, eng, fn, deps, dma):
        self.eng = eng
        self.fn = fn
        self.deps = deps
        self.dma = dma
        self.signal = False
        self.sigval = 0
        self.dmaidx = -1


class Sched:
    def __init__(self):
        self.recs = []
        self.ndma = {e: 0 for e in ENGS}

    def op(self, eng, fn, reads=(), writes=(), dma=False):
        idx = len(self.recs)
        recs = self.recs
        deps = set()
        rawset = set()
        for t in reads:
            if t.lastw is not None:
                deps.add(t.lastw)
                rawset.add(t.lastw)
        for t in writes:
            if t.lastw is not None:
                deps.add(t.lastw)
            deps.update(t.readers.values())
            deps.update(t.dreaders)
        real = []
        for d in deps:
            r = recs[d]
            if r.eng == eng and not r.dma and not dma:
                if eng == "pe":
                    continue
                if d not in rawset:
                    continue
            real.append(d)
        rec = Rec(eng, fn, real, dma)
        if dma:
            rec.dmaidx = self.ndma[eng]
            self.ndma[eng] += 1
        recs.append(rec)
        for d in real:
            recs[d].signal = True
        for t in reads:
            if dma:
                t.dreaders.append(idx)
            else:
                t.readers[eng] = idx
        for t in writes:
            t.lastw = idx
            t.readers = {}
            t.dreaders = []
        return idx

    def emit(self, nc, final_dma_wait_eng="sp"):
        recs = self.recs
        cnt = {e: 0 for e in ENGS}
        for r in recs:
            if r.dma:
                r.signal = True
                continue
            if r.signal:
                cnt[r.eng] += 1
                r.sigval = cnt[r.eng]
        with ExitStack() as es:
            prog = {e: es.enter_context(nc.semaphore("prog_" + e)) for e in ENGS}
            dsem = {}
            for e in ENGS:
                if self.ndma[e] > 0:
                    dsem[e] = [es.enter_context(nc.semaphore("dma_%s_%d" % (e, i)))
                               for i in range(min(NDMASEM, self.ndma[e]))]
            block = es.enter_context(nc.Block())

            nsem = {e: len(dsem[e]) for e in dsem}
            know = {e: {x: 0 for x in ENGS} for e in ENGS}
            dwaited = {e: {} for e in ENGS}
            sigcount = {e: 0 for e in ENGS}
            vc = [None] * len(recs)
            plan = [None] * len(recs)
            nw_before = 0
            nw_after = 0
            for i, r in enumerate(recs):
                E = r.eng
                K = know[E]
                waits = {}
                merged = []
                seen_old = set()
                for d in r.deps:
                    rd = recs[d]
                    if rd.dma:
                        k = nsem[rd.eng]
                        key = ("d", rd.eng, rd.dmaidx % k)
                        val = 16 * (rd.dmaidx // k + 1)
                        if dwaited[E].get(key, 0) < val:
                            waits[key] = max(waits.get(key, 0), val)
                            merged.append(d)
                    else:
                        X = rd.eng
                        seen_old.add(X)
                        if K[X] < rd.sigval:
                            waits[("p", X)] = max(waits.get(("p", X), 0), rd.sigval)
                            merged.append(d)
                nw_before += len(seen_old)
                if r.dma:
                    k = nsem[E]
                    if r.dmaidx >= k:
                        key = ("d", E, r.dmaidx % k)
                        val = 16 * (r.dmaidx // k)
                        if dwaited[E].get(key, 0) < val:
                            waits[key] = max(waits.get(key, 0), val)
                for key, val in waits.items():
                    if key[0] == "d":
                        dwaited[E][key] = val
                    else:
                        nw_after += 1
                        if K[key[1]] < val:
                            K[key[1]] = val
                for d in merged:
                    vd = vc[d]
                    for x in ENGS:
                        if vd[x] > K[x]:
                            K[x] = vd[x]
                v = dict(K)
                if not r.dma:
                    if r.signal:
                        sigcount[E] = r.sigval
                    if sigcount[E] > v[E]:
                        v[E] = sigcount[E]
                vc[i] = v
                plan[i] = list(waits.items())

            def sem_of(key):
                if key[0] == "d":
                    return dsem[key[1]][key[2]]
                return prog[key[1]]

            def run(eng_name, e):
                for i, r in enumerate(recs):
                    if r.eng != eng_name:
                        continue
                    todo = [(sem_of(key), val) for key, val in plan[i]]
                    for sem, val in todo[:-1]:
                        e.wait_ge(sem, val)
                    ins = r.fn(e)
                    if todo:
                        ins._wait_ge(todo[-1][0], todo[-1][1])
                    if r.dma:
                        k = len(dsem[r.eng])
                        ins.then_inc(dsem[r.eng][r.dmaidx % k], 16)
                    elif r.signal:
                        ins.then_inc(prog[r.eng], 1)
                if eng_name == final_dma_wait_eng:
                    for en in ENGS:
                        n = self.ndma[en]
                        if n == 0:
                            continue
                        k = len(dsem[en])
                        for j in range(k):
                            uses = (n - j + k - 1) // k
                            if uses > 0:
                                e.wait_ge(dsem[en][j], 16 * uses)

            @block.tensor
            def _(e):
                run("pe", e)

            @block.scalar
            def _(e):
                run("act", e)

            @block.vector
            def _(e):
                run("dve", e)

            @block.gpsimd
            def _(e):
                run("pool", e)

            @block.sync
            def _(e):
                run("sp", e)


P = 128
D = 1024
DC = 8
SEQ = 2048
NB_LOCAL = 2
STW = 1024
TT = 512
NTT = 2
NBLK = 8
DFF = 2816
FC = 22
MEM = 256
ALPHA = float(4 ** 0.25)
EPS = 1e-5
NSLOT = 6
POOL_WINDOWS = (2, 4, 8, 16)

PV = {}
_c = 0
for _l in range(2):
    for _n in ("mix_g", "mix_b", "xa_g", "xa_b", "ffn_g", "ffn_b"):
        PV[(_n, _l)] = _c
        _c += 8
PV["pool_scale"] = _c
_c += 4
PV["c_conv_w"] = _c
_c += 24
for _l in range(2):
    PV[("ffn_conv_w", _l)] = _c
    _c += 66
for _l in range(2):
    PV[("ffn_conv_b", _l)] = _c
    _c += 22
NPV = _c


def build_program(n_st=4, stop_after=None):
    nc = bass.Bass("TRN2", target_bir_lowering=False)
    S = Sched()

    def dram_in(name, shape):
        return nc.dram_tensor(name, list(shape), F32, kind="ExternalInput").ap()

    xT_d = dram_in("xT", [NB_LOCAL, DC, P, SEQ])
    memT_d = dram_in("memT", [NB_LOCAL, DC, P, MEM])
    ab_w_in = dram_in("ab_w_in", [D, 1536])
    ab_w_out = dram_in("ab_w_out", [D, D])
    c_w_in = dram_in("c_w_in", [D, 3072])
    c_w_out = dram_in("c_w_out", [D, D])
    xa_wq = dram_in("xa_wq", [2, D, D])
    xa_wkv = dram_in("xa_wkv", [2, D, 2 * D])
    xa_wo = dram_in("xa_wo", [2, D, D])
    ffn_w_up = dram_in("ffn_w_up", [2, D, 2 * DFF])
    ffn_w_down = dram_in("ffn_w_down", [2, DFF, D])
    pv_d = dram_in("pv", [P, NPV])
    sgub_d = dram_in("sgub", [P, 1024])
    sguwT_d = dram_in("sguwT", [P, 4, P])
    sgubias_d = dram_in("sgubias", [1, 512])
    poolmats_d = dram_in("poolmats", [P, 20, P])
    poolw_d = dram_in("poolw", [P, 4, P])
    outT_d = nc.dram_tensor("outT", [NB_LOCAL, DC, P, SEQ], F32, kind="ExternalOutput").ap()

    with ExitStack() as es:
        def sb(name, shape, dt):
            return es.enter_context(nc.sbuf_tensor("sb_" + name, list(shape), dt))

        xT32 = sb("xT32", [P, DC, STW], F32)
        xTb = sb("xTb", [P, DC, STW], BF16)
        NAR = 44
        arena = sb("arena", [P, NAR, TT], BF16)
        wring = [sb("wslot%d" % i, [P, 4096], BF16) for i in range(NSLOT)]
        kT = [sb("kT%d" % l, [P, DC, MEM], BF16) for l in range(2)]
        vtok = [sb("vtok%d" % l, [P, 2, D], BF16) for l in range(2)]
        pv = sb("pv", [P, NPV], F32)
        sgub = sb("sgub", [P, 1024], F32)
        WmT = sb("WmT", [P, 4, P], BF16)
        bs_hi = sb("bs_hi", [1, 512], BF16)
        bs_lo = sb("bs_lo", [1, 512], BF16)
        poolmats = sb("poolmats", [P, 20, P], BF16)
        poolw = sb("poolw", [P, 4, P], BF16)
        onesm = sb("onesm", [P, P], BF16)
        ones1 = sb("ones1", [P, P], BF16)
        neghalf = sb("neghalf", [P, 8], F32)
        NF = 6
        ftmp = [sb("ftmp%d" % i, [P, TT], F32) for i in range(NF)]
        bs32 = ftmp[0][0:1, :]
        bsh32 = ftmp[1][0:1, :]
        rstd_b = [sb("rstd%d" % i, [P, TT], F32) for i in range(2)]
        nmr_b = [sb("nmr%d" % i, [P, TT], F32) for i in range(2)]
        rsq_b = [sb("rsq%d" % i, [P, TT], BF16) for i in range(3)]
        gbuf = [sb("gbuf%d" % i, [P, TT + 2], F32) for i in range(4)]
        halo_c = sb("halo_c", [P, DC, 2], F32)
        halo_f = [sb("halo_f%d" % l, [P, FC, 2], F32) for l in range(2)]
        small = sb("small", [P, 64], F32)
        xbt = sb("xbt", [P, 4, TT], BF16)
        psum = [es.enter_context(nc.psum_tensor("ps%d" % i, [P, TT], F32)) for i in range(8)]

        t_xT32 = [[Tile("x32_%d_%d" % (c, t)) for t in range(NTT)] for c in range(DC)]
        t_xTb = [[Tile("xb_%d_%d" % (c, t)) for t in range(NTT)] for c in range(DC)]
        t_ar = [Tile("ar%d" % i) for i in range(NAR)]
        t_ws = [Tile("ws%d" % i) for i in range(NSLOT)]
        t_kT = [Tile("kT%d" % l) for l in range(2)]
        t_vt = [Tile("vt%d" % l) for l in range(2)]
        t_const = Tile("const")
        t_ft = [Tile("ft%d" % i) for i in range(NF)]
        t_rstd = [Tile("rstd%d" % i) for i in range(2)]
        t_nmr = [Tile("nmr%d" % i) for i in range(2)]
        t_rsq = [Tile("rsq%d" % i) for i in range(3)]
        t_gbuf = [Tile("gbuf%d" % i) for i in range(4)]
        t_halo_c = Tile("halo_c")
        t_halo_f = [Tile("halo_f%d" % l) for l in range(2)]
        t_small = Tile("small")
        t_xbt = [Tile("xbt%d" % i) for i in range(4)]
        t_ps = [Tile("ps%d" % i) for i in range(8)]
        t_out = Tile("out")

        def x32(c, tt):
            return xT32[:, c, tt * TT:(tt + 1) * TT]

        def xb(c, tt):
            return xTb[:, c, tt * TT:(tt + 1) * TT]

        def ar(i):
            return arena[:, i, :]

        ps_state = {"free": list(range(8)), "i": 0}

        def ps_next():
            fl = ps_state["free"]
            b = fl[ps_state["i"] % len(fl)]
            ps_state["i"] += 1
            return b

        ft_state = {"i": 0}

        def ft_next():
            i = ft_state["i"] % NF
            ft_state["i"] += 1
            return i

        wplan = []

        def wview(W, c0, n):
            return W.rearrange("(k p) n -> p k n", p=P)[:, :, c0:c0 + n]

        def plan_kv(l):
            for i in range(4):
                wplan.append((wview(xa_wkv[l], i * 512, 512), 8, 512))

        def plan_layer(l, kv_after_mix=False):
            if l == 0:
                for i in range(3):
                    wplan.append((wview(ab_w_in, i * 512, 512), 8, 512))
                for i in range(2):
                    wplan.append((wview(ab_w_out, i * 512, 512), 8, 512))
                if kv_after_mix:
                    plan_kv(0)
                    plan_kv(1)
            else:
                for q in range(2):
                    for part in (1, 2, 0):
                        wplan.append((wview(c_w_in, part * 1024 + q * 512, 512), 8, 512))
                for i in range(2):
                    wplan.append((wview(c_w_out, i * 512, 512), 8, 512))
            for i in range(2):
                wplan.append((wview(xa_wq[l], i * 512, 512), 8, 512))
            for i in range(2):
                wplan.append((wview(xa_wo[l], i * 512, 512), 8, 512))
            for q in range(6):
                n = 512 if q < 5 else 256
                wplan.append((wview(ffn_w_up[l], DFF + q * 512, n), 8, n))
                wplan.append((wview(ffn_w_up[l], q * 512, n), 8, n))
            for m in range(8):
                wplan.append((wview(ffn_w_down[l], m * 128, 128), FC, 128))

        for st in range(n_st):
            if st % 2 == 0 and st > 0:
                plan_kv(0)
                plan_kv(1)
            plan_layer(0, st == 0)
            plan_layer(1)

        wstate = {"issued": 0, "next": 0, "done": 0}

        def w_pump():
            lim = min(len(wplan), wstate["done"] + NSLOT)
            while wstate["issued"] < lim:
                j = wstate["issued"]
                src, kc, n = wplan[j]
                s = j % NSLOT
                dst = wring[s][:, 0:kc * n].rearrange("p (k n) -> p k n", k=kc)
                S.op("pool", lambda e, dst=dst, src=src: e.dma_start(out=dst, in_=src),
                     writes=[t_ws[s]], dma=True)
                wstate["issued"] += 1

        def w_release(upto):
            if upto > wstate["done"]:
                wstate["done"] = upto
            w_pump()

        def w_next(kc, n, release_prior=True):
            i = wstate["next"]
            wstate["next"] += 1
            assert wplan[i][1] == kc and wplan[i][2] == n, (i, wplan[i][1:], kc, n)
            if release_prior:
                w_release(i)
            else:
                w_pump()
            assert wstate["issued"] > i, "too many live weight slots"
            s = i % NSLOT
            view = wring[s][:, 0:kc * n].rearrange("p (k n) -> p k n", k=kc)
            return view, t_ws[s]

        S.op("sp", lambda e: e.dma_start(out=pv[:], in_=pv_d), writes=[t_const], dma=True)
        S.op("sp", lambda e: e.dma_start(out=sgub[:], in_=sgub_d), writes=[t_const], dma=True)
        S.op("sp", lambda e: e.dma_start(out=bs32, in_=sgubias_d), writes=[t_ft[0]], dma=True)
        t_c2 = Tile("const2")
        S.op("pool", lambda e: e.dma_start(out=WmT[:], in_=sguwT_d), writes=[t_c2], dma=True)
        t_c3 = Tile("const3")
        S.op("pool", lambda e: e.dma_start(out=poolmats[:], in_=poolmats_d), writes=[t_c3], dma=True)
        t_c4 = Tile("const4")
        S.op("pool", lambda e: e.dma_start(out=poolw[:], in_=poolw_d), writes=[t_c4], dma=True)
        S.op("dve", lambda e: e.memset(WmT[64:128, :, 0:64], 0.0), writes=[t_c2])
        t_c5 = Tile("const5")
        S.op("dve", lambda e: e.memset(onesm[:], 1.0 / 1024.0), writes=[t_c5])
        S.op("dve", lambda e: e.memset(ones1[:], 1.0), writes=[t_c5])
        S.op("dve", lambda e: e.memset(neghalf[:], -0.5), writes=[t_c5])
        t_bs = Tile("bs")
        S.op("dve", lambda e: e.tensor_copy(bs_hi[:], bs32), reads=[t_ft[0]], writes=[t_bs])
        S.op("dve", lambda e: e.tensor_copy(bsh32, bs_hi[:]), reads=[t_bs], writes=[t_ft[1]])
        S.op("dve", lambda e: e.tensor_tensor(bsh32, bs32, bsh32, ALU.subtract),
             reads=[t_ft[0], t_ft[1]], writes=[t_ft[1]])
        S.op("dve", lambda e: e.tensor_copy(bs_lo[:], bsh32), reads=[t_ft[1]], writes=[t_bs])
        t_consts_all = [t_const, t_c2, t_c3, t_c4, t_c5, t_bs]

        def pvc(key, i=0, n=1):
            c = PV[key] + i
            return pv[:, c:c + n]

        from collections import deque
        deferred = deque()

        def pump(k=1):
            for _ in range(k):
                if not deferred:
                    return
                deferred.popleft()[1]()

        def pump_crit():
            while deferred and deferred[0][0]:
                deferred.popleft()[1]()

        def flush():
            while deferred:
                deferred.popleft()[1]()

        def mm_unit(wv, wt, coff, KC, in_ap, in_tile, tt):
            b = ps_next()
            for k in range(KC):
                S.op("pe", lambda e, b=b, k=k: e.matmul(
                    psum[b][:], wv[:, k, coff:coff + P], in_ap(k, tt),
                    start=(k == 0), stop=(k == KC - 1)),
                    reads=[wt, in_tile(k, tt)], writes=[t_ps[b]])
            return b

        def proj_ln(groups, fetch, KC, in_ap, in_tile, gkey, bkey, l, boundary=None):
            saved_free = ps_state["free"]
            nfl = len(saved_free)
            order = [saved_free[(ps_state["i"] + j) % nfl] for j in range(nfl)]
            assert nfl == 8
            stat = [order[4], order[6], order[5], order[7]]
            ps_state["free"] = order[0:4]
            ps_state["i"] = 0
            nseen = [0, 0]

            def stats(m, tt, r, first, last):
                S.op("pe", lambda e: e.matmul(
                    psum[stat[tt]][:], onesm[:], xb(m, tt), start=first, stop=last),
                    reads=[t_c5, t_xTb[m][tt]], writes=[t_ps[stat[tt]]])
                S.op("pe", lambda e: e.matmul(
                    psum[stat[2 + tt]][:], onesm[:], rsq_b[r][:], start=first, stop=last),
                    reads=[t_c5, t_rsq[r]], writes=[t_ps[stat[2 + tt]]])

            def finalize_items(tt, after_head=None, per_chunk=None, after_all=None):
                items = []

                def head():
                    f0 = ft_next()
                    S.op("act", lambda e: e.activation(ftmp[f0][:], psum[stat[tt]][:], AF.Square),
                         reads=[t_ps[stat[tt]]], writes=[t_ft[f0]])
                    S.op("dve", lambda e: e.scalar_tensor_tensor(
                        ftmp[f0][:], psum[stat[2 + tt]][:], EPS, ftmp[f0][:], ALU.add, ALU.subtract),
                        reads=[t_ps[stat[2 + tt]], t_ft[f0]], writes=[t_ft[f0]])
                    S.op("act", lambda e: e.activation(ftmp[f0][:], ftmp[f0][:], AF.Ln),
                         reads=[t_ft[f0]], writes=[t_ft[f0]])
                    S.op("act", lambda e: e.activation(rstd_b[tt][:], ftmp[f0][:], AF.Exp, scale=-0.5),
                         reads=[t_ft[f0]], writes=[t_rstd[tt]])
                    S.op("dve", lambda e: e.scalar_tensor_tensor(
                        nmr_b[tt][:], psum[stat[tt]][:], -1.0, rstd_b[tt][:], ALU.mult, ALU.mult),
                        reads=[t_ps[stat[tt]], t_rstd[tt]], writes=[t_nmr[tt]])
                items.append((True, head))
                if after_head is not None:
                    items.append((True, after_head))
                elif tt == 0:
                    def more_banks():
                        ps_state["free"] = list(ps_state["free"]) + [stat[0], stat[2]]
                    items.append((True, more_banks))
                late = []
                for m in range(DC):
                    def app(m=m):
                        eng = "dve"
                        S.op(eng, lambda e: e.tensor_tensor(
                            x32(m, tt), x32(m, tt), rstd_b[tt][:], ALU.mult),
                            reads=[t_xT32[m][tt], t_rstd[tt]], writes=[t_xT32[m][tt]])
                        S.op(eng, lambda e: e.tensor_tensor(
                            x32(m, tt), x32(m, tt), nmr_b[tt][:], ALU.add),
                            reads=[t_xT32[m][tt], t_nmr[tt]], writes=[t_xT32[m][tt]])
                        if per_chunk is None:
                            S.op("act", lambda e: e.activation(
                                xb(m, tt), x32(m, tt), AF.Identity,
                                bias=pvc((bkey, l), m), scale=pvc((gkey, l), m)),
                                reads=[t_xT32[m][tt], t_const], writes=[t_xTb[m][tt]])
                    items.append((True, app))

                    def aff(m=m):
                        if m in (3, 6, 7):
                            S.op("act", lambda e: e.activation(
                                x32(m, tt), x32(m, tt), AF.Identity,
                                bias=pvc((bkey, l), m), scale=pvc((gkey, l), m)),
                                reads=[t_xT32[m][tt], t_const], writes=[t_xT32[m][tt]])
                        else:
                            S.op("dve", lambda e: e.tensor_scalar(
                                x32(m, tt), x32(m, tt), pvc((gkey, l), m), pvc((bkey, l), m),
                                ALU.mult, ALU.add),
                                reads=[t_xT32[m][tt], t_const], writes=[t_xT32[m][tt]])
                    if per_chunk is not None:
                        items.append((True, aff))
                        items.append((True, per_chunk[m]))
                    else:
                        late.append((False, aff))
                items.extend(late)
                if after_all is not None:
                    items.extend((True, f) for f in after_all)
                return items

            STAT_LAG = 2
            pendq = deque()

            def emit_stats(p):
                stats(*p)
                if p[4] and p[1] == 0:
                    if boundary is not None:
                        deferred.extend(finalize_items(0, None, boundary[0], boundary[1]))
                    else:
                        deferred.extend(finalize_items(0))

            ng = len(groups)
            for gi, group in enumerate(groups):
                ws = fetch(group)
                for tt in range(NTT):
                    for m in group:
                        wv, wt, coff = ws[m]
                        b = mm_unit(wv, wt, coff, KC, in_ap, in_tile, tt)
                        S.op("dve", lambda e, m=m, tt=tt, b=b: e.scalar_tensor_tensor(
                            x32(m, tt), x32(m, tt), ALPHA, psum[b][:], ALU.mult, ALU.add),
                            reads=[t_xT32[m][tt], t_ps[b]], writes=[t_xT32[m][tt]])
                        S.op("act", lambda e, m=m, tt=tt: e.activation(xb(m, tt), x32(m, tt), AF.Copy),
                             reads=[t_xT32[m][tt]], writes=[t_xTb[m][tt]])
                        r = rsq_state["i"] % 3
                        rsq_state["i"] += 1
                        S.op("act", lambda e, m=m, tt=tt, r=r: e.activation(rsq_b[r][:], x32(m, tt), AF.Square),
                             reads=[t_xT32[m][tt]], writes=[t_rsq[r]])
                        first = nseen[tt] == 0
                        nseen[tt] += 1
                        last = nseen[tt] == DC
                        pendq.append((m, tt, r, first, last))
                        while len(pendq) > STAT_LAG:
                            emit_stats(pendq.popleft())
                        if boundary is not None:
                            pump_crit()
                        if len(group) >= 8:
                            pump(2 if m == group[0] else 1)
                        else:
                            pump(3)
                    if gi == ng - 1 and tt == 0:
                        flush()
            flush()
            while pendq:
                deferred.append((True, lambda p=pendq.popleft(): stats(*p)))

            def restore():
                ps_state["free"] = saved_free
            if boundary is not None:
                deferred.extend(finalize_items(1, restore, boundary[2], boundary[3]))
            else:
                deferred.extend(finalize_items(1, restore))

        def fetch_2x512(group_sizes=(8,)):
            def fetch(group):
                w0 = w_next(8, 512)
                w1 = w_next(8, 512, False)
                out = {}
                for m in group:
                    wv, wt = (w0, w1)[m // 4]
                    out[m] = (wv, wt, (m % 4) * P)
                return out
            return fetch

        def x_in_ap(k, tt):
            return xb(k, tt)

        def x_in_tile(k, tt):
            return t_xTb[k][tt]

        ALLM = [list(range(DC))]

        def mixer_ab(st):
            WA, tA = w_next(8, 512)
            iWA = wstate["next"] - 1
            WB, tB = w_next(8, 512, False)
            WC, tC = w_next(8, 512, False)
            seq_first_st = (st % 2 == 0)

            def uT(g, tt):
                return g * 2 + tt

            def yT(c, tt):
                return 8 + c * 2 + tt

            def pooledT(g, tt):
                return 24 + g * 2 + tt

            def u_units(tt):
                for m in range(4):
                    b = mm_unit(WA, tA, m * P, 8, x_in_ap, x_in_tile, tt)
                    i = uT(m, tt)
                    S.op("act", lambda e, b=b, i=i: e.activation(ar(i), psum[b][:], AF.Gelu_apprx_tanh),
                         reads=[t_ps[b]], writes=[t_ar[i]])
                    pump(2)

            binfo = {}

            def vxb(j):
                tt = j // 4
                gj = st * NBLK + j
                bv = ps_next()
                bx = ps_next()
                for k in range(8):
                    S.op("pe", lambda e, k=k: e.matmul(
                        psum[bv][:], xTb[:, k, j * P:(j + 1) * P], WB[:, k, :], start=(k == 0), stop=(k == 7)),
                        reads=[t_xTb[k][tt], tB], writes=[t_ps[bv]])
                for k in range(8):
                    S.op("pe", lambda e, k=k: e.matmul(
                        psum[bx][:], xTb[:, k, j * P:(j + 1) * P], WC[:, k, :], start=(k == 0), stop=(k == 7)),
                        reads=[t_xTb[k][tt], tC], writes=[t_ps[bx]])
                pump_crit()
                fv = ft_next()
                S.op("act", lambda e: e.activation(ftmp[fv][:], psum[bv][:], AF.Gelu_apprx_tanh),
                     reads=[t_ps[bv]], writes=[t_ft[fv]])
                ixb = gj % 4
                S.op("act", lambda e: e.activation(xbt[:, ixb, :], psum[bx][:], AF.Copy),
                     reads=[t_ps[bx]], writes=[t_xbt[ixb]])
                so = (gj % 4) * 16
                S.op("dve", lambda e: e.bn_stats(small[:, so:so + 6], ftmp[fv][:]),
                     reads=[t_ft[fv]], writes=[t_small])
                S.op("dve", lambda e: e.bn_aggr(small[:, so + 6:so + 8], small[:, so:so + 6]),
                     reads=[t_small], writes=[t_small])
                S.op("dve", lambda e: e.tensor_scalar(
                    small[:, so + 8:so + 9], small[:, so + 7:so + 8], EPS, None, ALU.add),
                    reads=[t_small], writes=[t_small])
                S.op("pool", lambda e: e.tensor_tensor(
                    small[:, so + 9:so + 10], small[:, so + 8:so + 9], neghalf[:, 0:1], ALU.pow),
                    reads=[t_small, t_c5], writes=[t_small])
                S.op("dve", lambda e: e.tensor_scalar(
                    ftmp[fv][:], ftmp[fv][:], small[:, so + 6:so + 7], small[:, so + 9:so + 10],
                    ALU.subtract, ALU.mult),
                    reads=[t_ft[fv], t_small], writes=[t_ft[fv]])
                S.op("dve", lambda e: e.tensor_tensor(ftmp[fv][:], ftmp[fv][:], sgub[:, 0:512], ALU.mult),
                     reads=[t_ft[fv], t_const], writes=[t_ft[fv]])
                ivl = 32 + gj % 4
                S.op("dve", lambda e: e.tensor_tensor(
                    ar(ivl), ftmp[fv][:], sgub[:, 512:1024], ALU.add),
                    reads=[t_ft[fv], t_const], writes=[t_ar[ivl]])
                binfo[j] = (ivl, ixb, (gj - 1) % 4)
                pump(2)

            def sgu_pool(j):
                tt = j // 4
                col = (j % 4) * P
                ivl, ixb, ixp = binfo[j]
                bs_ = ps_next()
                for g in range(4):
                    S.op("pe", lambda e, g=g: e.matmul(
                        psum[bs_][:, g * P:(g + 1) * P], arena[:, ivl, g * P:(g + 1) * P], WmT[:, g, :],
                        start=True, stop=False),
                        reads=[t_ar[ivl], t_c2], writes=[t_ps[bs_]])
                    S.op("pe", lambda e, g=g: e.matmul(
                        psum[bs_][:, g * P:(g + 1) * P], ones1[0:1, :], bs_hi[0:1, g * P:(g + 1) * P],
                        start=False, stop=False),
                        reads=[t_c5, t_bs], writes=[t_ps[bs_]])
                    S.op("pe", lambda e, g=g: e.matmul(
                        psum[bs_][:, g * P:(g + 1) * P], ones1[0:1, :], bs_lo[0:1, g * P:(g + 1) * P],
                        start=False, stop=True),
                        reads=[t_c5, t_bs], writes=[t_ps[bs_]])
                bp = ps_next()
                first = seq_first_st and j == 0
                for g in range(4):
                    if first:
                        S.op("pe", lambda e, g=g: e.matmul(
                            psum[bp][:, g * P:(g + 1) * P], xbt[:, ixb, g * P:(g + 1) * P], poolmats[:, 8 + g, :],
                            start=True, stop=False),
                            reads=[t_xbt[ixb], t_c3], writes=[t_ps[bp]])
                        S.op("pe", lambda e, g=g: e.matmul(
                            psum[bp][:, g * P:(g + 1) * P], xbt[:, ixb, g * P:(g + 1) * P], poolmats[:, 12 + g, :],
                            start=False, stop=False),
                            reads=[t_xbt[ixb], t_c3], writes=[t_ps[bp]])
                        S.op("pe", lambda e, g=g: e.matmul(
                            psum[bp][:, g * P:(g + 1) * P], xbt[:, ixb, g * P:(g + 1) * P], poolmats[:, 16 + g, :],
                            start=False, stop=True),
                            reads=[t_xbt[ixb], t_c3], writes=[t_ps[bp]])
                    else:
                        S.op("pe", lambda e, g=g: e.matmul(
                            psum[bp][:, g * P:(g + 1) * P], xbt[:, ixb, g * P:(g + 1) * P], poolmats[:, g, :],
                            start=True, stop=False),
                            reads=[t_xbt[ixb], t_c3], writes=[t_ps[bp]])
                        S.op("pe", lambda e, g=g: e.matmul(
                            psum[bp][:, g * P:(g + 1) * P], xbt[:, ixp, g * P:(g + 1) * P], poolmats[:, 4 + g, :],
                            start=False, stop=True),
                            reads=[t_xbt[ixp], t_c3], writes=[t_ps[bp]])
                for g in range(4):
                    iy = yT(g, tt)
                    iu = uT(g, tt)
                    S.op("dve", lambda e, g=g, iy=iy, iu=iu: e.tensor_tensor(
                        arena[:, iy, col:col + P], psum[bs_][:, g * P:(g + 1) * P], arena[:, iu, col:col + P],
                        ALU.mult),
                        reads=[t_ps[bs_], t_ar[iu]], writes=[t_ar[iy]])
                for g in range(4):
                    ip = pooledT(g, tt)
                    S.op("act", lambda e, g=g, ip=ip: e.activation(
                        arena[:, ip, col:col + P], psum[bp][:, g * P:(g + 1) * P], AF.Copy),
                        reads=[t_ps[bp]], writes=[t_ar[ip]])
                pump(1)

            def yb(tt):
                for g in range(4):
                    b = ps_next()
                    ip = pooledT(g, tt)
                    S.op("pe", lambda e, g=g, b=b, ip=ip: e.matmul(
                        psum[b][:], poolw[:, g, :], ar(ip), start=True, stop=True),
                        reads=[t_c4, t_ar[ip]], writes=[t_ps[b]])
                    iy = yT(4 + g, tt)
                    S.op("act", lambda e, g=g, b=b, iy=iy: e.activation(
                        ar(iy), psum[b][:], AF.Identity, scale=pvc("pool_scale", g)),
                        reads=[t_ps[b], t_const], writes=[t_ar[iy]])

            vxb(0)
            vxb(1)
            u_units(0)
            vxb(2)
            sgu_pool(0)
            vxb(3)
            sgu_pool(1)
            flush()
            vxb(4)
            sgu_pool(2)
            vxb(5)
            sgu_pool(3)
            u_units(1)
            yb(0)
            vxb(6)
            sgu_pool(4)
            vxb(7)
            sgu_pool(5)
            sgu_pool(6)
            sgu_pool(7)
            yb(1)
            w_release(iWA + 3)
            flush()
            proj_ln(ALLM, fetch_2x512(), 8, lambda k, tt: ar(yT(k, tt)), lambda k, tt: t_ar[yT(k, tt)],
                    "mix_g", "mix_b", 0)

        def conv3(gb_i, out_f, wcol):
            S.op("dve", lambda e: e.tensor_scalar(
                ftmp[out_f][:], gbuf[gb_i][:, 0:TT], pv[:, wcol:wcol + 1], None, ALU.mult),
                reads=[t_gbuf[gb_i], t_const], writes=[t_ft[out_f]])
            S.op("dve", lambda e: e.scalar_tensor_tensor(
                ftmp[out_f][:], gbuf[gb_i][:, 1:TT + 1], pv[:, wcol + 1:wcol + 2], ftmp[out_f][:],
                ALU.mult, ALU.add),
                reads=[t_gbuf[gb_i], t_const, t_ft[out_f]], writes=[t_ft[out_f]])
            S.op("dve", lambda e: e.scalar_tensor_tensor(
                ftmp[out_f][:], gbuf[gb_i][:, 2:TT + 2], pv[:, wcol + 2:wcol + 3], ftmp[out_f][:],
                ALU.mult, ALU.add),
                reads=[t_gbuf[gb_i], t_const, t_ft[out_f]], writes=[t_ft[out_f]])

        def halo_io(gi, halo_ap, halo_tile):
            S.op("pool", lambda e: e.tensor_copy(gbuf[gi][:, 0:2], halo_ap),
                 reads=[halo_tile], writes=[t_gbuf[gi]])
            S.op("pool", lambda e: e.tensor_copy(halo_ap, gbuf[gi][:, TT:TT + 2]),
                 reads=[t_gbuf[gi]], writes=[halo_tile])

        gb_state = {"i": 0}
        rsq_state = {"i": 0}

        def gb_next():
            i = gb_state["i"] % 4
            gb_state["i"] += 1
            return i

        def mixer_c(st):
            def yT(c, tt):
                return c * 2 + tt
            if st % 2 == 0:
                S.op("pool", lambda e: e.memset(halo_c[:], 0.0), writes=[t_halo_c])
            for q in range(2):
                wc_ = w_next(8, 512)
                wh_ = w_next(8, 512, False)
                wb_ = w_next(8, 512, False)
                for tt in range(NTT):
                    if q == 0 and tt == 1:
                        flush()
                    for m in range(q * 4, q * 4 + 4):
                        coff = (m % 4) * P
                        bc = mm_unit(wc_[0], wc_[1], coff, 8, x_in_ap, x_in_tile, tt)
                        bh = mm_unit(wh_[0], wh_[1], coff, 8, x_in_ap, x_in_tile, tt)
                        pump_crit()
                        fc_ = ft_next()
                        S.op("act", lambda e, fc_=fc_, bc=bc: e.activation(ftmp[fc_][:], psum[bc][:], AF.Copy),
                             reads=[t_ps[bc]], writes=[t_ft[fc_]])
                        gi = gb_next()
                        S.op("dve", lambda e, fc_=fc_, bh=bh, gi=gi: e.tensor_tensor(
                            gbuf[gi][:, 2:TT + 2], ftmp[fc_][:], psum[bh][:], ALU.mult),
                            reads=[t_ft[fc_], t_ps[bh]], writes=[t_gbuf[gi]])
                        halo_io(gi, halo_c[:, m, :], t_halo_c)
                        fo = ft_next()
                        conv3(gi, fo, PV["c_conv_w"] + m * 3)
                        bb = mm_unit(wb_[0], wb_[1], coff, 8, x_in_ap, x_in_tile, tt)
                        iy = yT(m, tt)
                        S.op("dve", lambda e, bb=bb, fo=fo, iy=iy: e.tensor_tensor(
                            ar(iy), ftmp[fo][:], psum[bb][:], ALU.mult),
                            reads=[t_ft[fo], t_ps[bb]], writes=[t_ar[iy]])
                        pump(3)
            flush()
            proj_ln(ALLM, fetch_2x512(), 8, lambda k, tt: ar(yT(k, tt)), lambda k, tt: t_ar[yT(k, tt)],
                    "mix_g", "mix_b", 1)

        def kv_phase(b, l):
            mT = arena[:, 40:44, :].rearrange("p a (c m) -> p (a c) m", m=MEM)
            t_m = t_ar[40:44]
            if l == 0:
                S.op("pool", lambda e: e.dma_start(out=mT, in_=memT_d[b].rearrange("c p m -> p c m")),
                     writes=list(t_m), dma=True)
            ws = [w_next(8, 512, i == 0) for i in range(4)]
            for c in range(DC):
                wv, wt = ws[c // 4]
                bk = ps_next()
                for k in range(8):
                    S.op("pe", lambda e, k=k, bk=bk, wv=wv, c=c: e.matmul(
                        psum[bk][:, 0:MEM], wv[:, k, (c % 4) * P:(c % 4 + 1) * P], mT[:, k, :],
                        start=(k == 0), stop=(k == 7)),
                        reads=[wt] + list(t_m), writes=[t_ps[bk]])
                S.op("act", lambda e, bk=bk, c=c: e.activation(kT[l][:, c, :], psum[bk][:, 0:MEM], AF.Copy),
                     reads=[t_ps[bk]], writes=[t_kT[l]])
            for mc in range(2):
                for n in range(2):
                    wv, wt = ws[2 + n]
                    bk = ps_next()
                    for k in range(8):
                        S.op("pe", lambda e, k=k, bk=bk, wv=wv, mc=mc: e.matmul(
                            psum[bk][:], mT[:, k, mc * P:(mc + 1) * P], wv[:, k, :],
                            start=(k == 0), stop=(k == 7)),
                            reads=[wt] + list(t_m), writes=[t_ps[bk]])
                    S.op("dve", lambda e, bk=bk, mc=mc, n=n: e.tensor_copy(
                        vtok[l][:, mc, n * TT:(n + 1) * TT], psum[bk][:]),
                        reads=[t_ps[bk]], writes=[t_vt[l]])

        def xattn(st, l):
            def qT(c, tt):
                return c * 2 + tt

            def pT(h, mc):
                return 16 + h * 2 + mc

            def aoT(c, tt):
                return 24 + c * 2 + tt
            wq0 = w_next(8, 512)
            wq1 = w_next(8, 512, False)

            def q_unit(m, tt):
                wv, wt = (wq0, wq1)[m // 4]
                b = mm_unit(wv, wt, (m % 4) * P, 8, x_in_ap, x_in_tile, tt)
                pump_crit()
                i = qT(m, tt)
                if m % 2 == 0:
                    S.op("act", lambda e: e.activation(ar(i), psum[b][:], AF.Identity, scale=1.0 / 16.0),
                         reads=[t_ps[b]], writes=[t_ar[i]])
                else:
                    S.op("dve", lambda e: e.tensor_scalar(
                        ar(i), psum[b][:], 1.0 / 16.0, None, ALU.mult),
                        reads=[t_ps[b]], writes=[t_ar[i]])

            def scores(h, tt):
                for mc in range(2):
                    b = ps_next()
                    for kc in range(2):
                        iq = qT(2 * h + kc, tt)
                        S.op("pe", lambda e, b=b, kc=kc, mc=mc, iq=iq: e.matmul(
                            psum[b][:], kT[l][:, 2 * h + kc, mc * P:(mc + 1) * P], ar(iq),
                            start=(kc == 0), stop=(kc == 1)),
                            reads=[t_kT[l], t_ar[iq]], writes=[t_ps[b]])
                    ip = pT(h, mc)
                    S.op("act", lambda e, b=b, ip=ip: e.activation(ar(ip), psum[b][:], AF.Exp),
                         reads=[t_ps[b]], writes=[t_ar[ip]])

            def den_pv(h, tt):
                bd = ps_next()
                for mc in range(2):
                    ip = pT(h, mc)
                    S.op("pe", lambda e, mc=mc, ip=ip: e.matmul(
                        psum[bd][:], ones1[:], ar(ip), start=(mc == 0), stop=(mc == 1)),
                        reads=[t_c5, t_ar[ip]], writes=[t_ps[bd]])
                fr = ft_next()
                S.op("act", lambda e: e.activation(ftmp[fr][:], psum[bd][:], AF.Ln),
                     reads=[t_ps[bd]], writes=[t_ft[fr]])
                S.op("act", lambda e: e.activation(ftmp[fr][:], ftmp[fr][:], AF.Exp, scale=-1.0),
                     reads=[t_ft[fr]], writes=[t_ft[fr]])
                for dc in range(2):
                    bo = ps_next()
                    for mc in range(2):
                        ip = pT(h, mc)
                        S.op("pe", lambda e, bo=bo, mc=mc, ip=ip, dc=dc: e.matmul(
                            psum[bo][:], vtok[l][:, mc, (2 * h + dc) * P:(2 * h + dc + 1) * P], ar(ip),
                            start=(mc == 0), stop=(mc == 1)),
                            reads=[t_vt[l], t_ar[ip]], writes=[t_ps[bo]])
                    io = aoT(2 * h + dc, tt)
                    S.op("dve", lambda e, bo=bo, io=io: e.tensor_tensor(
                        ar(io), psum[bo][:], ftmp[fr][:], ALU.mult),
                        reads=[t_ps[bo], t_ft[fr]], writes=[t_ar[io]])

            for m in range(DC):
                q_unit(m, 0)
                pump(2)
            scores(0, 0)
            scores(1, 0)
            flush()
            q_unit(0, 1)
            q_unit(1, 1)
            for h in range(4):
                if h + 2 < 4:
                    scores(h + 2, 0)
                q_unit(2 + h, 1)
                den_pv(h, 0)
            q_unit(6, 1)
            q_unit(7, 1)
            scores(0, 1)
            scores(1, 1)
            for h in range(4):
                if h + 2 < 4:
                    scores(h + 2, 1)
                den_pv(h, 1)
            flush()
            proj_ln(ALLM, fetch_2x512(), 8, lambda k, tt: ar(aoT(k, tt)), lambda k, tt: t_ar[aoT(k, tt)],
                    "xa_g", "xa_b", l)

        def ffn(st, l, boundary=None):
            def hT(f, tt):
                return f * 2 + tt
            if st % 2 == 0:
                S.op("pool", lambda e: e.memset(halo_f[l][:], 0.0), writes=[t_halo_f[l]])
            pend = [None]

            def gelu_of(p):
                fo, f, ba, ih = p
                S.op("act", lambda e: e.activation(
                    ftmp[fo][:], ftmp[fo][:], AF.Gelu_apprx_tanh, bias=pvc(("ffn_conv_b", l), f)),
                    reads=[t_ft[fo], t_const], writes=[t_ft[fo]])

            def mult_of(p):
                fo, f, ba, ih = p
                S.op("dve", lambda e: e.tensor_tensor(
                    ar(ih), ftmp[fo][:], psum[ba][:], ALU.mult),
                    reads=[t_ft[fo], t_ps[ba]], writes=[t_ar[ih]])

            for q in range(6):
                n = 512 if q < 5 else 256
                wg, tg = w_next(8, n)
                wa, ta = w_next(8, n, False)
                fs = list(range(q * 4, min(FC, q * 4 + 4)))
                for tt in range(NTT):
                    if q == 0 and tt == 1:
                        flush()
                    for f in fs:
                        coff = (f % 4) * P
                        bg = mm_unit(wg, tg, coff, 8, x_in_ap, x_in_tile, tt)
                        ba = mm_unit(wa, ta, coff, 8, x_in_ap, x_in_tile, tt)
                        pump_crit()
                        gi = gb_next()
                        S.op("act", lambda e, gi=gi, bg=bg: e.activation(gbuf[gi][:, 2:TT + 2], psum[bg][:], AF.Copy),
                             reads=[t_ps[bg]], writes=[t_gbuf[gi]])
                        halo_io(gi, halo_f[l][:, f, :], t_halo_f[l])
                        if pend[0] is not None:
                            gelu_of(pend[0])
                        fo = ft_next()
                        conv3(gi, fo, PV[("ffn_conv_w", l)] + f * 3)
                        if pend[0] is not None:
                            mult_of(pend[0])
                        pend[0] = (fo, f, ba, hT(f, tt))
                        pump(3)
            gelu_of(pend[0])
            mult_of(pend[0])
            flush()

            def fetch_down(group):
                out = {}
                for i, m in enumerate(group):
                    wv, wt = w_next(FC, 128, i == 0)
                    out[m] = (wv, wt, 0)
                return out
            proj_ln([[0, 1, 2, 3], [4, 5, 6, 7]], fetch_down, FC,
                    lambda k, tt: ar(hT(k, tt)), lambda k, tt: t_ar[hT(k, tt)], "ffn_g", "ffn_b", l,
                    boundary=boundary)

        def load_tile(st, c, tt):
            b = st // 2
            s0 = (st % 2) * STW
            S.op("sp", lambda e: e.dma_start(
                out=x32(c, tt), in_=xT_d[b, c, :, s0 + tt * TT:s0 + (tt + 1) * TT]),
                writes=[t_xT32[c][tt]], dma=True)
            S.op("pool", lambda e: e.dma_start(
                out=xb(c, tt), in_=xT_d[b, c, :, s0 + tt * TT:s0 + (tt + 1) * TT]),
                writes=[t_xTb[c][tt]], dma=True)

        def store_tile(st, c, tt):
            b = st // 2
            s0 = (st % 2) * STW
            S.op("sp", lambda e: e.dma_start(
                out=outT_d[b, c, :, s0 + tt * TT:s0 + (tt + 1) * TT], in_=x32(c, tt)),
                reads=[t_xT32[c][tt]], writes=[t_out], dma=True)

        done = False
        for st in range(n_st):
            b = st // 2
            if st == 0:
                for tt in range(NTT):
                    for c in range(DC):
                        load_tile(0, c, tt)
            elif st % 2 == 0:
                kv_phase(b, 0)
                kv_phase(b, 1)
            for l in range(2):
                if l == 0:
                    mixer_ab(st)
                else:
                    mixer_c(st)
                if stop_after == (l, "mix"):
                    done = True
                    break
                if st == 0 and l == 0:
                    kv_phase(b, 0)
                    kv_phase(b, 1)
                xattn(st, l)
                if stop_after == (l, "xa"):
                    done = True
                    break
                bnd = None
                if l == 1 and stop_after is None:
                    nxt = st < n_st - 1
                    bnd = ([(lambda st=st, c=c: store_tile(st, c, 0)) for c in range(DC)],
                           [(lambda st=st, c=c: load_tile(st + 1, c, 0)) for c in range(DC)] if nxt else [],
                           [(lambda st=st, c=c: store_tile(st, c, 1)) for c in range(DC)],
                           [(lambda st=st, c=c: load_tile(st + 1, c, 1)) for c in range(DC)] if nxt else [])
                ffn(st, l, bnd)
                if stop_after == (l, "ffn"):
                    done = True
                    break
            last = done or st == n_st - 1
            if last:
                flush()
                if stop_after is not None:
                    for tt in range(NTT):
                        for c in range(DC):
                            store_tile(st, c, tt)
                break
        S.emit(nc)
    return nc, len(S.recs)


def _bf16_round(a):
    u = np.ascontiguousarray(a, dtype=np.float32).view(np.uint32).astype(np.uint64)
    r = ((u + 0x7FFF + ((u >> 16) & 1)) >> 16) << 16
    return r.astype(np.uint32).view(np.float32)


def _pool_mats():
    out = np.zeros((P, 20, P), np.float64)
    t = np.arange(P)
    for g, w in enumerate(POOL_WINDOWS):
        cur = np.zeros((P, P))
        prev = np.zeros((P, P))
        first = np.zeros((P, P))
        for tc in range(P):
            for tp in range(max(0, tc - w + 1), tc + 1):
                cur[tp, tc] += 1.0 / w
            for d in range(tc - w + 1, 0):
                prev[P + d, tc] += 1.0 / w
            cnt = min(tc + 1, w)
            for tp in range(max(0, tc - w + 1), tc + 1):
                first[tp, tc] += 1.0 / cnt
        cur -= np.eye(P)
        first -= np.eye(P)
        out[:, g, :] = cur
        out[:, 4 + g, :] = prev
        hi = _bf16_round(first.astype(np.float32)).astype(np.float64)
        lo = _bf16_round((first - hi).astype(np.float32)).astype(np.float64)
        lo2 = _bf16_round((first - hi - lo).astype(np.float32)).astype(np.float64)
        out[:, 8 + g, :] = hi
        out[:, 12 + g, :] = lo
        out[:, 16 + g, :] = lo2
    return out.astype(np.float32)


def _cols(v, nchunk):
    return np.ascontiguousarray(np.asarray(v, np.float32).reshape(nchunk, P).T)


def _prep_shared(inp):
    pvh = np.zeros((P, NPV), np.float32)
    for l in range(2):
        for n, key in (("mix_g", "ln_mix_g"), ("mix_b", "ln_mix_b"), ("xa_g", "ln_xa_g"),
                       ("xa_b", "ln_xa_b"), ("ffn_g", "ln_ffn_g"), ("ffn_b", "ln_ffn_b")):
            c = PV[(n, l)]
            pvh[:, c:c + 8] = _cols(inp[key][l], 8)
        c = PV[("ffn_conv_w", l)]
        w = np.asarray(inp["ffn_conv_w"][l], np.float32)
        pvh[:, c:c + 66] = w.reshape(3, FC, P).transpose(2, 1, 0).reshape(P, 66)
        c = PV[("ffn_conv_b", l)]
        pvh[:, c:c + 22] = _cols(inp["ffn_conv_b"][l], FC)
    c = PV["pool_scale"]
    pvh[:, c:c + 4] = _cols(inp["pool_scale"][0], 4)
    c = PV["c_conv_w"]
    w = np.asarray(inp["c_conv_w"][0], np.float32)
    pvh[:, c:c + 24] = w.reshape(3, DC, P).transpose(2, 1, 0).reshape(P, 24)
    sgub = np.empty((P, 1024), np.float32)
    sgub[:, 0:512] = np.asarray(inp["sgu_ln_g"][0], np.float32)[None, :]
    sgub[:, 512:1024] = np.asarray(inp["sgu_ln_b"][0], np.float32)[None, :]
    shared = {
        "ab_w_in": np.ascontiguousarray(inp["ab_w_in"][0], dtype=np.float32),
        "ab_w_out": np.ascontiguousarray(inp["ab_w_out"][0], dtype=np.float32),
        "c_w_in": np.ascontiguousarray(inp["c_w_in"][0], dtype=np.float32),
        "c_w_out": np.ascontiguousarray(inp["c_w_out"][0], dtype=np.float32),
        "xa_wq": np.ascontiguousarray(inp["xa_wq"], dtype=np.float32),
        "xa_wkv": np.ascontiguousarray(inp["xa_wkv"], dtype=np.float32),
        "xa_wo": np.ascontiguousarray(inp["xa_wo"], dtype=np.float32),
        "ffn_w_up": np.ascontiguousarray(inp["ffn_w_up"], dtype=np.float32),
        "ffn_w_down": np.ascontiguousarray(inp["ffn_w_down"], dtype=np.float32),
        "pv": pvh,
        "sgub": sgub,
        "sguwT": np.ascontiguousarray(np.asarray(inp["sgu_w"][0], np.float32).transpose(2, 0, 1)),
        "sgubias": np.ascontiguousarray(np.asarray(inp["sgu_b"][0], np.float32).reshape(1, 512)),
        "poolmats": _pool_mats(),
        "poolw": np.ascontiguousarray(np.asarray(inp["pool_w"][0], np.float32).transpose(1, 0, 2)),
    }
    return shared


_CACHE = {}


def _get_program(n_st=4, stop_after=None):
    key = (n_st, stop_after)
    if key not in _CACHE:
        _CACHE[key] = build_program(n_st, stop_after)[0]
    return _CACHE[key]


def kernel(**inputs):
    inp = {k: np.asarray(v) for k, v in inputs.items()}
    n = 8
    shared = _prep_shared(inp)
    x = np.asarray(inp["x"], np.float32)
    mem = np.asarray(inp["mem"], np.float32)
    in_maps = []
    for i in range(n):
        xs = x[2 * i:2 * i + 2]
        xT = np.ascontiguousarray(xs.transpose(0, 2, 1)).reshape(NB_LOCAL, DC, P, SEQ)
        ms = mem[2 * i:2 * i + 2]
        mT = np.ascontiguousarray(ms.transpose(0, 2, 1)).reshape(NB_LOCAL, DC, P, MEM)
        d = dict(shared)
        d["xT"] = xT
        d["memT"] = mT
        in_maps.append(d)
    nc = _get_program()
    res = run_bass_kernel_spmd(nc, in_maps, core_ids=list(range(n)))
    outs = []
    for i in range(n):
        oT = np.asarray(res.results[i]["outT"]).reshape(NB_LOCAL, D, SEQ)
        outs.append(oT.transpose(0, 2, 1))
    return np.ascontiguousarray(np.concatenate(outs, axis=0), dtype=np.float32)
```

```python
import numpy as np
from contextlib import ExitStack
import concourse.bass as bass
import concourse.mybir as mybir
from concourse.bass_utils import run_bass_kernel_spmd

F32 = mybir.dt.float32
BF16 = mybir.dt.bfloat16
AF = mybir.ActivationFunctionType
ALU = mybir.AluOpType

ENGS = ("pe", "act", "dve", "pool", "sp")
NDMASEM = 16


class Tile:
    __slots__ = ("name", "lastw", "readers", "dreaders")

    def __init__(self, name=""):
        self.name = name
        self.lastw = None
        self.readers = {}
        self.dreaders = []


class Rec:
    __slots__ = ("eng", "fn", "deps", "dma", "signal", "sigval", "dmaidx")

    def __init__(self, eng, fn, deps, dma):
        self.eng = eng
        self.fn = fn
        self.deps = deps
        self.dma = dma
        self.signal = False
        self.sigval = 0
        self.dmaidx = -1


class Sched:
    def __init__(self):
        self.recs = []
        self.ndma = {e: 0 for e in ENGS}

    def op(self, eng, fn, reads=(), writes=(), dma=False):
        idx = len(self.recs)
        recs = self.recs
        deps = set()
        rawset = set()
        for t in reads:
            if t.lastw is not None:
                deps.add(t.lastw)
                rawset.add(t.lastw)
        for t in writes:
            if t.lastw is not None:
                deps.add(t.lastw)
            deps.update(t.readers.values())
            deps.update(t.dreaders)
        real = []
        for d in deps:
            r = recs[d]
            if r.eng == eng and not r.dma and not dma:
                if eng == "pe":
                    continue
                if d not in rawset:
                    continue
            real.append(d)
        rec = Rec(eng, fn, real, dma)
        if dma:
            rec.dmaidx = self.ndma[eng]
            self.ndma[eng] += 1
        recs.append(rec)
        for d in real:
            recs[d].signal = True
        for t in reads:
            if dma:
                t.dreaders.append(idx)
            else:
                t.readers[eng] = idx
        for t in writes:
            t.lastw = idx
            t.readers = {}
            t.dreaders = []
        return idx

    def emit(self, nc, final_dma_wait_eng="sp"):
        recs = self.recs
        cnt = {e: 0 for e in ENGS}
        for r in recs:
            if r.dma:
                r.signal = True
                continue
            if r.signal:
                cnt[r.eng] += 1
                r.sigval = cnt[r.eng]
        with ExitStack() as es:
            prog = {e: es.enter_context(nc.semaphore("prog_" + e)) for e in ENGS}
            dsem = {}
            for e in ENGS:
                if self.ndma[e] > 0:
                    dsem[e] = [es.enter_context(nc.semaphore("dma_%s_%d" % (e, i)))
                               for i in range(min(NDMASEM, self.ndma[e]))]
            block = es.enter_context(nc.Block())

            nsem = {e: len(dsem[e]) for e in dsem}
            know = {e: {x: 0 for x in ENGS} for e in ENGS}
            dwaited = {e: {} for e in ENGS}
            sigcount = {e: 0 for e in ENGS}
            vc = [None] * len(recs)
            plan = [None] * len(recs)
            nw_before = 0
            nw_after = 0
            for i, r in enumerate(recs):
                E = r.eng
                K = know[E]
                waits = {}
                merged = []
                seen_old = set()
                for d in r.deps:
                    rd = recs[d]
                    if rd.dma:
                        k = nsem[rd.eng]
                        key = ("d", rd.eng, rd.dmaidx % k)
                        val = 16 * (rd.dmaidx // k + 1)
                        if dwaited[E].get(key, 0) < val:
                            waits[key] = max(waits.get(key, 0), val)
                            merged.append(d)
                    else:
                        X = rd.eng
                        seen_old.add(X)
                        if K[X] < rd.sigval:
                            waits[("p", X)] = max(waits.get(("p", X), 0), rd.sigval)
                            merged.append(d)
                nw_before += len(seen_old)
                if r.dma:
                    k = nsem[E]
                    if r.dmaidx >= k:
                        key = ("d", E, r.dmaidx % k)
                        val = 16 * (r.dmaidx // k)
                        if dwaited[E].get(key, 0) < val:
                            waits[key] = max(waits.get(key, 0), val)
                for key, val in waits.items():
                    if key[0] == "d":
                        dwaited[E][key] = val
                    else:
                        nw_after += 1
                        if K[key[1]] < val:
                            K[key[1]] = val
                for d in merged:
                    vd = vc[d]
                    for x in ENGS:
                        if vd[x] > K[x]:
                            K[x] = vd[x]
                v = dict(K)
                if not r.dma:
                    if r.signal:
                        sigcount[E] = r.sigval
                    if sigcount[E] > v[E]:
                        v[E] = sigcount[E]
                vc[i] = v
                plan[i] = list(waits.items())

            def sem_of(key):
                if key[0] == "d":
                    return dsem[key[1]][key[2]]
                return prog[key[1]]

            def run(eng_name, e):
                for i, r in enumerate(recs):
                    if r.eng != eng_name:
                        continue
                    todo = [(sem_of(key), val) for key, val in plan[i]]
                    for sem, val in todo[:-1]:
                        e.wait_ge(sem, val)
                    ins = r.fn(e)
                    if todo:
                        ins._wait_ge(todo[-1][0], todo[-1][1])
                    if r.dma:
                        k = len(dsem[r.eng])
                        ins.then_inc(dsem[r.eng][r.dmaidx % k], 16)
                    elif r.signal:
                        ins.then_inc(prog[r.eng], 1)
                if eng_name == final_dma_wait_eng:
                    for en in ENGS:
                        n = self.ndma[en]
                        if n == 0:
                            continue
                        k = len(dsem[en])
                        for j in range(k):
                            uses = (n - j + k - 1) // k
                            if uses > 0:
                                e.wait_ge(dsem[en][j], 16 * uses)

            @block.tensor
            def _(e):
                run("pe", e)

            @block.scalar
            def _(e):
                run("act", e)

            @block.vector
            def _(e):
                run("dve", e)

            @block.gpsimd
            def _(e):
                run("pool", e)

            @block.sync
            def _(e):
                run("sp", e)


P = 128
D = 1024
DC = 8
SEQ = 2048
NB_LOCAL = 2
STW = 1024
TT = 512
NTT = 2
NBLK = 8
DFF = 2816
FC = 22
MEM = 256
ALPHA = float(4 ** 0.25)
EPS = 1e-5
NSLOT = 6
POOL_WINDOWS = (2, 4, 8, 16)

PV = {}
_c = 0
for _l in range(2):
    for _n in ("mix_g", "mix_b", "xa_g", "xa_b", "ffn_g", "ffn_b"):
        PV[(_n, _l)] = _c
        _c += 8
PV["pool_scale"] = _c
_c += 4
PV["c_conv_w"] = _c
_c += 24
for _l in range(2):
    PV[("ffn_conv_w", _l)] = _c
    _c += 66
for _l in range(2):
    PV[("ffn_conv_b", _l)] = _c
    _c += 22
NPV = _c


def build_program(n_st=4, stop_after=None):
    nc = bass.Bass("TRN2", target_bir_lowering=False)
    S = Sched()

    def dram_in(name, shape):
        return nc.dram_tensor(name, list(shape), F32, kind="ExternalInput").ap()

    xT_d = dram_in("xT", [NB_LOCAL, DC, P, SEQ])
    memT_d = dram_in("memT", [NB_LOCAL, DC, P, MEM])
    ab_w_in = dram_in("ab_w_in", [D, 1536])
    ab_w_out = dram_in("ab_w_out", [D, D])
    c_w_in = dram_in("c_w_in", [D, 3072])
    c_w_out = dram_in("c_w_out", [D, D])
    xa_wq = dram_in("xa_wq", [2, D, D])
    xa_wkv = dram_in("xa_wkv", [2, D, 2 * D])
    xa_wo = dram_in("xa_wo", [2, D, D])
    ffn_w_up = dram_in("ffn_w_up", [2, D, 2 * DFF])
    ffn_w_down = dram_in("ffn_w_down", [2, DFF, D])
    pv_d = dram_in("pv", [P, NPV])
    sgub_d = dram_in("sgub", [P, 1024])
    sguwT_d = dram_in("sguwT", [P, 4, P])
    sgubias_d = dram_in("sgubias", [1, 512])
    poolmats_d = dram_in("poolmats", [P, 20, P])
    poolw_d = dram_in("poolw", [P, 4, P])
    outT_d = nc.dram_tensor("outT", [NB_LOCAL, DC, P, SEQ], F32, kind="ExternalOutput").ap()

    with ExitStack() as es:
        def sb(name, shape, dt):
            return es.enter_context(nc.sbuf_tensor("sb_" + name, list(shape), dt))

        xT32 = sb("xT32", [P, DC, STW], F32)
        xTb = sb("xTb", [P, DC, STW], BF16)
        NAR = 44
        arena = sb("arena", [P, NAR, TT], BF16)
        wring = [sb("wslot%d" % i, [P, 4096], BF16) for i in range(NSLOT)]
        kT = [sb("kT%d" % l, [P, DC, MEM], BF16) for l in range(2)]
        vtok = [sb("vtok%d" % l, [P, 2, D], BF16) for l in range(2)]
        pv = sb("pv", [P, NPV], F32)
        sgub = sb("sgub", [P, 1024], F32)
        WmT = sb("WmT", [P, 4, P], BF16)
        bs_hi = sb("bs_hi", [1, 512], BF16)
        bs_lo = sb("bs_lo", [1, 512], BF16)
        poolmats = sb("poolmats", [P, 20, P], BF16)
        poolw = sb("poolw", [P, 4, P], BF16)
        onesm = sb("onesm", [P, P], BF16)
        ones1 = sb("ones1", [P, P], BF16)
        neghalf = sb("neghalf", [P, 8], F32)
        NF = 6
        ftmp = [sb("ftmp%d" % i, [P, TT], F32) for i in range(NF)]
        bs32 = ftmp[0][0:1, :]
        bsh32 = ftmp[1][0:1, :]
        rstd_b = [sb("rstd%d" % i, [P, TT], F32) for i in range(2)]
        nmr_b = [sb("nmr%d" % i, [P, TT], F32) for i in range(2)]
        rsq_b = [sb("rsq%d" % i, [P, TT], BF16) for i in range(3)]
        gbuf = [sb("gbuf%d" % i, [P, TT + 2], F32) for i in range(4)]
        halo_c = sb("halo_c", [P, DC, 2], F32)
        halo_f = [sb("halo_f%d" % l, [P, FC, 2], F32) for l in range(2)]
        small = sb("small", [P, 64], F32)
        xbt = sb("xbt", [P, 4, TT], BF16)
        psum = [es.enter_context(nc.psum_tensor("ps%d" % i, [P, TT], F32)) for i in range(8)]

        t_xT32 = [[Tile("x32_%d_%d" % (c, t)) for t in range(NTT)] for c in range(DC)]
        t_xTb = [[Tile("xb_%d_%d" % (c, t)) for t in range(NTT)] for c in range(DC)]
        t_ar = [Tile("ar%d" % i) for i in range(NAR)]
        t_ws = [Tile("ws%d" % i) for i in range(NSLOT)]
        t_kT = [Tile("kT%d" % l) for l in range(2)]
        t_vt = [Tile("vt%d" % l) for l in range(2)]
        t_const = Tile("const")
        t_ft = [Tile("ft%d" % i) for i in range(NF)]
        t_rstd = [Tile("rstd%d" % i) for i in range(2)]
        t_nmr = [Tile("nmr%d" % i) for i in range(2)]
        t_rsq = [Tile("rsq%d" % i) for i in range(3)]
        t_gbuf = [Tile("gbuf%d" % i) for i in range(4)]
        t_halo_c = Tile("halo_c")
        t_halo_f = [Tile("halo_f%d" % l) for l in range(2)]
        t_small = Tile("small")
        t_xbt = [Tile("xbt%d" % i) for i in range(4)]
        t_ps = [Tile("ps%d" % i) for i in range(8)]
        t_out = Tile("out")

        def x32(c, tt):
            return xT32[:, c, tt * TT:(tt + 1) * TT]

        def xb(c, tt):
            return xTb[:, c, tt * TT:(tt + 1) * TT]

        def ar(i):
            return arena[:, i, :]

        ps_state = {"free": list(range(8)), "i": 0}

        def ps_next():
            fl = ps_state["free"]
            b = fl[ps_state["i"] % len(fl)]
            ps_state["i"] += 1
            return b

        ft_state = {"i": 0}

        def ft_next():
            i = ft_state["i"] % NF
            ft_state["i"] += 1
            return i

        wplan = []

        def wview(W, c0, n):
            return W.rearrange("(k p) n -> p k n", p=P)[:, :, c0:c0 + n]

        def plan_kv(l):
            for i in range(4):
                wplan.append((wview(xa_wkv[l], i * 512, 512), 8, 512))

        def plan_layer(l, kv_after_mix=False):
            if l == 0:
                for i in range(3):
                    wplan.append((wview(ab_w_in, i * 512, 512), 8, 512))
                for i in range(2):
                    wplan.append((wview(ab_w_out, i * 512, 512), 8, 512))
                if kv_after_mix:
                    plan_kv(0)
                    plan_kv(1)
            else:
                for q in range(2):
                    for part in (1, 2, 0):
                        wplan.append((wview(c_w_in, part * 1024 + q * 512, 512), 8, 512))
                for i in range(2):
                    wplan.append((wview(c_w_out, i * 512, 512), 8, 512))
            for i in range(2):
                wplan.append((wview(xa_wq[l], i * 512, 512), 8, 512))
            for i in range(2):
                wplan.append((wview(xa_wo[l], i * 512, 512), 8, 512))
            for q in range(6):
                n = 512 if q < 5 else 256
                wplan.append((wview(ffn_w_up[l], DFF + q * 512, n), 8, n))
                wplan.append((wview(ffn_w_up[l], q * 512, n), 8, n))
            for m in range(8):
                wplan.append((wview(ffn_w_down[l], m * 128, 128), FC, 128))

        for st in range(n_st):
            if st % 2 == 0 and st > 0:
                plan_kv(0)
                plan_kv(1)
            plan_layer(0, st == 0)
            plan_layer(1)

        wstate = {"issued": 0, "next": 0, "done": 0}

        def w_pump():
            lim = min(len(wplan), wstate["done"] + NSLOT)
            while wstate["issued"] < lim:
                j = wstate["issued"]
                src, kc, n = wplan[j]
                s = j % NSLOT
                dst = wring[s][:, 0:kc * n].rearrange("p (k n) -> p k n", k=kc)
                S.op("pool", lambda e, dst=dst, src=src: e.dma_start(out=dst, in_=src),
                     writes=[t_ws[s]], dma=True)
                wstate["issued"] += 1

        def w_release(upto):
            if upto > wstate["done"]:
                wstate["done"] = upto
            w_pump()

        def w_next(kc, n, release_prior=True):
            i = wstate["next"]
            wstate["next"] += 1
            assert wplan[i][1] == kc and wplan[i][2] == n, (i, wplan[i][1:], kc, n)
            if release_prior:
                w_release(i)
            else:
                w_pump()
            assert wstate["issued"] > i, "too many live weight slots"
            s = i % NSLOT
            view = wring[s][:, 0:kc * n].rearrange("p (k n) -> p k n", k=kc)
            return view, t_ws[s]

        S.op("sp", lambda e: e.dma_start(out=pv[:], in_=pv_d), writes=[t_const], dma=True)
        S.op("sp", lambda e: e.dma_start(out=sgub[:], in_=sgub_d), writes=[t_const], dma=True)
        S.op("sp", lambda e: e.dma_start(out=bs32, in_=sgubias_d), writes=[t_ft[0]], dma=True)
        t_c2 = Tile("const2")
        S.op("pool", lambda e: e.dma_start(out=WmT[:], in_=sguwT_d), writes=[t_c2], dma=True)
        t_c3 = Tile("const3")
        S.op("pool", lambda e: e.dma_start(out=poolmats[:], in_=poolmats_d), writes=[t_c3], dma=True)
        t_c4 = Tile("const4")
        S.op("pool", lambda e: e.dma_start(out=poolw[:], in_=poolw_d), writes=[t_c4], dma=True)
        S.op("dve", lambda e: e.memset(WmT[64:128, :, 0:64], 0.0), writes=[t_c2])
        t_c5 = Tile("const5")
        S.op("dve", lambda e: e.memset(onesm[:], 1.0 / 1024.0), writes=[t_c5])
        S.op("dve", lambda e: e.memset(ones1[:], 1.0), writes=[t_c5])
        S.op("dve", lambda e: e.memset(neghalf[:], -0.5), writes=[t_c5])
        t_bs = Tile("bs")
        S.op("dve", lambda e: e.tensor_copy(bs_hi[:], bs32), reads=[t_ft[0]], writes=[t_bs])
        S.op("dve", lambda e: e.tensor_copy(bsh32, bs_hi[:]), reads=[t_bs], writes=[t_ft[1]])
        S.op("dve", lambda e: e.tensor_tensor(bsh32, bs32, bsh32, ALU.subtract),
             reads=[t_ft[0], t_ft[1]], writes=[t_ft[1]])
        S.op("dve", lambda e: e.tensor_copy(bs_lo[:], bsh32), reads=[t_ft[1]], writes=[t_bs])
        t_consts_all = [t_const, t_c2, t_c3, t_c4, t_c5, t_bs]

        def pvc(key, i=0, n=1):
            c = PV[key] + i
            return pv[:, c:c + n]

        from collections import deque
        deferred = deque()

        def pump(k=1):
            for _ in range(k):
                if not deferred:
                    return
                deferred.popleft()[1]()

        def pump_crit():
            while deferred and deferred[0][0]:
                deferred.popleft()[1]()

        def flush():
            while deferred:
                deferred.popleft()[1]()

        def mm_unit(wv, wt, coff, KC, in_ap, in_tile, tt):
            b = ps_next()
            for k in range(KC):
                S.op("pe", lambda e, b=b, k=k: e.matmul(
                    psum[b][:], wv[:, k, coff:coff + P], in_ap(k, tt),
                    start=(k == 0), stop=(k == KC - 1)),
                    reads=[wt, in_tile(k, tt)], writes=[t_ps[b]])
            return b

        def proj_ln(groups, fetch, KC, in_ap, in_tile, gkey, bkey, l, boundary=None):
            saved_free = ps_state["free"]
            nfl = len(saved_free)
            order = [saved_free[(ps_state["i"] + j) % nfl] for j in range(nfl)]
            assert nfl == 8
            stat = [order[4], order[6], order[5], order[7]]
            ps_state["free"] = order[0:4]
            ps_state["i"] = 0
            nseen = [0, 0]

            def stats(m, tt, r, first, last):
                S.op("pe", lambda e: e.matmul(
                    psum[stat[tt]][:], onesm[:], xb(m, tt), start=first, stop=last),
                    reads=[t_c5, t_xTb[m][tt]], writes=[t_ps[stat[tt]]])
                S.op("pe", lambda e: e.matmul(
                    psum[stat[2 + tt]][:], onesm[:], rsq_b[r][:], start=first, stop=last),
                    reads=[t_c5, t_rsq[r]], writes=[t_ps[stat[2 + tt]]])

            def finalize_items(tt, after_head=None, per_chunk=None, after_all=None):
                items = []

                def head():
                    f0 = ft_next()
                    S.op("act", lambda e: e.activation(ftmp[f0][:], psum[stat[tt]][:], AF.Square),
                         reads=[t_ps[stat[tt]]], writes=[t_ft[f0]])
                    S.op("dve", lambda e: e.scalar_tensor_tensor(
                        ftmp[f0][:], psum[stat[2 + tt]][:], EPS, ftmp[f0][:], ALU.add, ALU.subtract),
                        reads=[t_ps[stat[2 + tt]], t_ft[f0]], writes=[t_ft[f0]])
                    S.op("act", lambda e: e.activation(ftmp[f0][:], ftmp[f0][:], AF.Ln),
                         reads=[t_ft[f0]], writes=[t_ft[f0]])
                    S.op("act", lambda e: e.activation(rstd_b[tt][:], ftmp[f0][:], AF.Exp, scale=-0.5),
                         reads=[t_ft[f0]], writes=[t_rstd[tt]])
                    S.op("dve", lambda e: e.scalar_tensor_tensor(
                        nmr_b[tt][:], psum[stat[tt]][:], -1.0, rstd_b[tt][:], ALU.mult, ALU.mult),
                        reads=[t_ps[stat[tt]], t_rstd[tt]], writes=[t_nmr[tt]])
                items.append((True, head))
                if after_head is not None:
                    items.append((True, after_head))
                late = []
                for m in range(DC):
                    def app(m=m):
                        eng = "dve"
                        S.op(eng, lambda e: e.tensor_tensor(
                            x32(m, tt), x32(m, tt), rstd_b[tt][:], ALU.mult),
                            reads=[t_xT32[m][tt], t_rstd[tt]], writes=[t_xT32[m][tt]])
                        S.op(eng, lambda e: e.tensor_tensor(
                            x32(m, tt), x32(m, tt), nmr_b[tt][:], ALU.add),
                            reads=[t_xT32[m][tt], t_nmr[tt]], writes=[t_xT32[m][tt]])
                        if per_chunk is None:
                            S.op("act", lambda e: e.activation(
                                xb(m, tt), x32(m, tt), AF.Identity,
                                bias=pvc((bkey, l), m), scale=pvc((gkey, l), m)),
                                reads=[t_xT32[m][tt], t_const], writes=[t_xTb[m][tt]])
                    items.append((True, app))

                    def aff(m=m):
                        if m in (3, 6, 7):
                            S.op("act", lambda e: e.activation(
                                x32(m, tt), x32(m, tt), AF.Identity,
                                bias=pvc((bkey, l), m), scale=pvc((gkey, l), m)),
                                reads=[t_xT32[m][tt], t_const], writes=[t_xT32[m][tt]])
                        else:
                            S.op("dve", lambda e: e.tensor_scalar(
                                x32(m, tt), x32(m, tt), pvc((gkey, l), m), pvc((bkey, l), m),
                                ALU.mult, ALU.add),
                                reads=[t_xT32[m][tt], t_const], writes=[t_xT32[m][tt]])
                    if per_chunk is not None:
                        items.append((True, aff))
                        items.append((True, per_chunk[m]))
                    else:
                        late.append((False, aff))
                items.extend(late)
                if after_all is not None:
                    items.extend((True, f) for f in after_all)
                return items

            STAT_LAG = 2 if KC <= 8 else 1
            pendq = deque()

            def emit_stats(p):
                stats(*p)
                if p[4] and p[1] == 0:
                    if boundary is not None:
                        deferred.extend(finalize_items(0, None, boundary[0], boundary[1]))
                    else:
                        deferred.extend(finalize_items(0))

            ng = len(groups)
            for gi, group in enumerate(groups):
                ws = fetch(group)
                for tt in range(NTT):
                    for m in group:
                        wv, wt, coff = ws[m]
                        b = mm_unit(wv, wt, coff, KC, in_ap, in_tile, tt)
                        S.op("dve", lambda e, m=m, tt=tt, b=b: e.scalar_tensor_tensor(
                            x32(m, tt), x32(m, tt), ALPHA, psum[b][:], ALU.mult, ALU.add),
                            reads=[t_xT32[m][tt], t_ps[b]], writes=[t_xT32[m][tt]])
                        S.op("act", lambda e, m=m, tt=tt: e.activation(xb(m, tt), x32(m, tt), AF.Copy),
                             reads=[t_xT32[m][tt]], writes=[t_xTb[m][tt]])
                        r = rsq_state["i"] % 3
                        rsq_state["i"] += 1
                        S.op("act", lambda e, m=m, tt=tt, r=r: e.activation(rsq_b[r][:], x32(m, tt), AF.Square),
                             reads=[t_xT32[m][tt]], writes=[t_rsq[r]])
                        first = nseen[tt] == 0
                        nseen[tt] += 1
                        last = nseen[tt] == DC
                        pendq.append((m, tt, r, first, last))
                        while len(pendq) > STAT_LAG:
                            emit_stats(pendq.popleft())
                        if boundary is not None:
                            pump_crit()
                        if len(group) >= 8:
                            pump(2 if m == group[0] else 1)
                        else:
                            pump(3)
                    if gi == ng - 1 and tt == 0:
                        flush()
            flush()
            while pendq:
                deferred.append((True, lambda p=pendq.popleft(): stats(*p)))

            def restore():
                ps_state["free"] = saved_free
            if boundary is not None:
                deferred.extend(finalize_items(1, restore, boundary[2], boundary[3]))
            else:
                deferred.extend(finalize_items(1, restore))

        def fetch_2x512(group_sizes=(8,)):
            def fetch(group):
                w0 = w_next(8, 512)
                w1 = w_next(8, 512, False)
                out = {}
                for m in group:
                    wv, wt = (w0, w1)[m // 4]
                    out[m] = (wv, wt, (m % 4) * P)
                return out
            return fetch

        def x_in_ap(k, tt):
            return xb(k, tt)

        def x_in_tile(k, tt):
            return t_xTb[k][tt]

        ALLM = [list(range(DC))]

        def mixer_ab(st):
            WA, tA = w_next(8, 512)
            iWA = wstate["next"] - 1
            WB, tB = w_next(8, 512, False)
            WC, tC = w_next(8, 512, False)
            seq_first_st = (st % 2 == 0)

            def uT(g, tt):
                return g * 2 + tt

            def yT(c, tt):
                return 8 + c * 2 + tt

            def pooledT(g, tt):
                return 24 + g * 2 + tt

            def u_units(tt):
                for m in range(4):
                    b = mm_unit(WA, tA, m * P, 8, x_in_ap, x_in_tile, tt)
                    i = uT(m, tt)
                    S.op("act", lambda e, b=b, i=i: e.activation(ar(i), psum[b][:], AF.Gelu_apprx_tanh),
                         reads=[t_ps[b]], writes=[t_ar[i]])
                    pump(2)

            binfo = {}

            def vxb(j):
                tt = j // 4
                gj = st * NBLK + j
                bv = ps_next()
                bx = ps_next()
                for k in range(8):
                    S.op("pe", lambda e, k=k: e.matmul(
                        psum[bv][:], xTb[:, k, j * P:(j + 1) * P], WB[:, k, :], start=(k == 0), stop=(k == 7)),
                        reads=[t_xTb[k][tt], tB], writes=[t_ps[bv]])
                for k in range(8):
                    S.op("pe", lambda e, k=k: e.matmul(
                        psum[bx][:], xTb[:, k, j * P:(j + 1) * P], WC[:, k, :], start=(k == 0), stop=(k == 7)),
                        reads=[t_xTb[k][tt], tC], writes=[t_ps[bx]])
                pump_crit()
                fv = ft_next()
                S.op("act", lambda e: e.activation(ftmp[fv][:], psum[bv][:], AF.Gelu_apprx_tanh),
                     reads=[t_ps[bv]], writes=[t_ft[fv]])
                ixb = gj % 4
                S.op("act", lambda e: e.activation(xbt[:, ixb, :], psum[bx][:], AF.Copy),
                     reads=[t_ps[bx]], writes=[t_xbt[ixb]])
                so = (gj % 4) * 16
                S.op("dve", lambda e: e.bn_stats(small[:, so:so + 6], ftmp[fv][:]),
                     reads=[t_ft[fv]], writes=[t_small])
                S.op("dve", lambda e: e.bn_aggr(small[:, so + 6:so + 8], small[:, so:so + 6]),
                     reads=[t_small], writes=[t_small])
                S.op("dve", lambda e: e.tensor_scalar(
                    small[:, so + 8:so + 9], small[:, so + 7:so + 8], EPS, None, ALU.add),
                    reads=[t_small], writes=[t_small])
                S.op("pool", lambda e: e.tensor_tensor(
                    small[:, so + 9:so + 10], small[:, so + 8:so + 9], neghalf[:, 0:1], ALU.pow),
                    reads=[t_small, t_c5], writes=[t_small])
                S.op("dve", lambda e: e.tensor_scalar(
                    ftmp[fv][:], ftmp[fv][:], small[:, so + 6:so + 7], small[:, so + 9:so + 10],
                    ALU.subtract, ALU.mult),
                    reads=[t_ft[fv], t_small], writes=[t_ft[fv]])
                S.op("dve", lambda e: e.tensor_tensor(ftmp[fv][:], ftmp[fv][:], sgub[:, 0:512], ALU.mult),
                     reads=[t_ft[fv], t_const], writes=[t_ft[fv]])
                ivl = 32 + gj % 4
                S.op("dve", lambda e: e.tensor_tensor(
                    ar(ivl), ftmp[fv][:], sgub[:, 512:1024], ALU.add),
                    reads=[t_ft[fv], t_const], writes=[t_ar[ivl]])
                binfo[j] = (ivl, ixb, (gj - 1) % 4)
                pump(2)

            def sgu_pool(j):
                tt = j // 4
                col = (j % 4) * P
                ivl, ixb, ixp = binfo[j]
                bs_ = ps_next()
                for g in range(4):
                    S.op("pe", lambda e, g=g: e.matmul(
                        psum[bs_][:, g * P:(g + 1) * P], arena[:, ivl, g * P:(g + 1) * P], WmT[:, g, :],
                        start=True, stop=False),
                        reads=[t_ar[ivl], t_c2], writes=[t_ps[bs_]])
                    S.op("pe", lambda e, g=g: e.matmul(
                        psum[bs_][:, g * P:(g + 1) * P], ones1[0:1, :], bs_hi[0:1, g * P:(g + 1) * P],
                        start=False, stop=False),
                        reads=[t_c5, t_bs], writes=[t_ps[bs_]])
                    S.op("pe", lambda e, g=g: e.matmul(
                        psum[bs_][:, g * P:(g + 1) * P], ones1[0:1, :], bs_lo[0:1, g * P:(g + 1) * P],
                        start=False, stop=True),
                        reads=[t_c5, t_bs], writes=[t_ps[bs_]])
                bp = ps_next()
                first = seq_first_st and j == 0
                for g in range(4):
                    if first:
                        S.op("pe", lambda e, g=g: e.matmul(
                            psum[bp][:, g * P:(g + 1) * P], xbt[:, ixb, g * P:(g + 1) * P], poolmats[:, 8 + g, :],
                            start=True, stop=False),
                            reads=[t_xbt[ixb], t_c3], writes=[t_ps[bp]])
                        S.op("pe", lambda e, g=g: e.matmul(
                            psum[bp][:, g * P:(g + 1) * P], xbt[:, ixb, g * P:(g + 1) * P], poolmats[:, 12 + g, :],
                            start=False, stop=False),
                            reads=[t_xbt[ixb], t_c3], writes=[t_ps[bp]])
                        S.op("pe", lambda e, g=g: e.matmul(
                            psum[bp][:, g * P:(g + 1) * P], xbt[:, ixb, g * P:(g + 1) * P], poolmats[:, 16 + g, :],
                            start=False, stop=True),
                            reads=[t_xbt[ixb], t_c3], writes=[t_ps[bp]])
                    else:
                        S.op("pe", lambda e, g=g: e.matmul(
                            psum[bp][:, g * P:(g + 1) * P], xbt[:, ixb, g * P:(g + 1) * P], poolmats[:, g, :],
                            start=True, stop=False),
                            reads=[t_xbt[ixb], t_c3], writes=[t_ps[bp]])
                        S.op("pe", lambda e, g=g: e.matmul(
                            psum[bp][:, g * P:(g + 1) * P], xbt[:, ixp, g * P:(g + 1) * P], poolmats[:, 4 + g, :],
                            start=False, stop=True),
                            reads=[t_xbt[ixp], t_c3], writes=[t_ps[bp]])
                for g in range(4):
                    iy = yT(g, tt)
                    iu = uT(g, tt)
                    S.op("dve", lambda e, g=g, iy=iy, iu=iu: e.tensor_tensor(
                        arena[:, iy, col:col + P], psum[bs_][:, g * P:(g + 1) * P], arena[:, iu, col:col + P],
                        ALU.mult),
                        reads=[t_ps[bs_], t_ar[iu]], writes=[t_ar[iy]])
                for g in range(4):
                    ip = pooledT(g, tt)
                    S.op("act", lambda e, g=g, ip=ip: e.activation(
                        arena[:, ip, col:col + P], psum[bp][:, g * P:(g + 1) * P], AF.Copy),
                        reads=[t_ps[bp]], writes=[t_ar[ip]])
                pump(1)

            def yb(tt):
                for g in range(4):
                    b = ps_next()
                    ip = pooledT(g, tt)
                    S.op("pe", lambda e, g=g, b=b, ip=ip: e.matmul(
                        psum[b][:], poolw[:, g, :], ar(ip), start=True, stop=True),
                        reads=[t_c4, t_ar[ip]], writes=[t_ps[b]])
                    iy = yT(4 + g, tt)
                    S.op("act", lambda e, g=g, b=b, iy=iy: e.activation(
                        ar(iy), psum[b][:], AF.Identity, scale=pvc("pool_scale", g)),
                        reads=[t_ps[b], t_const], writes=[t_ar[iy]])

            vxb(0)
            vxb(1)
            u_units(0)
            vxb(2)
            sgu_pool(0)
            vxb(3)
            sgu_pool(1)
            flush()
            vxb(4)
            sgu_pool(2)
            vxb(5)
            sgu_pool(3)
            u_units(1)
            yb(0)
            vxb(6)
            sgu_pool(4)
            vxb(7)
            sgu_pool(5)
            sgu_pool(6)
            sgu_pool(7)
            yb(1)
            w_release(iWA + 3)
            flush()
            proj_ln(ALLM, fetch_2x512(), 8, lambda k, tt: ar(yT(k, tt)), lambda k, tt: t_ar[yT(k, tt)],
                    "mix_g", "mix_b", 0)

        def conv3(gb_i, out_f, wcol):
            S.op("dve", lambda e: e.tensor_scalar(
                ftmp[out_f][:], gbuf[gb_i][:, 0:TT], pv[:, wcol:wcol + 1], None, ALU.mult),
                reads=[t_gbuf[gb_i], t_const], writes=[t_ft[out_f]])
            S.op("dve", lambda e: e.scalar_tensor_tensor(
                ftmp[out_f][:], gbuf[gb_i][:, 1:TT + 1], pv[:, wcol + 1:wcol + 2], ftmp[out_f][:],
                ALU.mult, ALU.add),
                reads=[t_gbuf[gb_i], t_const, t_ft[out_f]], writes=[t_ft[out_f]])
            S.op("dve", lambda e: e.scalar_tensor_tensor(
                ftmp[out_f][:], gbuf[gb_i][:, 2:TT + 2], pv[:, wcol + 2:wcol + 3], ftmp[out_f][:],
                ALU.mult, ALU.add),
                reads=[t_gbuf[gb_i], t_const, t_ft[out_f]], writes=[t_ft[out_f]])

        def halo_io(gi, halo_ap, halo_tile):
            S.op("pool", lambda e: e.tensor_copy(gbuf[gi][:, 0:2], halo_ap),
                 reads=[halo_tile], writes=[t_gbuf[gi]])
            S.op("pool", lambda e: e.tensor_copy(halo_ap, gbuf[gi][:, TT:TT + 2]),
                 reads=[t_gbuf[gi]], writes=[halo_tile])

        gb_state = {"i": 0}
        rsq_state = {"i": 0}

        def gb_next():
            i = gb_state["i"] % 4
            gb_state["i"] += 1
            return i

        def mixer_c(st):
            def yT(c, tt):
                return c * 2 + tt
            if st % 2 == 0:
                S.op("pool", lambda e: e.memset(halo_c[:], 0.0), writes=[t_halo_c])
            for q in range(2):
                wc_ = w_next(8, 512)
                wh_ = w_next(8, 512, False)
                wb_ = w_next(8, 512, False)
                for tt in range(NTT):
                    if q == 0 and tt == 1:
                        flush()
                    for m in range(q * 4, q * 4 + 4):
                        coff = (m % 4) * P
                        bc = mm_unit(wc_[0], wc_[1], coff, 8, x_in_ap, x_in_tile, tt)
                        bh = mm_unit(wh_[0], wh_[1], coff, 8, x_in_ap, x_in_tile, tt)
                        pump_crit()
                        fc_ = ft_next()
                        S.op("act", lambda e, fc_=fc_, bc=bc: e.activation(ftmp[fc_][:], psum[bc][:], AF.Copy),
                             reads=[t_ps[bc]], writes=[t_ft[fc_]])
                        gi = gb_next()
                        S.op("dve", lambda e, fc_=fc_, bh=bh, gi=gi: e.tensor_tensor(
                            gbuf[gi][:, 2:TT + 2], ftmp[fc_][:], psum[bh][:], ALU.mult),
                            reads=[t_ft[fc_], t_ps[bh]], writes=[t_gbuf[gi]])
                        halo_io(gi, halo_c[:, m, :], t_halo_c)
                        fo = ft_next()
                        conv3(gi, fo, PV["c_conv_w"] + m * 3)
                        bb = mm_unit(wb_[0], wb_[1], coff, 8, x_in_ap, x_in_tile, tt)
                        iy = yT(m, tt)
                        S.op("dve", lambda e, bb=bb, fo=fo, iy=iy: e.tensor_tensor(
                            ar(iy), ftmp[fo][:], psum[bb][:], ALU.mult),
                            reads=[t_ft[fo], t_ps[bb]], writes=[t_ar[iy]])
                        pump(3)
            flush()
            proj_ln(ALLM, fetch_2x512(), 8, lambda k, tt: ar(yT(k, tt)), lambda k, tt: t_ar[yT(k, tt)],
                    "mix_g", "mix_b", 1)

        def kv_phase(b, l):
            mT = arena[:, 40:44, :].rearrange("p a (c m) -> p (a c) m", m=MEM)
            t_m = t_ar[40:44]
            if l == 0:
                S.op("pool", lambda e: e.dma_start(out=mT, in_=memT_d[b].rearrange("c p m -> p c m")),
                     writes=list(t_m), dma=True)
            ws = [w_next(8, 512, i == 0) for i in range(4)]
            for c in range(DC):
                wv, wt = ws[c // 4]
                bk = ps_next()
                for k in range(8):
                    S.op("pe", lambda e, k=k, bk=bk, wv=wv, c=c: e.matmul(
                        psum[bk][:, 0:MEM], wv[:, k, (c % 4) * P:(c % 4 + 1) * P], mT[:, k, :],
                        start=(k == 0), stop=(k == 7)),
                        reads=[wt] + list(t_m), writes=[t_ps[bk]])
                S.op("act", lambda e, bk=bk, c=c: e.activation(kT[l][:, c, :], psum[bk][:, 0:MEM], AF.Copy),
                     reads=[t_ps[bk]], writes=[t_kT[l]])
            for mc in range(2):
                for n in range(2):
                    wv, wt = ws[2 + n]
                    bk = ps_next()
                    for k in range(8):
                        S.op("pe", lambda e, k=k, bk=bk, wv=wv, mc=mc: e.matmul(
                            psum[bk][:], mT[:, k, mc * P:(mc + 1) * P], wv[:, k, :],
                            start=(k == 0), stop=(k == 7)),
                            reads=[wt] + list(t_m), writes=[t_ps[bk]])
                    S.op("dve", lambda e, bk=bk, mc=mc, n=n: e.tensor_copy(
                        vtok[l][:, mc, n * TT:(n + 1) * TT], psum[bk][:]),
                        reads=[t_ps[bk]], writes=[t_vt[l]])

        def xattn(st, l):
            def qT(c, tt):
                return c * 2 + tt

            def pT(h, mc):
                return 16 + h * 2 + mc

            def aoT(c, tt):
                return 24 + c * 2 + tt
            wq0 = w_next(8, 512)
            wq1 = w_next(8, 512, False)

            def q_unit(m, tt):
                wv, wt = (wq0, wq1)[m // 4]
                b = mm_unit(wv, wt, (m % 4) * P, 8, x_in_ap, x_in_tile, tt)
                pump_crit()
                i = qT(m, tt)
                if m % 2 == 0:
                    S.op("act", lambda e: e.activation(ar(i), psum[b][:], AF.Identity, scale=1.0 / 16.0),
                         reads=[t_ps[b]], writes=[t_ar[i]])
                else:
                    S.op("dve", lambda e: e.tensor_scalar(
                        ar(i), psum[b][:], 1.0 / 16.0, None, ALU.mult),
                        reads=[t_ps[b]], writes=[t_ar[i]])

            def scores(h, tt):
                for mc in range(2):
                    b = ps_next()
                    for kc in range(2):
                        iq = qT(2 * h + kc, tt)
                        S.op("pe", lambda e, b=b, kc=kc, mc=mc, iq=iq: e.matmul(
                            psum[b][:], kT[l][:, 2 * h + kc, mc * P:(mc + 1) * P], ar(iq),
                            start=(kc == 0), stop=(kc == 1)),
                            reads=[t_kT[l], t_ar[iq]], writes=[t_ps[b]])
                    ip = pT(h, mc)
                    S.op("act", lambda e, b=b, ip=ip: e.activation(ar(ip), psum[b][:], AF.Exp),
                         reads=[t_ps[b]], writes=[t_ar[ip]])

            def den_pv(h, tt):
                bd = ps_next()
                for mc in range(2):
                    ip = pT(h, mc)
                    S.op("pe", lambda e, mc=mc, ip=ip: e.matmul(
                        psum[bd][:], ones1[:], ar(ip), start=(mc == 0), stop=(mc == 1)),
                        reads=[t_c5, t_ar[ip]], writes=[t_ps[bd]])
                fr = ft_next()
                S.op("act", lambda e: e.activation(ftmp[fr][:], psum[bd][:], AF.Ln),
                     reads=[t_ps[bd]], writes=[t_ft[fr]])
                S.op("act", lambda e: e.activation(ftmp[fr][:], ftmp[fr][:], AF.Exp, scale=-1.0),
                     reads=[t_ft[fr]], writes=[t_ft[fr]])
                for dc in range(2):
                    bo = ps_next()
                    for mc in range(2):
                        ip = pT(h, mc)
                        S.op("pe", lambda e, bo=bo, mc=mc, ip=ip, dc=dc: e.matmul(
                            psum[bo][:], vtok[l][:, mc, (2 * h + dc) * P:(2 * h + dc + 1) * P], ar(ip),
                            start=(mc == 0), stop=(mc == 1)),
                            reads=[t_vt[l], t_ar[ip]], writes=[t_ps[bo]])
                    io = aoT(2 * h + dc, tt)
                    S.op("dve", lambda e, bo=bo, io=io: e.tensor_tensor(
                        ar(io), psum[bo][:], ftmp[fr][:], ALU.mult),
                        reads=[t_ps[bo], t_ft[fr]], writes=[t_ar[io]])

            for m in range(DC):
                q_unit(m, 0)
                pump(2)
            scores(0, 0)
            scores(1, 0)
            flush()
            q_unit(0, 1)
            q_unit(1, 1)
            for h in range(4):
                if h + 2 < 4:
                    scores(h + 2, 0)
                q_unit(2 + h, 1)
                den_pv(h, 0)
            q_unit(6, 1)
            q_unit(7, 1)
            scores(0, 1)
            scores(1, 1)
            for h in range(4):
                if h + 2 < 4:
                    scores(h + 2, 1)
                den_pv(h, 1)
            flush()
            proj_ln(ALLM, fetch_2x512(), 8, lambda k, tt: ar(aoT(k, tt)), lambda k, tt: t_ar[aoT(k, tt)],
                    "xa_g", "xa_b", l)

        def ffn(st, l, boundary=None):
            def hT(f, tt):
                return f * 2 + tt
            if st % 2 == 0:
                S.op("pool", lambda e: e.memset(halo_f[l][:], 0.0), writes=[t_halo_f[l]])
            pend = [None]

            def gelu_of(p):
                fo, f, ba, ih = p
                S.op("act", lambda e: e.activation(
                    ftmp[fo][:], ftmp[fo][:], AF.Gelu_apprx_tanh, bias=pvc(("ffn_conv_b", l), f)),
                    reads=[t_ft[fo], t_const], writes=[t_ft[fo]])

            def mult_of(p):
                fo, f, ba, ih = p
                S.op("dve", lambda e: e.tensor_tensor(
                    ar(ih), ftmp[fo][:], psum[ba][:], ALU.mult),
                    reads=[t_ft[fo], t_ps[ba]], writes=[t_ar[ih]])

            for q in range(6):
                n = 512 if q < 5 else 256
                wg, tg = w_next(8, n)
                wa, ta = w_next(8, n, False)
                fs = list(range(q * 4, min(FC, q * 4 + 4)))
                for tt in range(NTT):
                    if q == 0 and tt == 1:
                        flush()
                    for f in fs:
                        coff = (f % 4) * P
                        bg = mm_unit(wg, tg, coff, 8, x_in_ap, x_in_tile, tt)
                        ba = mm_unit(wa, ta, coff, 8, x_in_ap, x_in_tile, tt)
                        pump_crit()
                        gi = gb_next()
                        S.op("act", lambda e, gi=gi, bg=bg: e.activation(gbuf[gi][:, 2:TT + 2], psum[bg][:], AF.Copy),
                             reads=[t_ps[bg]], writes=[t_gbuf[gi]])
                        halo_io(gi, halo_f[l][:, f, :], t_halo_f[l])
                        if pend[0] is not None:
                            gelu_of(pend[0])
                        fo = ft_next()
                        conv3(gi, fo, PV[("ffn_conv_w", l)] + f * 3)
                        if pend[0] is not None:
                            mult_of(pend[0])
                        pend[0] = (fo, f, ba, hT(f, tt))
                        pump(3)
            gelu_of(pend[0])
            mult_of(pend[0])
            flush()

            def fetch_down(group):
                out = {}
                for i, m in enumerate(group):
                    wv, wt = w_next(FC, 128, i == 0)
                    out[m] = (wv, wt, 0)
                return out
            proj_ln([[0, 1, 2, 3], [4, 5, 6, 7]], fetch_down, FC,
                    lambda k, tt: ar(hT(k, tt)), lambda k, tt: t_ar[hT(k, tt)], "ffn_g", "ffn_b", l,
                    boundary=boundary)

        def load_tile(st, c, tt):
            b = st // 2
            s0 = (st % 2) * STW
            S.op("sp", lambda e: e.dma_start(
                out=x32(c, tt), in_=xT_d[b, c, :, s0 + tt * TT:s0 + (tt + 1) * TT]),
                writes=[t_xT32[c][tt]], dma=True)
            S.op("pool", lambda e: e.dma_start(
                out=xb(c, tt), in_=xT_d[b, c, :, s0 + tt * TT:s0 + (tt + 1) * TT]),
                writes=[t_xTb[c][tt]], dma=True)

        def store_tile(st, c, tt):
            b = st // 2
            s0 = (st % 2) * STW
            S.op("sp", lambda e: e.dma_start(
                out=outT_d[b, c, :, s0 + tt * TT:s0 + (tt + 1) * TT], in_=x32(c, tt)),
                reads=[t_xT32[c][tt]], writes=[t_out], dma=True)

        done = False
        for st in range(n_st):
            b = st // 2
            if st == 0:
                for tt in range(NTT):
                    for c in range(DC):
                        load_tile(0, c, tt)
            elif st % 2 == 0:
                kv_phase(b, 0)
                kv_phase(b, 1)
            for l in range(2):
                if l == 0:
                    mixer_ab(st)
                else:
                    mixer_c(st)
                if stop_after == (l, "mix"):
                    done = True
                    break
                if st == 0 and l == 0:
                    kv_phase(b, 0)
                    kv_phase(b, 1)
                xattn(st, l)
                if stop_after == (l, "xa"):
                    done = True
                    break
                bnd = None
                if l == 1 and stop_after is None:
                    nxt = st < n_st - 1
                    bnd = ([(lambda st=st, c=c: store_tile(st, c, 0)) for c in range(DC)],
                           [(lambda st=st, c=c: load_tile(st + 1, c, 0)) for c in range(DC)] if nxt else [],
                           [(lambda st=st, c=c: store_tile(st, c, 1)) for c in range(DC)],
                           [(lambda st=st, c=c: load_tile(st + 1, c, 1)) for c in range(DC)] if nxt else [])
                ffn(st, l, bnd)
                if stop_after == (l, "ffn"):
                    done = True
                    break
            last = done or st == n_st - 1
            if last:
                flush()
                if stop_after is not None:
                    for tt in range(NTT):
                        for c in range(DC):
                            store_tile(st, c, tt)
                break
        S.emit(nc)
    return nc, len(S.recs)


def _bf16_round(a):
    u = np.ascontiguousarray(a, dtype=np.float32).view(np.uint32).astype(np.uint64)
    r = ((u + 0x7FFF + ((u >> 16) & 1)) >> 16) << 16
    return r.astype(np.uint32).view(np.float32)


def _pool_mats():
    out = np.zeros((P, 20, P), np.float64)
    t = np.arange(P)
    for g, w in enumerate(POOL_WINDOWS):
        cur = np.zeros((P, P))
        prev = np.zeros((P, P))
        first = np.zeros((P, P))
        for tc in range(P):
            for tp in range(max(0, tc - w + 1), tc + 1):
                cur[tp, tc] += 1.0 / w
            for d in range(tc - w + 1, 0):
                prev[P + d, tc] += 1.0 / w
            cnt = min(tc + 1, w)
            for tp in range(max(0, tc - w + 1), tc + 1):
                first[tp, tc] += 1.0 / cnt
        cur -= np.eye(P)
        first -= np.eye(P)
        out[:, g, :] = cur
        out[:, 4 + g, :] = prev
        hi = _bf16_round(first.astype(np.float32)).astype(np.float64)
        lo = _bf16_round((first - hi).astype(np.float32)).astype(np.float64)
        lo2 = _bf16_round((first - hi - lo).astype(np.float32)).astype(np.float64)
        out[:, 8 + g, :] = hi
        out[:, 12 + g, :] = lo
        out[:, 16 + g, :] = lo2
    return out.astype(np.float32)


def _cols(v, nchunk):
    return np.ascontiguousarray(np.asarray(v, np.float32).reshape(nchunk, P).T)


def _prep_shared(inp):
    pvh = np.zeros((P, NPV), np.float32)
    for l in range(2):
        for n, key in (("mix_g", "ln_mix_g"), ("mix_b", "ln_mix_b"), ("xa_g", "ln_xa_g"),
                       ("xa_b", "ln_xa_b"), ("ffn_g", "ln_ffn_g"), ("ffn_b", "ln_ffn_b")):
            c = PV[(n, l)]
            pvh[:, c:c + 8] = _cols(inp[key][l], 8)
        c = PV[("ffn_conv_w", l)]
        w = np.asarray(inp["ffn_conv_w"][l], np.float32)
        pvh[:, c:c + 66] = w.reshape(3, FC, P).transpose(2, 1, 0).reshape(P, 66)
        c = PV[("ffn_conv_b", l)]
        pvh[:, c:c + 22] = _cols(inp["ffn_conv_b"][l], FC)
    c = PV["pool_scale"]
    pvh[:, c:c + 4] = _cols(inp["pool_scale"][0], 4)
    c = PV["c_conv_w"]
    w = np.asarray(inp["c_conv_w"][0], np.float32)
    pvh[:, c:c + 24] = w.reshape(3, DC, P).transpose(2, 1, 0).reshape(P, 24)
    sgub = np.empty((P, 1024), np.float32)
    sgub[:, 0:512] = np.asarray(inp["sgu_ln_g"][0], np.float32)[None, :]
    sgub[:, 512:1024] = np.asarray(inp["sgu_ln_b"][0], np.float32)[None, :]
    shared = {
        "ab_w_in": np.ascontiguousarray(inp["ab_w_in"][0], dtype=np.float32),
        "ab_w_out": np.ascontiguousarray(inp["ab_w_out"][0], dtype=np.float32),
        "c_w_in": np.ascontiguousarray(inp["c_w_in"][0], dtype=np.float32),
        "c_w_out": np.ascontiguousarray(inp["c_w_out"][0], dtype=np.float32),
        "xa_wq": np.ascontiguousarray(inp["xa_wq"], dtype=np.float32),
        "xa_wkv": np.ascontiguousarray(inp["xa_wkv"], dtype=np.float32),
        "xa_wo": np.ascontiguousarray(inp["xa_wo"], dtype=np.float32),
        "ffn_w_up": np.ascontiguousarray(inp["ffn_w_up"], dtype=np.float32),
        "ffn_w_down": np.ascontiguousarray(inp["ffn_w_down"], dtype=np.float32),
        "pv": pvh,
        "sgub": sgub,
        "sguwT": np.ascontiguousarray(np.asarray(inp["sgu_w"][0], np.float32).transpose(2, 0, 1)),
        "sgubias": np.ascontiguousarray(np.asarray(inp["sgu_b"][0], np.float32).reshape(1, 512)),
        "poolmats": _pool_mats(),
        "poolw": np.ascontiguousarray(np.asarray(inp["pool_w"][0], np.float32).transpose(1, 0, 2)),
    }
    return shared


_CACHE = {}


def _get_program(n_st=4, stop_after=None):
    key = (n_st, stop_after)
    if key not in _CACHE:
        _CACHE[key] = build_program(n_st, stop_after)[0]
    return _CACHE[key]


def kernel(**inputs):
    inp = {k: np.asarray(v) for k, v in inputs.items()}
    n = 8
    shared = _prep_shared(inp)
    x = np.asarray(inp["x"], np.float32)
    mem = np.asarray(inp["mem"], np.float32)
    in_maps = []
    for i in range(n):
        xs = x[2 * i:2 * i + 2]
        xT = np.ascontiguousarray(xs.transpose(0, 2, 1)).reshape(NB_LOCAL, DC, P, SEQ)
        ms = mem[2 * i:2 * i + 2]
        mT = np.ascontiguousarray(ms.transpose(0, 2, 1)).reshape(NB_LOCAL, DC, P, MEM)
        d = dict(shared)
        d["xT"] = xT
        d["memT"] = mT
        in_maps.append(d)
    nc = _get_program()
    res = run_bass_kernel_spmd(nc, in_maps, core_ids=list(range(n)))
    outs = []
    for i in range(n):
        oT = np.asarray(res.results[i]["outT"]).reshape(NB_LOCAL, D, SEQ)
        outs.append(oT.transpose(0, 2, 1))
    return np.ascontiguousarray(np.concatenate(outs, axis=0), dtype=np.float32)
```

```python
import numpy as np
from contextlib import ExitStack
import concourse.bass as bass
import concourse.mybir as mybir
from concourse.bass_utils import run_bass_kernel_spmd

F32 = mybir.dt.float32
BF16 = mybir.dt.bfloat16
AF = mybir.ActivationFunctionType
ALU = mybir.AluOpType

ENGS = ("pe", "act", "dve", "pool", "sp")
NDMASEM = 16


class Tile:
    __slots__ = ("name", "lastw", "readers", "dreaders")

    def __init__(self, name=""):
        self.name = name
        self.lastw = None
        self.readers = {}
        self.dreaders = []


class Rec:
    __slots__ = ("eng", "fn", "deps", "dma", "signal", "sigval", "dmaidx")

    def __init__(self, eng, fn, deps, dma):
        self.eng = eng
        self.fn = fn
        self.deps = deps
        self.dma = dma
        self.signal = False
        self.sigval = 0
        self.dmaidx = -1


class Sched:
    def __init__(self):
        self.recs = []
        self.ndma = {e: 0 for e in ENGS}

    def op(self, eng, fn, reads=(), writes=(), dma=False):
        idx = len(self.recs)
        recs = self.recs
        deps = set()
        rawset = set()
        for t in reads:
            if t.lastw is not None:
                deps.add(t.lastw)
                rawset.add(t.lastw)
        for t in writes:
            if t.lastw is not None:
                deps.add(t.lastw)
            deps.update(t.readers.values())
            deps.update(t.dreaders)
        real = []
        for d in deps:
            r = recs[d]
            if r.eng == eng and not r.dma and not dma:
                if eng == "pe":
                    continue
                if d not in rawset:
                    continue
            real.append(d)
        rec = Rec(eng, fn, real, dma)
        if dma:
            rec.dmaidx = self.ndma[eng]
            self.ndma[eng] += 1
        recs.append(rec)
        for d in real:
            recs[d].signal = True
        for t in reads:
            if dma:
                t.dreaders.append(idx)
            else:
                t.readers[eng] = idx
        for t in writes:
            t.lastw = idx
            t.readers = {}
            t.dreaders = []
        return idx

    def emit(self, nc, final_dma_wait_eng="sp"):
        recs = self.recs
        cnt = {e: 0 for e in ENGS}
        for r in recs:
            if r.dma:
                r.signal = True
                continue
            if r.signal:
                cnt[r.eng] += 1
                r.sigval = cnt[r.eng]
        with ExitStack() as es:
            prog = {e: es.enter_context(nc.semaphore("prog_" + e)) for e in ENGS}
            dsem = {}
            for e in ENGS:
                if self.ndma[e] > 0:
                    dsem[e] = [es.enter_context(nc.semaphore("dma_%s_%d" % (e, i)))
                               for i in range(min(NDMASEM, self.ndma[e]))]
            block = es.enter_context(nc.Block())

            nsem = {e: len(dsem[e]) for e in dsem}
            know = {e: {x: 0 for x in ENGS} for e in ENGS}
            dwaited = {e: {} for e in ENGS}
            sigcount = {e: 0 for e in ENGS}
            vc = [None] * len(recs)
            plan = [None] * len(recs)
            nw_before = 0
            nw_after = 0
            for i, r in enumerate(recs):
                E = r.eng
                K = know[E]
                waits = {}
                merged = []
                seen_old = set()
                for d in r.deps:
                    rd = recs[d]
                    if rd.dma:
                        k = nsem[rd.eng]
                        key = ("d", rd.eng, rd.dmaidx % k)
                        val = 16 * (rd.dmaidx // k + 1)
                        if dwaited[E].get(key, 0) < val:
                            waits[key] = max(waits.get(key, 0), val)
                            merged.append(d)
                    else:
                        X = rd.eng
                        seen_old.add(X)
                        if K[X] < rd.sigval:
                            waits[("p", X)] = max(waits.get(("p", X), 0), rd.sigval)
                            merged.append(d)
                nw_before += len(seen_old)
                if r.dma:
                    k = nsem[E]
                    if r.dmaidx >= k:
                        key = ("d", E, r.dmaidx % k)
                        val = 16 * (r.dmaidx // k)
                        if dwaited[E].get(key, 0) < val:
                            waits[key] = max(waits.get(key, 0), val)
                for key, val in waits.items():
                    if key[0] == "d":
                        dwaited[E][key] = val
                    else:
                        nw_after += 1
                        if K[key[1]] < val:
                            K[key[1]] = val
                for d in merged:
                    vd = vc[d]
                    for x in ENGS:
                        if vd[x] > K[x]:
                            K[x] = vd[x]
                v = dict(K)
                if not r.dma:
                    if r.signal:
                        sigcount[E] = r.sigval
                    if sigcount[E] > v[E]:
                        v[E] = sigcount[E]
                vc[i] = v
                plan[i] = list(waits.items())

            def sem_of(key):
                if key[0] == "d":
                    return dsem[key[1]][key[2]]
                return prog[key[1]]

            def run(eng_name, e):
                for i, r in enumerate(recs):
                    if r.eng != eng_name:
                        continue
                    todo = [(sem_of(key), val) for key, val in plan[i]]
                    for sem, val in todo[:-1]:
                        e.wait_ge(sem, val)
                    ins = r.fn(e)
                    if todo:
                        ins._wait_ge(todo[-1][0], todo[-1][1])
                    if r.dma:
                        k = len(dsem[r.eng])
                        ins.then_inc(dsem[r.eng][r.dmaidx % k], 16)
                    elif r.signal:
                        ins.then_inc(prog[r.eng], 1)
                if eng_name == final_dma_wait_eng:
                    for en in ENGS:
                        n = self.ndma[en]
                        if n == 0:
                            continue
                        k = len(dsem[en])
                        for j in range(k):
                            uses = (n - j + k - 1) // k
                            if uses > 0:
                                e.wait_ge(dsem[en][j], 16 * uses)

            @block.tensor
            def _(e):
                run("pe", e)

            @block.scalar
            def _(e):
                run("act", e)

            @block.vector
            def _(e):
                run("dve", e)

            @block.gpsimd
            def _(e):
                run("pool", e)

            @block.sync
            def _(e):
                run("sp", e)


P = 128
D = 1024
DC = 8
SEQ = 2048
NB_LOCAL = 2
STW = 1024
TT = 512
NTT = 2
NBLK = 8
DFF = 2816
FC = 22
MEM = 256
ALPHA = float(4 ** 0.25)
EPS = 1e-5
NSLOT = 6
POOL_WINDOWS = (2, 4, 8, 16)

PV = {}
_c = 0
for _l in range(2):
    for _n in ("mix_g", "mix_b", "xa_g", "xa_b", "ffn_g", "ffn_b"):
        PV[(_n, _l)] = _c
        _c += 8
PV["pool_scale"] = _c
_c += 4
PV["c_conv_w"] = _c
_c += 24
for _l in range(2):
    PV[("ffn_conv_w", _l)] = _c
    _c += 66
for _l in range(2):
    PV[("ffn_conv_b", _l)] = _c
    _c += 22
NPV = _c


def build_program(n_st=4, stop_after=None):
    nc = bass.Bass("TRN2", target_bir_lowering=False)
    S = Sched()

    def dram_in(name, shape):
        return nc.dram_tensor(name, list(shape), F32, kind="ExternalInput").ap()

    xT_d = dram_in("xT", [NB_LOCAL, DC, P, SEQ])
    memT_d = dram_in("memT", [NB_LOCAL, DC, P, MEM])
    ab_w_in = dram_in("ab_w_in", [D, 1536])
    ab_w_out = dram_in("ab_w_out", [D, D])
    c_w_in = dram_in("c_w_in", [D, 3072])
    c_w_out = dram_in("c_w_out", [D, D])
    xa_wq = dram_in("xa_wq", [2, D, D])
    xa_wkv = dram_in("xa_wkv", [2, D, 2 * D])
    xa_wo = dram_in("xa_wo", [2, D, D])
    ffn_w_up = dram_in("ffn_w_up", [2, D, 2 * DFF])
    ffn_w_down = dram_in("ffn_w_down", [2, DFF, D])
    pv_d = dram_in("pv", [P, NPV])
    sgub_d = dram_in("sgub", [P, 1024])
    sguwT_d = dram_in("sguwT", [P, 4, P])
    sgubias_d = dram_in("sgubias", [1, 512])
    poolmats_d = dram_in("poolmats", [P, 20, P])
    poolw_d = dram_in("poolw", [P, 4, P])
    outT_d = nc.dram_tensor("outT", [NB_LOCAL, DC, P, SEQ], F32, kind="ExternalOutput").ap()

    with ExitStack() as es:
        def sb(name, shape, dt):
            return es.enter_context(nc.sbuf_tensor("sb_" + name, list(shape), dt))

        xT32 = sb("xT32", [P, DC, STW], F32)
        xTb = sb("xTb", [P, DC, STW], BF16)
        NAR = 44
        arena = sb("arena", [P, NAR, TT], BF16)
        wring = [sb("wslot%d" % i, [P, 4096], BF16) for i in range(NSLOT)]
        kT = [sb("kT%d" % l, [P, DC, MEM], BF16) for l in range(2)]
        vtok = [sb("vtok%d" % l, [P, 2, D], BF16) for l in range(2)]
        pv = sb("pv", [P, NPV], F32)
        sgub = sb("sgub", [P, 1024], F32)
        WmT = sb("WmT", [P, 4, P], BF16)
        bs_hi = sb("bs_hi", [1, 512], BF16)
        bs_lo = sb("bs_lo", [1, 512], BF16)
        poolmats = sb("poolmats", [P, 20, P], BF16)
        poolw = sb("poolw", [P, 4, P], BF16)
        onesm = sb("onesm", [P, P], BF16)
        ones1 = sb("ones1", [P, P], BF16)
        neghalf = sb("neghalf", [P, 8], F32)
        NF = 6
        ftmp = [sb("ftmp%d" % i, [P, TT], F32) for i in range(NF)]
        bs32 = ftmp[0][0:1, :]
        bsh32 = ftmp[1][0:1, :]
        rstd_b = [sb("rstd%d" % i, [P, TT], F32) for i in range(2)]
        nmr_b = [sb("nmr%d" % i, [P, TT], F32) for i in range(2)]
        rsq_b = [sb("rsq%d" % i, [P, TT], BF16) for i in range(3)]
        gbuf = [sb("gbuf%d" % i, [P, TT + 2], F32) for i in range(4)]
        halo_c = sb("halo_c", [P, DC, 2], F32)
        halo_f = [sb("halo_f%d" % l, [P, FC, 2], F32) for l in range(2)]
        small = sb("small", [P, 64], F32)
        xbt = sb("xbt", [P, 4, TT], BF16)
        psum = [es.enter_context(nc.psum_tensor("ps%d" % i, [P, TT], F32)) for i in range(8)]

        t_xT32 = [[Tile("x32_%d_%d" % (c, t)) for t in range(NTT)] for c in range(DC)]
        t_xTb = [[Tile("xb_%d_%d" % (c, t)) for t in range(NTT)] for c in range(DC)]
        t_ar = [Tile("ar%d" % i) for i in range(NAR)]
        t_ws = [Tile("ws%d" % i) for i in range(NSLOT)]
        t_kT = [Tile("kT%d" % l) for l in range(2)]
        t_vt = [Tile("vt%d" % l) for l in range(2)]
        t_const = Tile("const")
        t_ft = [Tile("ft%d" % i) for i in range(NF)]
        t_rstd = [Tile("rstd%d" % i) for i in range(2)]
        t_nmr = [Tile("nmr%d" % i) for i in range(2)]
        t_rsq = [Tile("rsq%d" % i) for i in range(3)]
        t_gbuf = [Tile("gbuf%d" % i) for i in range(4)]
        t_halo_c = Tile("halo_c")
        t_halo_f = [Tile("halo_f%d" % l) for l in range(2)]
        t_small = Tile("small")
        t_xbt = [Tile("xbt%d" % i) for i in range(4)]
        t_ps = [Tile("ps%d" % i) for i in range(8)]
        t_out = Tile("out")

        def x32(c, tt):
            return xT32[:, c, tt * TT:(tt + 1) * TT]

        def xb(c, tt):
            return xTb[:, c, tt * TT:(tt + 1) * TT]

        def ar(i):
            return arena[:, i, :]

        ps_state = {"free": list(range(8)), "i": 0}

        def ps_next():
            fl = ps_state["free"]
            b = fl[ps_state["i"] % len(fl)]
            ps_state["i"] += 1
            return b

        ft_state = {"i": 0}

        def ft_next():
            i = ft_state["i"] % NF
            ft_state["i"] += 1
            return i

        wplan = []

        def wview(W, c0, n):
            return W.rearrange("(k p) n -> p k n", p=P)[:, :, c0:c0 + n]

        def plan_kv(l):
            for i in range(4):
                wplan.append((wview(xa_wkv[l], i * 512, 512), 8, 512))

        def plan_layer(l, kv_after_mix=False):
            if l == 0:
                for i in range(3):
                    wplan.append((wview(ab_w_in, i * 512, 512), 8, 512))
                for i in range(2):
                    wplan.append((wview(ab_w_out, i * 512, 512), 8, 512))
                if kv_after_mix:
                    plan_kv(0)
                    plan_kv(1)
            else:
                for q in range(2):
                    for part in (1, 2, 0):
                        wplan.append((wview(c_w_in, part * 1024 + q * 512, 512), 8, 512))
                for i in range(2):
                    wplan.append((wview(c_w_out, i * 512, 512), 8, 512))
            for i in range(2):
                wplan.append((wview(xa_wq[l], i * 512, 512), 8, 512))
            for i in range(2):
                wplan.append((wview(xa_wo[l], i * 512, 512), 8, 512))
            for q in range(6):
                n = 512 if q < 5 else 256
                wplan.append((wview(ffn_w_up[l], DFF + q * 512, n), 8, n))
                wplan.append((wview(ffn_w_up[l], q * 512, n), 8, n))
            for m in range(8):
                wplan.append((wview(ffn_w_down[l], m * 128, 128), FC, 128))

        for st in range(n_st):
            if st % 2 == 0 and st > 0:
                plan_kv(0)
                plan_kv(1)
            plan_layer(0, st == 0)
            plan_layer(1)

        wstate = {"issued": 0, "next": 0, "done": 0}

        def w_pump():
            lim = min(len(wplan), wstate["done"] + NSLOT)
            while wstate["issued"] < lim:
                j = wstate["issued"]
                src, kc, n = wplan[j]
                s = j % NSLOT
                dst = wring[s][:, 0:kc * n].rearrange("p (k n) -> p k n", k=kc)
                S.op("pool", lambda e, dst=dst, src=src: e.dma_start(out=dst, in_=src),
                     writes=[t_ws[s]], dma=True)
                wstate["issued"] += 1

        def w_release(upto):
            if upto > wstate["done"]:
                wstate["done"] = upto
            w_pump()

        def w_next(kc, n, release_prior=True):
            i = wstate["next"]
            wstate["next"] += 1
            assert wplan[i][1] == kc and wplan[i][2] == n, (i, wplan[i][1:], kc, n)
            if release_prior:
                w_release(i)
            else:
                w_pump()
            assert wstate["issued"] > i, "too many live weight slots"
            s = i % NSLOT
            view = wring[s][:, 0:kc * n].rearrange("p (k n) -> p k n", k=kc)
            return view, t_ws[s]

        S.op("sp", lambda e: e.dma_start(out=pv[:], in_=pv_d), writes=[t_const], dma=True)
        S.op("sp", lambda e: e.dma_start(out=sgub[:], in_=sgub_d), writes=[t_const], dma=True)
        S.op("sp", lambda e: e.dma_start(out=bs32, in_=sgubias_d), writes=[t_ft[0]], dma=True)
        t_c2 = Tile("const2")
        S.op("pool", lambda e: e.dma_start(out=WmT[:], in_=sguwT_d), writes=[t_c2], dma=True)
        t_c3 = Tile("const3")
        S.op("pool", lambda e: e.dma_start(out=poolmats[:], in_=poolmats_d), writes=[t_c3], dma=True)
        t_c4 = Tile("const4")
        S.op("pool", lambda e: e.dma_start(out=poolw[:], in_=poolw_d), writes=[t_c4], dma=True)
        S.op("dve", lambda e: e.memset(WmT[64:128, :, 0:64], 0.0), writes=[t_c2])
        t_c5 = Tile("const5")
        S.op("dve", lambda e: e.memset(onesm[:], 1.0 / 1024.0), writes=[t_c5])
        S.op("dve", lambda e: e.memset(ones1[:], 1.0), writes=[t_c5])
        S.op("dve", lambda e: e.memset(neghalf[:], -0.5), writes=[t_c5])
        t_bs = Tile("bs")
        S.op("dve", lambda e: e.tensor_copy(bs_hi[:], bs32), reads=[t_ft[0]], writes=[t_bs])
        S.op("dve", lambda e: e.tensor_copy(bsh32, bs_hi[:]), reads=[t_bs], writes=[t_ft[1]])
        S.op("dve", lambda e: e.tensor_tensor(bsh32, bs32, bsh32, ALU.subtract),
             reads=[t_ft[0], t_ft[1]], writes=[t_ft[1]])
        S.op("dve", lambda e: e.tensor_copy(bs_lo[:], bsh32), reads=[t_ft[1]], writes=[t_bs])
        t_consts_all = [t_const, t_c2, t_c3, t_c4, t_c5, t_bs]

        def pvc(key, i=0, n=1):
            c = PV[key] + i
            return pv[:, c:c + n]

        from collections import deque
        deferred = deque()

        def pump(k=1):
            for _ in range(k):
                if not deferred:
                    return
                deferred.popleft()[1]()

        def pump_crit():
            while deferred and deferred[0][0]:
                deferred.popleft()[1]()

        def flush():
            while deferred:
                deferred.popleft()[1]()

        def mm_unit(wv, wt, coff, KC, in_ap, in_tile, tt):
            b = ps_next()
            for k in range(KC):
                S.op("pe", lambda e, b=b, k=k: e.matmul(
                    psum[b][:], wv[:, k, coff:coff + P], in_ap(k, tt),
                    start=(k == 0), stop=(k == KC - 1)),
                    reads=[wt, in_tile(k, tt)], writes=[t_ps[b]])
            return b

        def proj_ln(groups, fetch, KC, in_ap, in_tile, gkey, bkey, l, boundary=None):
            saved_free = ps_state["free"]
            nfl = len(saved_free)
            order = [saved_free[(ps_state["i"] + j) % nfl] for j in range(nfl)]
            assert nfl == 8
            stat = [order[4], order[6], order[5], order[7]]
            ps_state["free"] = order[0:4]
            ps_state["i"] = 0
            nseen = [0, 0]

            def stats(m, tt, r, first, last):
                S.op("pe", lambda e: e.matmul(
                    psum[stat[tt]][:], onesm[:], xb(m, tt), start=first, stop=last),
                    reads=[t_c5, t_xTb[m][tt]], writes=[t_ps[stat[tt]]])
                S.op("pe", lambda e: e.matmul(
                    psum[stat[2 + tt]][:], onesm[:], rsq_b[r][:], start=first, stop=last),
                    reads=[t_c5, t_rsq[r]], writes=[t_ps[stat[2 + tt]]])

            def finalize_items(tt, after_head=None, per_chunk=None, after_all=None):
                items = []

                def head():
                    f0 = ft_next()
                    S.op("act", lambda e: e.activation(ftmp[f0][:], psum[stat[tt]][:], AF.Square),
                         reads=[t_ps[stat[tt]]], writes=[t_ft[f0]])
                    S.op("dve", lambda e: e.scalar_tensor_tensor(
                        ftmp[f0][:], psum[stat[2 + tt]][:], EPS, ftmp[f0][:], ALU.add, ALU.subtract),
                        reads=[t_ps[stat[2 + tt]], t_ft[f0]], writes=[t_ft[f0]])
                    S.op("act", lambda e: e.activation(ftmp[f0][:], ftmp[f0][:], AF.Ln),
                         reads=[t_ft[f0]], writes=[t_ft[f0]])
                    S.op("act", lambda e: e.activation(rstd_b[tt][:], ftmp[f0][:], AF.Exp, scale=-0.5),
                         reads=[t_ft[f0]], writes=[t_rstd[tt]])
                    S.op("dve", lambda e: e.scalar_tensor_tensor(
                        nmr_b[tt][:], psum[stat[tt]][:], -1.0, rstd_b[tt][:], ALU.mult, ALU.mult),
                        reads=[t_ps[stat[tt]], t_rstd[tt]], writes=[t_nmr[tt]])
                items.append((True, head))
                if after_head is not None:
                    items.append((True, after_head))
                late = []
                for m in range(DC):
                    def app(m=m):
                        eng = "dve"
                        S.op(eng, lambda e: e.tensor_tensor(
                            x32(m, tt), x32(m, tt), rstd_b[tt][:], ALU.mult),
                            reads=[t_xT32[m][tt], t_rstd[tt]], writes=[t_xT32[m][tt]])
                        S.op(eng, lambda e: e.tensor_tensor(
                            x32(m, tt), x32(m, tt), nmr_b[tt][:], ALU.add),
                            reads=[t_xT32[m][tt], t_nmr[tt]], writes=[t_xT32[m][tt]])
                        if per_chunk is None:
                            S.op("act", lambda e: e.activation(
                                xb(m, tt), x32(m, tt), AF.Identity,
                                bias=pvc((bkey, l), m), scale=pvc((gkey, l), m)),
                                reads=[t_xT32[m][tt], t_const], writes=[t_xTb[m][tt]])
                    items.append((True, app))

                    def aff(m=m):
                        if m in (3, 6, 7):
                            S.op("act", lambda e: e.activation(
                                x32(m, tt), x32(m, tt), AF.Identity,
                                bias=pvc((bkey, l), m), scale=pvc((gkey, l), m)),
                                reads=[t_xT32[m][tt], t_const], writes=[t_xT32[m][tt]])
                        else:
                            S.op("dve", lambda e: e.tensor_scalar(
                                x32(m, tt), x32(m, tt), pvc((gkey, l), m), pvc((bkey, l), m),
                                ALU.mult, ALU.add),
                                reads=[t_xT32[m][tt], t_const], writes=[t_xT32[m][tt]])
                    if per_chunk is not None:
                        items.append((True, aff))
                        items.append((True, per_chunk[m]))
                    else:
                        late.append((False, aff))
                items.extend(late)
                if after_all is not None:
                    items.extend((True, f) for f in after_all)
                return items

            STAT_LAG = 2
            pendq = deque()

            def emit_stats(p):
                stats(*p)
                if p[4] and p[1] == 0:
                    if boundary is not None:
                        deferred.extend(finalize_items(0, None, boundary[0], boundary[1]))
                    else:
                        deferred.extend(finalize_items(0))

            ng = len(groups)
            for gi, group in enumerate(groups):
                ws = fetch(group)
                for tt in range(NTT):
                    for m in group:
                        wv, wt, coff = ws[m]
                        b = mm_unit(wv, wt, coff, KC, in_ap, in_tile, tt)
                        S.op("dve", lambda e, m=m, tt=tt, b=b: e.scalar_tensor_tensor(
                            x32(m, tt), x32(m, tt), ALPHA, psum[b][:], ALU.mult, ALU.add),
                            reads=[t_xT32[m][tt], t_ps[b]], writes=[t_xT32[m][tt]])
                        S.op("act", lambda e, m=m, tt=tt: e.activation(xb(m, tt), x32(m, tt), AF.Copy),
                             reads=[t_xT32[m][tt]], writes=[t_xTb[m][tt]])
                        r = rsq_state["i"] % 3
                        rsq_state["i"] += 1
                        S.op("act", lambda e, m=m, tt=tt, r=r: e.activation(rsq_b[r][:], x32(m, tt), AF.Square),
                             reads=[t_xT32[m][tt]], writes=[t_rsq[r]])
                        first = nseen[tt] == 0
                        nseen[tt] += 1
                        last = nseen[tt] == DC
                        pendq.append((m, tt, r, first, last))
                        while len(pendq) > STAT_LAG:
                            emit_stats(pendq.popleft())
                        if boundary is not None:
                            pump_crit()
                        if len(group) >= 8:
                            pump(2 if m == group[0] else 1)
                        else:
                            pump(3)
                    if gi == ng - 1 and tt == 0:
                        flush()
            flush()
            while pendq:
                deferred.append((True, lambda p=pendq.popleft(): stats(*p)))

            def restore():
                ps_state["free"] = saved_free
            if boundary is not None:
                deferred.extend(finalize_items(1, restore, boundary[2], boundary[3]))
            else:
                deferred.extend(finalize_items(1, restore))

        def fetch_2x512(group_sizes=(8,)):
            def fetch(group):
                w0 = w_next(8, 512)
                w1 = w_next(8, 512, False)
                out = {}
                for m in group:
                    wv, wt = (w0, w1)[m // 4]
                    out[m] = (wv, wt, (m % 4) * P)
                return out
            return fetch

        def x_in_ap(k, tt):
            return xb(k, tt)

        def x_in_tile(k, tt):
            return t_xTb[k][tt]

        ALLM = [list(range(DC))]

        def mixer_ab(st):
            WA, tA = w_next(8, 512)
            iWA = wstate["next"] - 1
            WB, tB = w_next(8, 512, False)
            WC, tC = w_next(8, 512, False)
            seq_first_st = (st % 2 == 0)

            def uT(g, tt):
                return g * 2 + tt

            def yT(c, tt):
                return 8 + c * 2 + tt

            def pooledT(g, tt):
                return 24 + g * 2 + tt

            def u_units(tt):
                for m in range(4):
                    b = mm_unit(WA, tA, m * P, 8, x_in_ap, x_in_tile, tt)
                    i = uT(m, tt)
                    S.op("act", lambda e, b=b, i=i: e.activation(ar(i), psum[b][:], AF.Gelu_apprx_tanh),
                         reads=[t_ps[b]], writes=[t_ar[i]])
                    pump(2)

            binfo = {}
            p2_pending = {}

            def ensure_p2(j):
                f = p2_pending.pop(j, None)
                if f is not None:
                    f()

            def vxb(j):
                tt = j // 4
                gj = st * NBLK + j
                bv = ps_next()
                bx = ps_next()
                for k in range(8):
                    S.op("pe", lambda e, k=k: e.matmul(
                        psum[bv][:], xTb[:, k, j * P:(j + 1) * P], WB[:, k, :], start=(k == 0), stop=(k == 7)),
                        reads=[t_xTb[k][tt], tB], writes=[t_ps[bv]])
                for k in range(8):
                    S.op("pe", lambda e, k=k: e.matmul(
                        psum[bx][:], xTb[:, k, j * P:(j + 1) * P], WC[:, k, :], start=(k == 0), stop=(k == 7)),
                        reads=[t_xTb[k][tt], tC], writes=[t_ps[bx]])
                pump_crit()
                fv = ft_next()
                S.op("act", lambda e: e.activation(ftmp[fv][:], psum[bv][:], AF.Gelu_apprx_tanh),
                     reads=[t_ps[bv]], writes=[t_ft[fv]])
                ixb = gj % 4
                S.op("act", lambda e: e.activation(xbt[:, ixb, :], psum[bx][:], AF.Copy),
                     reads=[t_ps[bx]], writes=[t_xbt[ixb]])
                so = (gj % 4) * 16
                S.op("dve", lambda e: e.bn_stats(small[:, so:so + 6], ftmp[fv][:]),
                     reads=[t_ft[fv]], writes=[t_small])
                S.op("dve", lambda e: e.bn_aggr(small[:, so + 6:so + 8], small[:, so:so + 6]),
                     reads=[t_small], writes=[t_small])
                S.op("dve", lambda e: e.tensor_scalar(
                    small[:, so + 8:so + 9], small[:, so + 7:so + 8], EPS, None, ALU.add),
                    reads=[t_small], writes=[t_small])
                S.op("pool", lambda e: e.tensor_tensor(
                    small[:, so + 9:so + 10], small[:, so + 8:so + 9], neghalf[:, 0:1], ALU.pow),
                    reads=[t_small, t_c5], writes=[t_small])
                ivl = 32 + gj % 4

                def part2():
                    S.op("dve", lambda e: e.tensor_scalar(
                        ftmp[fv][:], ftmp[fv][:], small[:, so + 6:so + 7], small[:, so + 9:so + 10],
                        ALU.subtract, ALU.mult),
                        reads=[t_ft[fv], t_small], writes=[t_ft[fv]])
                    S.op("dve", lambda e: e.tensor_tensor(ftmp[fv][:], ftmp[fv][:], sgub[:, 0:512], ALU.mult),
                         reads=[t_ft[fv], t_const], writes=[t_ft[fv]])
                    S.op("dve", lambda e: e.tensor_tensor(
                        ar(ivl), ftmp[fv][:], sgub[:, 512:1024], ALU.add),
                        reads=[t_ft[fv], t_const], writes=[t_ar[ivl]])
                p2_pending[j] = part2
                ensure_p2(j - 1)
                binfo[j] = (ivl, ixb, (gj - 1) % 4)
                pump(2)

            def sgu_pool(j):
                tt = j // 4
                col = (j % 4) * P
                ensure_p2(j)
                ivl, ixb, ixp = binfo[j]
                bs_ = ps_next()
                for g in range(4):
                    S.op("pe", lambda e, g=g: e.matmul(
                        psum[bs_][:, g * P:(g + 1) * P], arena[:, ivl, g * P:(g + 1) * P], WmT[:, g, :],
                        start=True, stop=False),
                        reads=[t_ar[ivl], t_c2], writes=[t_ps[bs_]])
                    S.op("pe", lambda e, g=g: e.matmul(
                        psum[bs_][:, g * P:(g + 1) * P], ones1[0:1, :], bs_hi[0:1, g * P:(g + 1) * P],
                        start=False, stop=False),
                        reads=[t_c5, t_bs], writes=[t_ps[bs_]])
                    S.op("pe", lambda e, g=g: e.matmul(
                        psum[bs_][:, g * P:(g + 1) * P], ones1[0:1, :], bs_lo[0:1, g * P:(g + 1) * P],
                        start=False, stop=True),
                        reads=[t_c5, t_bs], writes=[t_ps[bs_]])
                bp = ps_next()
                first = seq_first_st and j == 0
                for g in range(4):
                    if first:
                        S.op("pe", lambda e, g=g: e.matmul(
                            psum[bp][:, g * P:(g + 1) * P], xbt[:, ixb, g * P:(g + 1) * P], poolmats[:, 8 + g, :],
                            start=True, stop=False),
                            reads=[t_xbt[ixb], t_c3], writes=[t_ps[bp]])
                        S.op("pe", lambda e, g=g: e.matmul(
                            psum[bp][:, g * P:(g + 1) * P], xbt[:, ixb, g * P:(g + 1) * P], poolmats[:, 12 + g, :],
                            start=False, stop=False),
                            reads=[t_xbt[ixb], t_c3], writes=[t_ps[bp]])
                        S.op("pe", lambda e, g=g: e.matmul(
                            psum[bp][:, g * P:(g + 1) * P], xbt[:, ixb, g * P:(g + 1) * P], poolmats[:, 16 + g, :],
                            start=False, stop=True),
                            reads=[t_xbt[ixb], t_c3], writes=[t_ps[bp]])
                    else:
                        S.op("pe", lambda e, g=g: e.matmul(
                            psum[bp][:, g * P:(g + 1) * P], xbt[:, ixb, g * P:(g + 1) * P], poolmats[:, g, :],
                            start=True, stop=False),
                            reads=[t_xbt[ixb], t_c3], writes=[t_ps[bp]])
                        S.op("pe", lambda e, g=g: e.matmul(
                            psum[bp][:, g * P:(g + 1) * P], xbt[:, ixp, g * P:(g + 1) * P], poolmats[:, 4 + g, :],
                            start=False, stop=True),
                            reads=[t_xbt[ixp], t_c3], writes=[t_ps[bp]])
                for g in range(4):
                    iy = yT(g, tt)
                    iu = uT(g, tt)
                    S.op("dve", lambda e, g=g, iy=iy, iu=iu: e.tensor_tensor(
                        arena[:, iy, col:col + P], psum[bs_][:, g * P:(g + 1) * P], arena[:, iu, col:col + P],
                        ALU.mult),
                        reads=[t_ps[bs_], t_ar[iu]], writes=[t_ar[iy]])
                for g in range(4):
                    ip = pooledT(g, tt)
                    S.op("act", lambda e, g=g, ip=ip: e.activation(
                        arena[:, ip, col:col + P], psum[bp][:, g * P:(g + 1) * P], AF.Copy),
                        reads=[t_ps[bp]], writes=[t_ar[ip]])
                pump(1)

            def yb(tt):
                for g in range(4):
                    b = ps_next()
                    ip = pooledT(g, tt)
                    S.op("pe", lambda e, g=g, b=b, ip=ip: e.matmul(
                        psum[b][:], poolw[:, g, :], ar(ip), start=True, stop=True),
                        reads=[t_c4, t_ar[ip]], writes=[t_ps[b]])
                    iy = yT(4 + g, tt)
                    S.op("act", lambda e, g=g, b=b, iy=iy: e.activation(
                        ar(iy), psum[b][:], AF.Identity, scale=pvc("pool_scale", g)),
                        reads=[t_ps[b], t_const], writes=[t_ar[iy]])

            vxb(0)
            vxb(1)
            u_units(0)
            vxb(2)
            sgu_pool(0)
            vxb(3)
            sgu_pool(1)
            flush()
            vxb(4)
            sgu_pool(2)
            vxb(5)
            sgu_pool(3)
            u_units(1)
            yb(0)
            vxb(6)
            sgu_pool(4)
            vxb(7)
            sgu_pool(5)
            sgu_pool(6)
            sgu_pool(7)
            yb(1)
            w_release(iWA + 3)
            flush()
            proj_ln(ALLM, fetch_2x512(), 8, lambda k, tt: ar(yT(k, tt)), lambda k, tt: t_ar[yT(k, tt)],
                    "mix_g", "mix_b", 0)

        def conv3(gb_i, out_f, wcol):
            S.op("dve", lambda e: e.tensor_scalar(
                ftmp[out_f][:], gbuf[gb_i][:, 0:TT], pv[:, wcol:wcol + 1], None, ALU.mult),
                reads=[t_gbuf[gb_i], t_const], writes=[t_ft[out_f]])
            S.op("dve", lambda e: e.scalar_tensor_tensor(
                ftmp[out_f][:], gbuf[gb_i][:, 1:TT + 1], pv[:, wcol + 1:wcol + 2], ftmp[out_f][:],
                ALU.mult, ALU.add),
                reads=[t_gbuf[gb_i], t_const, t_ft[out_f]], writes=[t_ft[out_f]])
            S.op("dve", lambda e: e.scalar_tensor_tensor(
                ftmp[out_f][:], gbuf[gb_i][:, 2:TT + 2], pv[:, wcol + 2:wcol + 3], ftmp[out_f][:],
                ALU.mult, ALU.add),
                reads=[t_gbuf[gb_i], t_const, t_ft[out_f]], writes=[t_ft[out_f]])

        def halo_io(gi, halo_ap, halo_tile):
            S.op("pool", lambda e: e.tensor_copy(gbuf[gi][:, 0:2], halo_ap),
                 reads=[halo_tile], writes=[t_gbuf[gi]])
            S.op("pool", lambda e: e.tensor_copy(halo_ap, gbuf[gi][:, TT:TT + 2]),
                 reads=[t_gbuf[gi]], writes=[halo_tile])

        gb_state = {"i": 0}
        rsq_state = {"i": 0}

        def gb_next():
            i = gb_state["i"] % 4
            gb_state["i"] += 1
            return i

        def mixer_c(st):
            def yT(c, tt):
                return c * 2 + tt
            if st % 2 == 0:
                S.op("pool", lambda e: e.memset(halo_c[:], 0.0), writes=[t_halo_c])
            for q in range(2):
                wc_ = w_next(8, 512)
                wh_ = w_next(8, 512, False)
                wb_ = w_next(8, 512, False)
                for tt in range(NTT):
                    if q == 0 and tt == 1:
                        flush()
                    for m in range(q * 4, q * 4 + 4):
                        coff = (m % 4) * P
                        bc = mm_unit(wc_[0], wc_[1], coff, 8, x_in_ap, x_in_tile, tt)
                        bh = mm_unit(wh_[0], wh_[1], coff, 8, x_in_ap, x_in_tile, tt)
                        pump_crit()
                        fc_ = ft_next()
                        S.op("act", lambda e, fc_=fc_, bc=bc: e.activation(ftmp[fc_][:], psum[bc][:], AF.Copy),
                             reads=[t_ps[bc]], writes=[t_ft[fc_]])
                        gi = gb_next()
                        S.op("dve", lambda e, fc_=fc_, bh=bh, gi=gi: e.tensor_tensor(
                            gbuf[gi][:, 2:TT + 2], ftmp[fc_][:], psum[bh][:], ALU.mult),
                            reads=[t_ft[fc_], t_ps[bh]], writes=[t_gbuf[gi]])
                        halo_io(gi, halo_c[:, m, :], t_halo_c)
                        fo = ft_next()
                        conv3(gi, fo, PV["c_conv_w"] + m * 3)
                        bb = mm_unit(wb_[0], wb_[1], coff, 8, x_in_ap, x_in_tile, tt)
                        iy = yT(m, tt)
                        S.op("dve", lambda e, bb=bb, fo=fo, iy=iy: e.tensor_tensor(
                            ar(iy), ftmp[fo][:], psum[bb][:], ALU.mult),
                            reads=[t_ft[fo], t_ps[bb]], writes=[t_ar[iy]])
                        pump(3)
            flush()
            proj_ln(ALLM, fetch_2x512(), 8, lambda k, tt: ar(yT(k, tt)), lambda k, tt: t_ar[yT(k, tt)],
                    "mix_g", "mix_b", 1)

        def kv_phase(b, l):
            mT = arena[:, 40:44, :].rearrange("p a (c m) -> p (a c) m", m=MEM)
            t_m = t_ar[40:44]
            if l == 0:
                S.op("pool", lambda e: e.dma_start(out=mT, in_=memT_d[b].rearrange("c p m -> p c m")),
                     writes=list(t_m), dma=True)
            ws = [w_next(8, 512, i == 0) for i in range(4)]
            for c in range(DC):
                wv, wt = ws[c // 4]
                bk = ps_next()
                for k in range(8):
                    S.op("pe", lambda e, k=k, bk=bk, wv=wv, c=c: e.matmul(
                        psum[bk][:, 0:MEM], wv[:, k, (c % 4) * P:(c % 4 + 1) * P], mT[:, k, :],
                        start=(k == 0), stop=(k == 7)),
                        reads=[wt] + list(t_m), writes=[t_ps[bk]])
                S.op("act", lambda e, bk=bk, c=c: e.activation(kT[l][:, c, :], psum[bk][:, 0:MEM], AF.Copy),
                     reads=[t_ps[bk]], writes=[t_kT[l]])
            for mc in range(2):
                for n in range(2):
                    wv, wt = ws[2 + n]
                    bk = ps_next()
                    for k in range(8):
                        S.op("pe", lambda e, k=k, bk=bk, wv=wv, mc=mc: e.matmul(
                            psum[bk][:], mT[:, k, mc * P:(mc + 1) * P], wv[:, k, :],
                            start=(k == 0), stop=(k == 7)),
                            reads=[wt] + list(t_m), writes=[t_ps[bk]])
                    S.op("dve", lambda e, bk=bk, mc=mc, n=n: e.tensor_copy(
                        vtok[l][:, mc, n * TT:(n + 1) * TT], psum[bk][:]),
                        reads=[t_ps[bk]], writes=[t_vt[l]])

        def xattn(st, l):
            def qT(c, tt):
                return c * 2 + tt

            def pT(h, mc):
                return 16 + h * 2 + mc

            def aoT(c, tt):
                return 24 + c * 2 + tt
            wq0 = w_next(8, 512)
            wq1 = w_next(8, 512, False)

            def q_unit(m, tt):
                wv, wt = (wq0, wq1)[m // 4]
                b = mm_unit(wv, wt, (m % 4) * P, 8, x_in_ap, x_in_tile, tt)
                pump_crit()
                i = qT(m, tt)
                if m % 2 == 0:
                    S.op("act", lambda e: e.activation(ar(i), psum[b][:], AF.Identity, scale=1.0 / 16.0),
                         reads=[t_ps[b]], writes=[t_ar[i]])
                else:
                    S.op("dve", lambda e: e.tensor_scalar(
                        ar(i), psum[b][:], 1.0 / 16.0, None, ALU.mult),
                        reads=[t_ps[b]], writes=[t_ar[i]])

            def scores(h, tt):
                for mc in range(2):
                    b = ps_next()
                    for kc in range(2):
                        iq = qT(2 * h + kc, tt)
                        S.op("pe", lambda e, b=b, kc=kc, mc=mc, iq=iq: e.matmul(
                            psum[b][:], kT[l][:, 2 * h + kc, mc * P:(mc + 1) * P], ar(iq),
                            start=(kc == 0), stop=(kc == 1)),
                            reads=[t_kT[l], t_ar[iq]], writes=[t_ps[b]])
                    ip = pT(h, mc)
                    S.op("act", lambda e, b=b, ip=ip: e.activation(ar(ip), psum[b][:], AF.Exp),
                         reads=[t_ps[b]], writes=[t_ar[ip]])

            def den_pv(h, tt):
                bd = ps_next()
                for mc in range(2):
                    ip = pT(h, mc)
                    S.op("pe", lambda e, mc=mc, ip=ip: e.matmul(
                        psum[bd][:], ones1[:], ar(ip), start=(mc == 0), stop=(mc == 1)),
                        reads=[t_c5, t_ar[ip]], writes=[t_ps[bd]])
                fr = ft_next()
                S.op("act", lambda e: e.activation(ftmp[fr][:], psum[bd][:], AF.Ln),
                     reads=[t_ps[bd]], writes=[t_ft[fr]])
                S.op("act", lambda e: e.activation(ftmp[fr][:], ftmp[fr][:], AF.Exp, scale=-1.0),
                     reads=[t_ft[fr]], writes=[t_ft[fr]])
                for dc in range(2):
                    bo = ps_next()
                    for mc in range(2):
                        ip = pT(h, mc)
                        S.op("pe", lambda e, bo=bo, mc=mc, ip=ip, dc=dc: e.matmul(
                            psum[bo][:], vtok[l][:, mc, (2 * h + dc) * P:(2 * h + dc + 1) * P], ar(ip),
                            start=(mc == 0), stop=(mc == 1)),
                            reads=[t_vt[l], t_ar[ip]], writes=[t_ps[bo]])
                    io = aoT(2 * h + dc, tt)
                    S.op("dve", lambda e, bo=bo, io=io: e.tensor_tensor(
                        ar(io), psum[bo][:], ftmp[fr][:], ALU.mult),
                        reads=[t_ps[bo], t_ft[fr]], writes=[t_ar[io]])

            for m in range(DC):
                q_unit(m, 0)
                pump(2)
            scores(0, 0)
            scores(1, 0)
            flush()
            q_unit(0, 1)
            q_unit(1, 1)
            for h in range(4):
                if h + 2 < 4:
                    scores(h + 2, 0)
                q_unit(2 + h, 1)
                den_pv(h, 0)
            q_unit(6, 1)
            q_unit(7, 1)
            scores(0, 1)
            scores(1, 1)
            for h in range(4):
                if h + 2 < 4:
                    scores(h + 2, 1)
                den_pv(h, 1)
            flush()
            proj_ln(ALLM, fetch_2x512(), 8, lambda k, tt: ar(aoT(k, tt)), lambda k, tt: t_ar[aoT(k, tt)],
                    "xa_g", "xa_b", l)

        def ffn(st, l, boundary=None):
            def hT(f, tt):
                return f * 2 + tt
            if st % 2 == 0:
                S.op("pool", lambda e: e.memset(halo_f[l][:], 0.0), writes=[t_halo_f[l]])
            pend = [None]

            def gelu_of(p):
                fo, f, ba, ih = p
                S.op("act", lambda e: e.activation(
                    ftmp[fo][:], ftmp[fo][:], AF.Gelu_apprx_tanh, bias=pvc(("ffn_conv_b", l), f)),
                    reads=[t_ft[fo], t_const], writes=[t_ft[fo]])

            def mult_of(p):
                fo, f, ba, ih = p
                S.op("dve", lambda e: e.tensor_tensor(
                    ar(ih), ftmp[fo][:], psum[ba][:], ALU.mult),
                    reads=[t_ft[fo], t_ps[ba]], writes=[t_ar[ih]])

            for q in range(6):
                n = 512 if q < 5 else 256
                wg, tg = w_next(8, n)
                wa, ta = w_next(8, n, False)
                fs = list(range(q * 4, min(FC, q * 4 + 4)))
                for tt in range(NTT):
                    if q == 0 and tt == 1:
                        flush()
                    for f in fs:
                        coff = (f % 4) * P
                        bg = mm_unit(wg, tg, coff, 8, x_in_ap, x_in_tile, tt)
                        ba = mm_unit(wa, ta, coff, 8, x_in_ap, x_in_tile, tt)
                        pump_crit()
                        gi = gb_next()
                        S.op("act", lambda e, gi=gi, bg=bg: e.activation(gbuf[gi][:, 2:TT + 2], psum[bg][:], AF.Copy),
                             reads=[t_ps[bg]], writes=[t_gbuf[gi]])
                        halo_io(gi, halo_f[l][:, f, :], t_halo_f[l])
                        if pend[0] is not None:
                            gelu_of(pend[0])
                        fo = ft_next()
                        conv3(gi, fo, PV[("ffn_conv_w", l)] + f * 3)
                        if pend[0] is not None:
                            mult_of(pend[0])
                        pend[0] = (fo, f, ba, hT(f, tt))
                        pump(3)
            gelu_of(pend[0])
            mult_of(pend[0])
            flush()

            def fetch_down(group):
                out = {}
                for i, m in enumerate(group):
                    wv, wt = w_next(FC, 128, i == 0)
                    out[m] = (wv, wt, 0)
                return out
            proj_ln([[0, 1, 2, 3], [4, 5, 6, 7]], fetch_down, FC,
                    lambda k, tt: ar(hT(k, tt)), lambda k, tt: t_ar[hT(k, tt)], "ffn_g", "ffn_b", l,
                    boundary=boundary)

        def load_tile(st, c, tt):
            b = st // 2
            s0 = (st % 2) * STW
            S.op("sp", lambda e: e.dma_start(
                out=x32(c, tt), in_=xT_d[b, c, :, s0 + tt * TT:s0 + (tt + 1) * TT]),
                writes=[t_xT32[c][tt]], dma=True)
            S.op("pool", lambda e: e.dma_start(
                out=xb(c, tt), in_=xT_d[b, c, :, s0 + tt * TT:s0 + (tt + 1) * TT]),
                writes=[t_xTb[c][tt]], dma=True)

        def store_tile(st, c, tt):
            b = st // 2
            s0 = (st % 2) * STW
            S.op("sp", lambda e: e.dma_start(
                out=outT_d[b, c, :, s0 + tt * TT:s0 + (tt + 1) * TT], in_=x32(c, tt)),
                reads=[t_xT32[c][tt]], writes=[t_out], dma=True)

        done = False
        for st in range(n_st):
            b = st // 2
            if st == 0:
                for tt in range(NTT):
                    for c in range(DC):
                        load_tile(0, c, tt)
            elif st % 2 == 0:
                kv_phase(b, 0)
                kv_phase(b, 1)
            for l in range(2):
                if l == 0:
                    mixer_ab(st)
                else:
                    mixer_c(st)
                if stop_after == (l, "mix"):
                    done = True
                    break
                if st == 0 and l == 0:
                    kv_phase(b, 0)
                    kv_phase(b, 1)
                xattn(st, l)
                if stop_after == (l, "xa"):
                    done = True
                    break
                bnd = None
                if l == 1 and stop_after is None:
                    nxt = st < n_st - 1
                    bnd = ([(lambda st=st, c=c: store_tile(st, c, 0)) for c in range(DC)],
                           [(lambda st=st, c=c: load_tile(st + 1, c, 0)) for c in range(DC)] if nxt else [],
                           [(lambda st=st, c=c: store_tile(st, c, 1)) for c in range(DC)],
                           [(lambda st=st, c=c: load_tile(st + 1, c, 1)) for c in range(DC)] if nxt else [])
                ffn(st, l, bnd)
                if stop_after == (l, "ffn"):
                    done = True
                    break
            last = done or st == n_st - 1
            if last:
                flush()
                if stop_after is not None:
                    for tt in range(NTT):
                        for c in range(DC):
                            store_tile(st, c, tt)
                break
        S.emit(nc)
    return nc, len(S.recs)


def _bf16_round(a):
    u = np.ascontiguousarray(a, dtype=np.float32).view(np.uint32).astype(np.uint64)
    r = ((u + 0x7FFF + ((u >> 16) & 1)) >> 16) << 16
    return r.astype(np.uint32).view(np.float32)


def _pool_mats():
    out = np.zeros((P, 20, P), np.float64)
    t = np.arange(P)
    for g, w in enumerate(POOL_WINDOWS):
        cur = np.zeros((P, P))
        prev = np.zeros((P, P))
        first = np.zeros((P, P))
        for tc in range(P):
            for tp in range(max(0, tc - w + 1), tc + 1):
                cur[tp, tc] += 1.0 / w
            for d in range(tc - w + 1, 0):
                prev[P + d, tc] += 1.0 / w
            cnt = min(tc + 1, w)
            for tp in range(max(0, tc - w + 1), tc + 1):
                first[tp, tc] += 1.0 / cnt
        cur -= np.eye(P)
        first -= np.eye(P)
        out[:, g, :] = cur
        out[:, 4 + g, :] = prev
        hi = _bf16_round(first.astype(np.float32)).astype(np.float64)
        lo = _bf16_round((first - hi).astype(np.float32)).astype(np.float64)
        lo2 = _bf16_round((first - hi - lo).astype(np.float32)).astype(np.float64)
        out[:, 8 + g, :] = hi
        out[:, 12 + g, :] = lo
        out[:, 16 + g, :] = lo2
    return out.astype(np.float32)


def _cols(v, nchunk):
    return np.ascontiguousarray(np.asarray(v, np.float32).reshape(nchunk, P).T)


def _prep_shared(inp):
    pvh = np.zeros((P, NPV), np.float32)
    for l in range(2):
        for n, key in (("mix_g", "ln_mix_g"), ("mix_b", "ln_mix_b"), ("xa_g", "ln_xa_g"),
                       ("xa_b", "ln_xa_b"), ("ffn_g", "ln_ffn_g"), ("ffn_b", "ln_ffn_b")):
            c = PV[(n, l)]
            pvh[:, c:c + 8] = _cols(inp[key][l], 8)
        c = PV[("ffn_conv_w", l)]
        w = np.asarray(inp["ffn_conv_w"][l], np.float32)
        pvh[:, c:c + 66] = w.reshape(3, FC, P).transpose(2, 1, 0).reshape(P, 66)
        c = PV[("ffn_conv_b", l)]
        pvh[:, c:c + 22] = _cols(inp["ffn_conv_b"][l], FC)
    c = PV["pool_scale"]
    pvh[:, c:c + 4] = _cols(inp["pool_scale"][0], 4)
    c = PV["c_conv_w"]
    w = np.asarray(inp["c_conv_w"][0], np.float32)
    pvh[:, c:c + 24] = w.reshape(3, DC, P).transpose(2, 1, 0).reshape(P, 24)
    sgub = np.empty((P, 1024), np.float32)
    sgub[:, 0:512] = np.asarray(inp["sgu_ln_g"][0], np.float32)[None, :]
    sgub[:, 512:1024] = np.asarray(inp["sgu_ln_b"][0], np.float32)[None, :]
    shared = {
        "ab_w_in": np.ascontiguousarray(inp["ab_w_in"][0], dtype=np.float32),
        "ab_w_out": np.ascontiguousarray(inp["ab_w_out"][0], dtype=np.float32),
        "c_w_in": np.ascontiguousarray(inp["c_w_in"][0], dtype=np.float32),
        "c_w_out": np.ascontiguousarray(inp["c_w_out"][0], dtype=np.float32),
        "xa_wq": np.ascontiguousarray(inp["xa_wq"], dtype=np.float32),
        "xa_wkv": np.ascontiguousarray(inp["xa_wkv"], dtype=np.float32),
        "xa_wo": np.ascontiguousarray(inp["xa_wo"], dtype=np.float32),
        "ffn_w_up": np.ascontiguousarray(inp["ffn_w_up"], dtype=np.float32),
        "ffn_w_down": np.ascontiguousarray(inp["ffn_w_down"], dtype=np.float32),
        "pv": pvh,
        "sgub": sgub,
        "sguwT": np.ascontiguousarray(np.asarray(inp["sgu_w"][0], np.float32).transpose(2, 0, 1)),
        "sgubias": np.ascontiguousarray(np.asarray(inp["sgu_b"][0], np.float32).reshape(1, 512)),
        "poolmats": _pool_mats(),
        "poolw": np.ascontiguousarray(np.asarray(inp["pool_w"][0], np.float32).transpose(1, 0, 2)),
    }
    return shared


_CACHE = {}


def _get_program(n_st=4, stop_after=None):
    key = (n_st, stop_after)
    if key not in _CACHE:
        _CACHE[key] = build_program(n_st, stop_after)[0]
    return _CACHE[key]


def kernel(**inputs):
    inp = {k: np.asarray(v) for k, v in inputs.items()}
    n = 8
    shared = _prep_shared(inp)
    x = np.asarray(inp["x"], np.float32)
    mem = np.asarray(inp["mem"], np.float32)
    in_maps = []
    for i in range(n):
        xs = x[2 * i:2 * i + 2]
        xT = np.ascontiguousarray(xs.transpose(0, 2, 1)).reshape(NB_LOCAL, DC, P, SEQ)
        ms = mem[2 * i:2 * i + 2]
        mT = np.ascontiguousarray(ms.transpose(0, 2, 1)).reshape(NB_LOCAL, DC, P, MEM)
        d = dict(shared)
        d["xT"] = xT
        d["memT"] = mT
        in_maps.append(d)
    nc = _get_program()
    res = run_bass_kernel_spmd(nc, in_maps, core_ids=list(range(n)))
    outs = []
    for i in range(n):
        oT = np.asarray(res.results[i]["outT"]).reshape(NB_LOCAL, D, SEQ)
        outs.append(oT.transpose(0, 2, 1))
    return np.ascontiguousarray(np.concatenate(outs, axis=0), dtype=np.float32)
```
